# Optimizing a Trainium2 kernel written in Bass

```python
import math
import jax, jax.numpy as jnp
from jax import lax
import numpy as np

D_MODEL = 1024
BATCH = 16
SEQ = 2048
DEPTH = 4
DEC_BATCH = 32
DEC_SEQ = 32
PAST_LEN = 4096

CHUNK = 64
Q_BLOCK = 128
GMLP_CHUNK = 128
D_A = D_MODEL // 4
N_A_HEADS = 4
A_HEAD = D_A // N_A_HEADS
D_B = D_MODEL // 2
N_B_HEADS = 8
V_DIM = D_B // N_B_HEADS
NOPE_DIM = 64
ROPE_DIM = 32
Q_LORA = 384
KV_LORA = 256
ROPE_THETA = 10000.0
ATTN_SCALE = (NOPE_DIM + ROPE_DIM) ** -0.5
D_C = D_MODEL // 4
N_C_BLOCKS = 4
C_HEAD = D_C // N_C_BLOCKS
LRU_CONV = 4
LRU_C = 8.0
D_MIX = D_A + D_B + D_C
IN_SPLITS = (D_A, D_A, Q_LORA, KV_LORA, ROPE_DIM, D_C, D_C)
D_IN = sum(IN_SPLITS)
D_FF = 2816
FFN_CONV = 3
ALPHA = (2.0 * DEPTH) ** 0.25
BETA = (8.0 * DEPTH) ** -0.25
LN_EPS = 1e-5
RMS_EPS = 1e-6

kernel_name = "hymba_style_gmlp_mla_rglru_streaming_step"


def _split_points(sizes):
    pts, acc = [], 0
    for s in sizes[:-1]:
        acc += s
        pts.append(acc)
    return pts


def layer_norm(x, g, b):
    xf = x.astype(jnp.float32)
    mu = jnp.mean(xf, axis=-1, keepdims=True)
    var = jnp.mean(jnp.square(xf - mu), axis=-1, keepdims=True)
    return ((xf - mu) * lax.rsqrt(var + LN_EPS) * g + b).astype(x.dtype)


def rms_norm(x, g):
    xf = x.astype(jnp.float32)
    ms = jnp.mean(jnp.square(xf), axis=-1, keepdims=True)
    return (xf * lax.rsqrt(ms + RMS_EPS) * g).astype(x.dtype)


def apply_rope(x, pos):
    r = x.shape[-1]
    inv = 1.0 / (ROPE_THETA ** (jnp.arange(0, r, 2, dtype=jnp.float32) / r))
    ang = pos.astype(jnp.float32)[:, None] * inv[None, :]
    cos = jnp.cos(ang)[None, :, None, :]
    sin = jnp.sin(ang)[None, :, None, :]
    xf = x.astype(jnp.float32)
    x1, x2 = xf[..., : r // 2], xf[..., r // 2:]
    return jnp.concatenate([x1 * cos - x2 * sin, x1 * sin + x2 * cos], axis=-1).astype(x.dtype)


def causal_dwconv(x, prev, w, b):
    k = w.shape[0]
    t = x.shape[1]
    xp = jnp.concatenate([prev.astype(x.dtype), x], axis=1)
    y = b + xp[:, 0:t] * w[0]
    for j in range(1, k):
        y = y + xp[:, j:j + t] * w[j]
    return y, xp[:, t:]


def gmlp_spatial_gate(u, v, w_s, b_s):
    bsz, t, _ = v.shape
    ln = min(t, GMLP_CHUNK)
    n = t // ln
    w = jnp.tril(w_s[:, :ln, :ln])
    vb = v.reshape(bsz, n, ln, N_A_HEADS, A_HEAD)
    s = jnp.einsum('hij,bnjhd->bnihd', w, vb) + b_s[:, :ln].T[None, None, :, :, None]
    return u * s.reshape(bsz, t, D_A)


def mla_prompt_attention(q_nope, q_rope, k_nope, k_rope, v):
    bsz, t, h, _ = q_nope.shape
    nb = t // Q_BLOCK
    kchunk = jnp.arange(t) // CHUNK
    neg = jnp.finfo(jnp.float32).min

    def block(args):
        qn, qr, qc = args
        s = (jnp.einsum('bqhd,bkhd->bhqk', qn, k_nope)
             + jnp.einsum('bqhr,bkr->bhqk', qr, k_rope)).astype(jnp.float32) * ATTN_SCALE
        s = jnp.where(kchunk[None, :] <= qc[:, None], s, neg)
        p = jax.nn.softmax(s, axis=-1).astype(v.dtype)
        return jnp.einsum('bhqk,bkhd->bqhd', p, v)

    qn_b = q_nope.reshape(bsz, nb, Q_BLOCK, h, NOPE_DIM).swapaxes(0, 1)
    qr_b = q_rope.reshape(bsz, nb, Q_BLOCK, h, ROPE_DIM).swapaxes(0, 1)
    qc_b = (jnp.arange(t) // CHUNK).reshape(nb, Q_BLOCK)
    out = lax.map(block, (qn_b, qr_b, qc_b))
    return out.swapaxes(0, 1).reshape(bsz, t, h, V_DIM)


def mla_sample_attention(q_nope, q_rope, c_all, kr_all, w_uk, w_uv):
    q_lat = jnp.einsum('bqhd,chd->bqhc', q_nope, w_uk)
    s = (jnp.einsum('bqhc,bkc->bhqk', q_lat, c_all)
         + jnp.einsum('bqhr,bkr->bhqk', q_rope, kr_all)).astype(jnp.float32) * ATTN_SCALE
    p = jax.nn.softmax(s, axis=-1).astype(c_all.dtype)
    o_lat = jnp.einsum('bhqk,bkc->bqhc', p, c_all)
    return jnp.einsum('bqhc,chd->bqhd', o_lat, w_uv)


def linear_recurrence(a, b, h0):
    def combine(left, right):
        a1, b1 = left
        a2, b2 = right
        return a1 * a2, a2 * b1 + b2
    a_cum, b_cum = lax.associative_scan(combine, (a, b), axis=1)
    h = a_cum * h0[:, None, :] + b_cum
    return h, h[:, -1]


def trunk_layer(x, pos, lp, lat_cache, kr_cache, lru_conv_prev, lru_h0, ffn_conv_prev):
    bsz, t, _ = x.shape
    z = x @ lp['w_in']
    u, v, cq, ckv, kr, xc_in, gate_c = jnp.split(z, _split_points(IN_SPLITS), axis=-1)

    u = jax.nn.gelu(u)
    v = jax.nn.gelu(v)
    a_out = gmlp_spatial_gate(u, v, lp['gmlp_w_s'], lp['gmlp_b_s'])

    cq_n = rms_norm(cq, lp['mla_q_norm_g'])
    q = (cq_n @ lp['mla_w_uq']).reshape(bsz, t, N_B_HEADS, NOPE_DIM + ROPE_DIM)
    q_nope = q[..., :NOPE_DIM]
    q_rope = apply_rope(q[..., NOPE_DIM:], pos)
    ckv_n = rms_norm(ckv, lp['mla_kv_norm_g'])
    kr_rot = apply_rope(kr[:, :, None, :], pos)[:, :, 0, :]
    if lat_cache is None:
        k_nope = jnp.einsum('btc,chd->bthd', ckv_n, lp['mla_w_uk'])
        v_b = jnp.einsum('btc,chd->bthd', ckv_n, lp['mla_w_uv'])
        b_out = mla_prompt_attention(q_nope, q_rope, k_nope, kr_rot, v_b)
    else:
        c_all = jnp.concatenate([lat_cache.astype(ckv_n.dtype), ckv_n], axis=1)
        kr_all = jnp.concatenate([kr_cache.astype(kr_rot.dtype), kr_rot], axis=1)
        b_out = mla_sample_attention(q_nope, q_rope, c_all, kr_all, lp['mla_w_uk'], lp['mla_w_uv'])

    xc, lru_conv_new = causal_dwconv(xc_in, lru_conv_prev, lp['lru_conv_w'], lp['lru_conv_b'])
    xb = xc.reshape(bsz, t, N_C_BLOCKS, C_HEAD)
    r = jax.nn.sigmoid(jnp.einsum('btnd,nde->btne', xb, lp['lru_w_r']).reshape(bsz, t, D_C) + lp['lru_b_r'])
    ig = jax.nn.sigmoid(jnp.einsum('btnd,nde->btne', xb, lp['lru_w_i']).reshape(bsz, t, D_C) + lp['lru_b_i'])
    log_a = -LRU_C * r.astype(jnp.float32) * jax.nn.softplus(-lp['lru_lam'].astype(jnp.float32))
    a = jnp.exp(log_a)
    b_in = jnp.sqrt(-jnp.expm1(2.0 * log_a)) * (ig * xc).astype(jnp.float32)
    h, h_last = linear_recurrence(a, b_in, lru_h0.astype(jnp.float32))
    c_out = h.astype(x.dtype) * jax.nn.gelu(gate_c)

    mix = jnp.concatenate([a_out, b_out.reshape(bsz, t, D_B), c_out], axis=-1) @ lp['w_o']
    x = layer_norm(ALPHA * x + mix, lp['ln1_g'], lp['ln1_b'])

    up = x @ lp['ffn_w_up']
    upc, ffn_conv_new = causal_dwconv(up, ffn_conv_prev, lp['ffn_conv_w'], lp['ffn_conv_b'])
    g_ff, val = upc[..., :D_FF], upc[..., D_FF:]
    f = (jax.nn.gelu(g_ff) * val) @ lp['ffn_w_down']
    x = layer_norm(ALPHA * x + f, lp['ln2_g'], lp['ln2_b'])
    return x, v, ckv_n, kr_rot, lru_conv_new, h_last.astype(x.dtype), ffn_conv_new


def setup_inputs(seed: int = 0) -> dict:
    key = jax.random.key(seed)
    ks = iter(jax.random.split(key, 40))
    f32 = jnp.float32

    def nrm(shape, scale):
        return jax.random.normal(next(ks), shape, f32) * scale

    u = jax.random.uniform(next(ks), (DEPTH, D_C), f32, minval=0.9, maxval=0.999)
    a_base = u ** (1.0 / LRU_C)
    lam = jnp.log(a_base) - jnp.log1p(-a_base)
    return {
        'x_prompt': nrm((BATCH, SEQ, D_MODEL), 1.0),
        'x_sample': nrm((DEC_BATCH, DEC_SEQ, D_MODEL), 1.0),
        'cache_kv_latent': nrm((DEPTH, DEC_BATCH, PAST_LEN, KV_LORA), 1.0),
        'cache_k_rope': nrm((DEPTH, DEC_BATCH, PAST_LEN, ROPE_DIM), 1.0),
        'state_lru_h': nrm((DEPTH, DEC_BATCH, D_C), 0.5),
        'state_lru_conv': nrm((DEPTH, DEC_BATCH, LRU_CONV - 1, D_C), 1.0),
        'state_ffn_conv': nrm((DEPTH, DEC_BATCH, FFN_CONV - 1, 2 * D_FF), 1.0),
        'ln1_g': 1.0 + nrm((DEPTH, D_MODEL), 0.02),
        'ln1_b': nrm((DEPTH, D_MODEL), 0.02),
        'ln2_g': 1.0 + nrm((DEPTH, D_MODEL), 0.02),
        'ln2_b': nrm((DEPTH, D_MODEL), 0.02),
        'w_in': nrm((DEPTH, D_MODEL, D_IN), D_MODEL ** -0.5),
        'w_o': nrm((DEPTH, D_MIX, D_MODEL), BETA * D_MIX ** -0.5),
        'gmlp_w_s': nrm((DEPTH, N_A_HEADS, GMLP_CHUNK, GMLP_CHUNK), GMLP_CHUNK ** -0.5),
        'gmlp_b_s': 1.0 + nrm((DEPTH, N_A_HEADS, GMLP_CHUNK), 0.02),
        'mla_q_norm_g': 1.0 + nrm((DEPTH, Q_LORA), 0.02),
        'mla_w_uq': nrm((DEPTH, Q_LORA, N_B_HEADS * (NOPE_DIM + ROPE_DIM)), Q_LORA ** -0.5),
        'mla_kv_norm_g': 1.0 + nrm((DEPTH, KV_LORA), 0.02),
        'mla_w_uk': nrm((DEPTH, KV_LORA, N_B_HEADS, NOPE_DIM), KV_LORA ** -0.5),
        'mla_w_uv': nrm((DEPTH, KV_LORA, N_B_HEADS, V_DIM), KV_LORA ** -0.5),
        'lru_conv_w': nrm((DEPTH, LRU_CONV, D_C), LRU_CONV ** -0.5),
        'lru_conv_b': nrm((DEPTH, D_C), 0.01),
        'lru_w_r': nrm((DEPTH, N_C_BLOCKS, C_HEAD, C_HEAD), C_HEAD ** -0.5),
        'lru_b_r': nrm((DEPTH, D_C), 0.01),
        'lru_w_i': nrm((DEPTH, N_C_BLOCKS, C_HEAD, C_HEAD), C_HEAD ** -0.5),
        'lru_b_i': nrm((DEPTH, D_C), 0.01),
        'lru_lam': lam,
        'ffn_w_up': nrm((DEPTH, D_MODEL, 2 * D_FF), D_MODEL ** -0.5),
        'ffn_conv_w': nrm((DEPTH, FFN_CONV, 2 * D_FF), FFN_CONV ** -0.5),
        'ffn_conv_b': nrm((DEPTH, 2 * D_FF), 0.01),
        'ffn_w_down': nrm((DEPTH, D_FF, D_MODEL), BETA * D_FF ** -0.5),
    }


def reference(x_prompt, x_sample, cache_kv_latent, cache_k_rope, state_lru_h, state_lru_conv,
              state_ffn_conv, ln1_g, ln1_b, ln2_g, ln2_b, w_in, w_o, gmlp_w_s, gmlp_b_s,
              mla_q_norm_g, mla_w_uq, mla_kv_norm_g, mla_w_uk, mla_w_uv, lru_conv_w, lru_conv_b,
              lru_w_r, lru_b_r, lru_w_i, lru_b_i, lru_lam, ffn_w_up, ffn_conv_w, ffn_conv_b,
              ffn_w_down):
    bp, s_len, _ = x_prompt.shape
    t_len = x_sample.shape[1]
    past = cache_kv_latent.shape[2]
    pos_p = jnp.arange(s_len)
    pos_d = past + jnp.arange(t_len)
    zeros_lru_conv = jnp.zeros((bp, LRU_CONV - 1, D_C), x_prompt.dtype)
    zeros_lru_h = jnp.zeros((bp, D_C), x_prompt.dtype)
    zeros_ffn_conv = jnp.zeros((bp, FFN_CONV - 1, 2 * D_FF), x_prompt.dtype)

    xp, xd = x_prompt, x_sample
    p_lat, p_kr, p_h, p_lconv, p_fconv = [], [], [], [], []
    s_lat, s_kr, s_v, s_h, s_lconv, s_fconv = [], [], [], [], [], []
    for l in range(DEPTH):
        lp = {
            'w_in': w_in[l], 'w_o': w_o[l], 'ln1_g': ln1_g[l], 'ln1_b': ln1_b[l],
            'ln2_g': ln2_g[l], 'ln2_b': ln2_b[l], 'gmlp_w_s': gmlp_w_s[l], 'gmlp_b_s': gmlp_b_s[l],
            'mla_q_norm_g': mla_q_norm_g[l], 'mla_w_uq': mla_w_uq[l],
            'mla_kv_norm_g': mla_kv_norm_g[l], 'mla_w_uk': mla_w_uk[l], 'mla_w_uv': mla_w_uv[l],
            'lru_conv_w': lru_conv_w[l], 'lru_conv_b': lru_conv_b[l], 'lru_w_r': lru_w_r[l],
            'lru_b_r': lru_b_r[l], 'lru_w_i': lru_w_i[l], 'lru_b_i': lru_b_i[l], 'lru_lam': lru_lam[l],
            'ffn_w_up': ffn_w_up[l], 'ffn_conv_w': ffn_conv_w[l], 'ffn_conv_b': ffn_conv_b[l],
            'ffn_w_down': ffn_w_down[l],
        }
        xp, _, lat_p, kr_p, lconv_p, h_p, fconv_p = trunk_layer(
            xp, pos_p, lp, None, None, zeros_lru_conv, zeros_lru_h, zeros_ffn_conv)
        xd, v_d, lat_d, kr_d, lconv_d, h_d, fconv_d = trunk_layer(
            xd, pos_d, lp, cache_kv_latent[l], cache_k_rope[l], state_lru_conv[l],
            state_lru_h[l], state_ffn_conv[l])
        p_lat.append(lat_p); p_kr.append(kr_p); p_h.append(h_p)
        p_lconv.append(lconv_p); p_fconv.append(fconv_p)
        s_lat.append(lat_d); s_kr.append(kr_d); s_v.append(v_d); s_h.append(h_d)
        s_lconv.append(lconv_d); s_fconv.append(fconv_d)

    return (xp, xd,
            jnp.stack(p_lat), jnp.stack(p_kr), jnp.stack(p_h), jnp.stack(p_lconv), jnp.stack(p_fconv),
            jnp.stack(s_lat), jnp.stack(s_kr), jnp.stack(s_v), jnp.stack(s_h), jnp.stack(s_lconv),
            jnp.stack(s_fconv))
```

```python
import contextlib
import numpy as np
import concourse.bass as bass
import concourse.mybir as mybir
from concourse.bass_utils import run_bass_kernel_spmd

F32 = mybir.dt.float32
BF16 = mybir.dt.bfloat16
AF = mybir.ActivationFunctionType
ALU = mybir.AluOpType

D = 1024
DIN = 1696
DFF = 2816
NCORES = 8
ALPHA = (2.0 * 4) ** 0.25
LN_EPS = 1e-5
RMS_EPS = 1e-6
ATTN_SCALE = 96 ** -0.5
CU, CV, CQ, CKV, CKR, CXC, CG = 0, 256, 512, 896, 1152, 1184, 1440


class Sched:
    ENGS = ('pe', 'act', 'dve', 'pool', 'sp')

    def __init__(self, nc, stack, rings):
        self.nc = nc
        self.ops = {e: [] for e in self.ENGS}
        self.cnt = {e: 0 for e in self.ENGS}
        self.sems = {}
        for e in self.ENGS:
            self.sems[('e', e)] = stack.enter_context(nc.semaphore('s_' + e))
        self.rings = {}
        for q, n in rings.items():
            self.rings[q] = n
            for i in range(n):
                self.sems[('r', q, i)] = stack.enter_context(nc.semaphore('r_%s%d' % (q, i)))
        self.ring_n = {q: 0 for q in rings}
        self.waited = {e: {} for e in self.ENGS}
        self.last_w = {}
        self.readers = {}
        self.nops = 0
        self.alias = {}

    def _exp(self, keys):
        out = []
        for k in keys:
            out.extend(self.alias.get(k, (k,)))
        return out

    def _need(self, eng, semid, val, src, waits):
        if src == eng and eng == 'pe':
            return
        if self.waited[eng].get(semid, 0) >= val:
            return
        if waits.get(semid, 0) < val:
            waits[semid] = val

    def op(self, eng, fn, reads=(), writes=(), dma=False):
        reads = self._exp(reads)
        writes = self._exp(writes)
        waits = {}
        for k in reads:
            lw = self.last_w.get(k)
            if lw is not None:
                self._need(eng, lw[0], lw[1], lw[2], waits)
        for k in writes:
            lw = self.last_w.get(k)
            if lw is not None:
                self._need(eng, lw[0], lw[1], lw[2], waits)
            rd = self.readers.get(k)
            if rd:
                for semid, (val, se) in rd.items():
                    self._need(eng, semid, val, se, waits)
        if dma:
            q = eng
            i = self.ring_n[q]
            self.ring_n[q] += 1
            R = self.rings[q]
            semid = ('r', q, i % R)
            prev = 16 * (i // R)
            if prev > 0 and self.waited[eng].get(semid, 0) < prev and waits.get(semid, 0) < prev:
                waits[semid] = prev
            val = prev + 16
            inc = 16
            src = 'dma'
        else:
            self.cnt[eng] += 1
            semid = ('e', eng)
            val = self.cnt[eng]
            inc = 1
            src = eng
        for sid, v in waits.items():
            if self.waited[eng].get(sid, 0) < v:
                self.waited[eng][sid] = v
        self.ops[eng].append((list(waits.items()), fn, semid, inc))
        for k in reads:
            self.readers.setdefault(k, {})[semid] = (val, src)
        for k in writes:
            self.last_w[k] = (semid, val, src)
            self.readers[k] = {}
        self.nops += 1

    def barrier(self, skip_rings=()):
        cur = {}
        for e in self.ENGS:
            if self.cnt[e] > 0:
                cur[('e', e)] = self.cnt[e]
        for q, R in self.rings.items():
            if q in skip_rings:
                continue
            n = self.ring_n[q]
            for s in range(R):
                k = (n - 1 - s) // R + 1 if n - 1 - s >= 0 else 0
                if k > 0:
                    cur[('r', q, s)] = 16 * k
        for e in self.ENGS:
            waits = []
            for sid, v in cur.items():
                if sid == ('e', e):
                    continue
                if self.waited[e].get(sid, 0) < v:
                    waits.append((sid, v))
                    self.waited[e][sid] = v
            self.ops[e].append((waits, None, None, 0))

    def emit(self, block):
        sems = self.sems

        def run(name):
            def f(eng):
                for waits, fn, semid, inc in self.ops[name]:
                    for sid, v in waits:
                        eng.wait_ge(sems[sid], v)
                    if fn is None:
                        continue
                    fn(eng).then_inc(sems[semid], inc)
            return f
        block.tensor(run('pe'))
        block.scalar(run('act'))
        block.vector(run('dve'))
        block.gpsimd(run('pool'))
        block.sync(run('sp'))


class Cfg:
    def __init__(self, L=4, SEQ=2048, NP=2, NS=4, PAST=4096, do_prompt=True, do_sample=True):
        self.L, self.SEQ, self.NP, self.NS, self.PAST = L, SEQ, NP, NS, PAST
        self.do_prompt, self.do_sample = do_prompt, do_sample


WEIGHTS = [
    ('w_in', [D, DIN]), ('w_o', [D, D]), ('mla_w_uq', [384, 768]), ('mla_w_uk', [256, 512]),
    ('mla_w_uv', [256, 512]), ('lru_w_r', [256, 64]), ('lru_w_i', [256, 64]),
    ('ffn_w_up', [D, 2 * DFF]), ('ffn_w_down', [DFF, D]),
]
VECS = [('ln1_g', D), ('ln1_b', D), ('ln2_g', D), ('ln2_b', D), ('mla_q_norm_g', 384),
        ('lru_conv_b', 256), ('lru_b_r', 256), ('lru_b_i', 256), ('lru_lam', 256),
        ('ffn_conv_b', 2 * DFF)]


def build_nc(cfg):
    L, SEQ, NP, NS, PAST = cfg.L, cfg.SEQ, cfg.NP, cfg.NS, cfg.PAST
    TS = 32
    nc = bass.Bass("TRN2", target_bir_lowering=False, dynamic_dma_scratch_size=2048)
    din = lambda n, s: nc.dram_tensor(n, s, F32, kind="ExternalInput").ap()
    dout = lambda n, s: nc.dram_tensor(n, s, F32, kind="ExternalOutput").ap()
    xp = din('xp', [NP, SEQ, D])
    xs = din('xs', [NS * TS, D])
    clat = din('clat', [L, NS, PAST, 256])
    ckr = din('ckr', [L, NS, PAST, 32])
    lruh = din('lruh', [L, NS, 256])
    lruc = din('lruc', [L, NS, 3, 256])
    ffnc = din('ffnc', [L, NS, 2, 2 * DFF])
    wd = {n: din(n, [L] + s) for n, s in WEIGHTS}
    vd = {n: din(n, [L, s]) for n, s in VECS}
    gmlp_w = din('gmlp_w_s', [L, 4, 128, 128])
    gmlp_b = din('gmlp_b_s', [L, 4, 128])
    kvg = din('mla_kv_norm_g', [L, 256])
    lcw = din('lru_conv_w', [L, 4, 256])
    fcw = din('ffn_conv_w', [L, 3, 2 * DFF])
    ident_d = din('c_ident', [128, 128])
    tril_d = din('c_tril', [128, 128])
    cosp_d = din('c_cosp', [32, SEQ])
    sinp_d = din('c_sinp', [32, SEQ])
    coss_d = din('c_coss', [32, NS * TS])
    sins_d = din('c_sins', [32, NS * TS])

    yp = dout('yp', [NP, SEQ, D])
    ys = dout('ys', [NS * TS, D])
    o_plat = dout('o_plat', [L, NP, SEQ, 256])
    o_pkr = dout('o_pkr', [L, NP, SEQ, 32])
    o_ph = dout('o_ph', [L, NP, 256])
    o_pc = dout('o_pc', [L, NP, 3, 256])
    o_pf = dout('o_pf', [L, NP, 2, 2 * DFF])
    o_slat = dout('o_slat', [L, NS * TS, 256])
    o_skr = dout('o_skr', [L, NS * TS, 32])
    o_sv = dout('o_sv', [L, NS * TS, 256])
    o_sh = dout('o_sh', [L, NS, 256])
    o_sc = dout('o_sc', [L, NS, 3, 256])
    o_sf = dout('o_sf', [L, NS, 2, 2 * DFF])

    wb = {n: nc.dram_tensor(n + '_b', [L] + s, BF16).ap() for n, s in WEIGHTS}
    wsT_b = nc.dram_tensor('wsT_b', [L, 4, 128, 128], BF16).ap()

    with contextlib.ExitStack() as st:
        S = Sched(nc, st, {'sp': 16, 'act': 16, 'pool': 8})
        sbt = lambda n, s, d: st.enter_context(nc.sbuf_tensor(n, s, d))
        XW = max(SEQ, NS * TS)
        X = sbt('X', [128, 8, XW], F32)
        KT = sbt('KT', [128, 8, max(SEQ, 2048)], BF16)
        NKB = SEQ // 128
        VA = sbt('VA', [128, NKB, 576], BF16)
        ident = sbt('ident', [128, 128], F32)
        identb = sbt('identb', [128, 128], BF16)
        tril = sbt('tril', [128, 128], F32)
        onesS = sbt('onesS', [128, 128], BF16)
        ones1 = sbt('ones1', [128, 128], BF16)
        onesq = sbt('onesq', [128, 128], BF16)
        NV = 32 + 3 + 8 + 2 + 2 + 2 + 2 + 132 + 44
        PAR = sbt('PAR', [128, L, NV], F32)
        P_LN1G, P_LN1B, P_LN2G, P_LN2B, P_QG, P_LCW, P_LCB, P_BR, P_BI, P_LAM, P_FCW, P_FCB = \
            0, 8, 16, 24, 32, 35, 43, 45, 47, 49, 51, 183
        C1 = sbt('C1', [128, L, 4], F32)
        kvgb = sbt('kvgb', [128, 1, 256], F32)
        wq = sbt('wq', [128, 3, 8, 96], BF16)
        wqp = sbt('wqp', [128, 3, 8, 32], BF16)
        wkr = sbt('wkr', [128, 8, 32], BF16)
        wkrp = sbt('wkrp', [128, 8, 32], BF16)
        wuk = sbt('wuk', [128, 2, 8, 96], BF16)
        wuv = sbt('wuv', [128, 2, 512], BF16)
        wrbd = sbt('wrbd', [128, 2, 128], BF16)
        wibd = sbt('wibd', [128, 2, 128], BF16)
        wsT = sbt('wsT', [128, 4, 128], BF16)
        bsb = sbt('bsb', [64, 4, 128], F32)
        RING = [sbt('ring%d' % i, [128, 2048], BF16) for i in range(5)]
        TM = 256
        xb = sbt('xb', [128, 8, TM], BF16)
        uT = sbt('uT', [64, 4, TM], BF16)
        vtok = sbt('vtok', [128, 2, 256], BF16)
        cqT = sbt('cqT', [128, 3, TM], BF16)
        rqb = sbt('rqb', [128, TM], F32)
        cosrq = sbt('cosrq', [32, TM], F32)
        sinrq = sbt('sinrq', [32, TM], F32)
        cosb = sbt('cosb', [32, TM], F32)
        sinb = sbt('sinb', [32, TM], F32)
        ckvn = sbt('ckvn', [128, 2, 256], F32)
        ss = sbt('ss', [128, 8], F32)
        ckvT = sbt('ckvT', [128, 2, TM], BF16)
        krot = sbt('krot', [32, TM], F32)
        krtok = sbt('krtok', [128, 2, 32], F32)
        xcin = sbt('xcin', [128, 2, 4 * 35 + 224], F32)
        xcv = sbt('xcv', [128, TM], F32)
        xcb = sbt('xcb', [128, 2, TM], BF16)
        la = sbt('la', [128, TM], F32)
        lig = sbt('lig', [128, TM], F32)
        ls = sbt('ls', [128, TM], F32)
        hcar = sbt('hcar', [128, 2, 4], F32)
        gg = sbt('gg', [128, 2, TM], BF16)
        qT = sbt('qT', [128, 8, TM], BF16)
        qt1 = sbt('qt1', [32, TM], F32)
        qt2 = sbt('qt2', [32, TM], F32)
        PT = [sbt('pT%d' % i, [128, TM], BF16) for i in range(4)]
        mA = sbt('mA', [128, 4, TM], BF16)
        mB = sbt('mB', [128, 8, TM], BF16)
        mC = sbt('mC', [128, 2, TM], BF16)
        gtt = sbt('gtt', [128, 512], F32)
        gt = gtt[0:64, :]
        rden = gtt[:, 0:256]
        bcs = gtt[0:64, 256:512]
        meanb = la
        rstdb = lig
        hT = sbt('hT', [128, 22, TM], BF16)
        upb = [sbt('upb%d' % i, [128, 2, 4 * 34 + 124], F32) for i in range(2)]
        cgb = sbt('cgb', [128, 2, TM], F32)
        cgbB = sbt('cgbB', [128, 2, TM], F32)
        cgb2 = [cgb, cgbB]
        fcar = sbt('fcar', [128, 44, 4, 2], F32)
        stage = hT[:].rearrange("p a b -> p (a b)")[:, 0:2048].bitcast(F32)
        otok = stage
        S.alias.update({'sq': [('hT', c) for c in range(8)], 'lh': [('cgb0', 0), ('cgb0', 1)], 'krt1': ['qt1'],
                        'vf32': ['xcv'], 'meanb': ['la'], 'rstdb': ['lig'], 'rden': ['gtt'], 'bcs': ['gtt'],
                        'gt': ['gtt'], ('otok', 0): [('hT', c) for c in range(8)], ('otok', 1): [('hT', c) for c in range(8)],
                        'stage': [('hT', c) for c in range(8)], 'stage2': [('hT', c) for c in range(8)],
                        'cqsq': [('hT', 16), ('hT', 17), ('hT', 18)], 'lnscr': [('hT', c) for c in range(8, 16)]})
        cqsq = hT[:, 16:19, :]
        lnscr = hT[:, 8:16, :]
        sq = hT[:, 0:8, :]
        lh = cgb
        krt1 = qt1
        vf32 = xcv
        sstg = sbt('sstg', [128, 160], F32)
        PS = [st.enter_context(nc.psum_tensor('ps%d' % i, [128, 512], F32)) for i in range(8)]
        block = st.enter_context(nc.Block())

        def mm(out, lhsT, rhs, start, stop, rd, wr, skip=False):
            S.op('pe', lambda e: e.matmul(out, lhsT, rhs, start=start, stop=stop, skip_group_check=skip),
                 reads=rd, writes=wr)

        def tr(out, in_, idn, rd, wr):
            S.op('pe', lambda e: e.transpose(out, in_, idn), reads=rd, writes=wr)

        def act(out, in_, func, rd, wr, bias=None, scale=None, accum=None):
            kw = {}
            if bias is not None:
                kw['bias'] = bias
            if scale is not None:
                kw['scale'] = scale
            if accum is not None:
                kw['accum_out'] = accum
            S.op('act', lambda e: e.activation(out=out, in_=in_, func=func, **kw), reads=rd, writes=wr)

        def tt(eng, out, in0, in1, op, rd, wr):
            S.op(eng, lambda e: e.tensor_tensor(out=out, in0=in0, in1=in1, op=op), reads=rd, writes=wr)

        def ts(eng, out, in0, s1, s2, op0, op1, rd, wr):
            S.op(eng, lambda e: e.tensor_scalar(out=out, in0=in0, scalar1=s1, scalar2=s2, op0=op0, op1=op1),
                 reads=rd, writes=wr)

        def stt(out, in0, sc, in1, op0, op1, rd, wr):
            S.op('dve', lambda e: e.scalar_tensor_tensor(out=out, in0=in0, scalar=sc, in1=in1, op0=op0, op1=op1),
                 reads=rd, writes=wr)

        def cp(eng, out, in_, rd, wr):
            if eng == 'act':
                act(out, in_, AF.Copy, rd, wr)
            else:
                S.op(eng, lambda e: e.tensor_copy(out=out, in_=in_), reads=rd, writes=wr)

        def memset(eng, ap, v, wr):
            S.op(eng, lambda e: e.memset(ap, v), writes=wr)

        def dma(q, out, in_, rd, wr, slow=False):
            S.op(q, lambda e: e.dma_start(out=out, in_=in_, allow_slow_non_contiguous=slow),
                 reads=rd, writes=wr, dma=True)

        psn = [0]
        ps_open = set()

        def ps_get(hold=False):
            for _ in range(8):
                b = psn[0] % 8
                psn[0] += 1
                if b not in ps_open:
                    if hold:
                        ps_open.add(b)
                    return b
            raise RuntimeError("psum exhausted")

        def ps_rel(b):
            ps_open.discard(b)

        def psb(b):
            return PS[b][:].bitcast(BF16)

        ringn = [0]

        def ring_get():
            i = ringn[0] % 5
            ringn[0] += 1
            return RING[i], 'ring%d' % i

        outn = [0]

        def okey():
            outn[0] += 1
            return ('out', outn[0])

        dma('sp', ident[:], ident_d, [], ['ident'])
        dma('sp', tril[:], tril_d, [], ['tril'])
        cp('dve', identb[:], ident[:], ['ident'], ['identb'])
        memset('dve', onesS[:], 1.0 / 1024, ['onesS'])
        memset('dve', ones1[:], 1.0, ['ones1'])
        memset('dve', onesq[:], 1.0 / 384, ['onesq'])
        memset('pool', VA[:].rearrange("p a b -> p (a b)"), 0.0, ['VA'])
        memset('pool', wuk[:].rearrange("p a b c -> p (a b c)"), 0.0, ['wuk'])
        memset('pool', hcar[:].rearrange("p a b -> p (a b)"), 0.0, ['hcar'])
        memset('pool', qT[:].rearrange("p a b -> p (a b)"), 0.0, [('qT', h) for h in range(8)])
        memset('pool', KT[:].rearrange("p a b -> p (a b)"), 0.0, [('KT', h) for h in range(8)])
        memset('pool', mA[:].rearrange("p a b -> p (a b)"), 0.0, ['mA'])
        memset('pool', mB[:].rearrange("p a b -> p (a b)"), 0.0, [('mB', h) for h in range(8)])
        for i_ in range(5):
            memset('dve', RING[i_][:], 0.0, ['ring%d' % i_])
        for l_ in range(L):
            for n, s in WEIGHTS:
                rows = s[0] * s[1] // 2048
                src = wd[n][l_].rearrange("a b -> (a b)").rearrange("(r c) -> r c", c=2048)
                dst = wb[n][l_].rearrange("a b -> (a b)").rearrange("(r c) -> r c", c=2048)
                r0 = 0
                while r0 < rows:
                    r1 = min(rows, r0 + 1920)
                    dma('pool', dst[r0:r1, :], src[r0:r1, :], [], [(n + '_b', l_)])
                    r0 = r1

        lcn = [0]

        def load_cols(dst, vec, n):
            k = 2 + lcn[0] % 6
            lcn[0] += 1
            sv = stage[0:n, k * 128:(k + 1) * 128]
            dma('sp', sv, vec.rearrange("(c p) -> c p", p=128), [], [('stgs', k)])
            b = ps_get()
            tr(PS[b][:, 0:n], sv, ident[0:n, 0:n], [('stgs', k), 'ident'], ['ps%d' % b])
            cp('dve', dst, PS[b][:, 0:n], ['ps%d' % b], ['PAR'])

        for l in range(L):
            for nme, off, n in (('ln1_g', P_LN1G, 8), ('ln1_b', P_LN1B, 8), ('ln2_g', P_LN2G, 8),
                                ('ln2_b', P_LN2B, 8), ('mla_q_norm_g', P_QG, 3), ('lru_conv_b', P_LCB, 2),
                                ('lru_b_r', P_BR, 2), ('lru_b_i', P_BI, 2), ('lru_lam', P_LAM, 2),
                                ('ffn_conv_b', P_FCB, 44)):
                load_cols(PAR[:, l, off:off + n], vd[nme][l, :], n)
            for j in range(4):
                load_cols(PAR[:, l, P_LCW + 2 * j:P_LCW + 2 * j + 2], lcw[l, j, :], 2)
            for j in range(3):
                load_cols(PAR[:, l, P_FCW + 44 * j:P_FCW + 44 * j + 44], fcw[l, j, :], 44)
            act(C1[:, l, 0:2], PAR[:, l, P_LAM:P_LAM + 2], AF.Exp, ['PAR'], ['C1'], scale=-1.0)
            act(C1[:, l, 0:2], C1[:, l, 0:2], AF.Ln, ['C1'], ['C1'], bias=1.0)
            ts('dve', C1[:, l, 2:4], C1[:, l, 0:2], -16.0, None, ALU.mult, ALU.bypass, ['C1'], ['C1'])
            ts('dve', C1[:, l, 0:2], C1[:, l, 0:2], -8.0, None, ALU.mult, ALU.bypass, ['C1'], ['C1'])
            for h in range(4):
                dma('sp', stage[:, 0:128], gmlp_w[l, h, :, :], [], [('stgs', 0)])
                tt('dve', stage[:, 128:256], stage[:, 0:128], tril[:], ALU.mult, [('stgs', 0), 'tril'], [('stgs', 1)])
                b = ps_get()
                tr(PS[b][:, 0:128], stage[:, 128:256], ident[:], [('stgs', 1), 'ident'], ['ps%d' % b])
                cp('dve', xb[:, 0, 0:128], PS[b][:, 0:128], ['ps%d' % b], ['xb'])
                dma('sp', wsT_b[l, h, :, :], xb[:, 0, 0:128], ['xb'], ['wsT_b'])

        def load_layer(l, sample):
            uq = wb['mla_w_uq'][l].rearrange("(kc p) (h c) -> p kc h c", p=128, c=96)
            for kc in range(3):
                dma('sp', wq[:, kc, :, 0:32], uq[:, kc, :, 64:96], [('mla_w_uq_b', l)], ['wq'], slow=True)
                dma('sp', wq[:, kc, :, 32:96], uq[:, kc, :, 0:64], [('mla_w_uq_b', l)], ['wq'], slow=True)
                dma('sp', wqp[:, kc, :, 0:16], uq[:, kc, :, 80:96], [('mla_w_uq_b', l)], ['wqp'], slow=True)
                dma('sp', wqp[:, kc, :, 16:32], uq[:, kc, :, 64:80], [('mla_w_uq_b', l)], ['wqp'], slow=True)
            wi = wb['w_in'][l].rearrange("(kc p) n -> p kc n", p=128)
            dma('sp', wkr[:], wi[:, :, CKR:CKR + 32], [('w_in_b', l)], ['wkr'], slow=True)
            dma('sp', wkrp[:, :, 0:16], wi[:, :, CKR + 16:CKR + 32], [('w_in_b', l)], ['wkrp'], slow=True)
            dma('sp', wkrp[:, :, 16:32], wi[:, :, CKR:CKR + 16], [('w_in_b', l)], ['wkrp'], slow=True)
            uk = wb['mla_w_uk'][l].rearrange("(kc p) (h c) -> p kc h c", p=128, c=64)
            for kc in range(2):
                dma('sp', wuk[:, kc, :, 32:96], uk[:, kc, :, :], [('mla_w_uk_b', l)], ['wuk'], slow=True)
            dma('sp', kvgb[:, 0, :], kvg[l:l + 1, :].broadcast_to([128, 256]), [], ['kvgb'])
            dma('sp', wuv[:], wb['mla_w_uv'][l].rearrange("(kc p) n -> p kc n", p=128), [('mla_w_uv_b', l)], ['wuv'])
            memset('pool', wrbd[:].rearrange("p a b -> p (a b)"), 0.0, ['wrbd'])
            memset('pool', wibd[:].rearrange("p a b -> p (a b)"), 0.0, ['wibd'])
            for n in range(4):
                r0 = (n % 2) * 64
                dma('sp', wrbd[r0:r0 + 64, n // 2, r0:r0 + 64], wb['lru_w_r'][l, n * 64:(n + 1) * 64, :],
                    [('lru_w_r_b', l)], ['wrbd'], slow=True)
                dma('sp', wibd[r0:r0 + 64, n // 2, r0:r0 + 64], wb['lru_w_i'][l, n * 64:(n + 1) * 64, :],
                    [('lru_w_i_b', l)], ['wibd'], slow=True)
            if not sample:
                dma('sp', wsT[:], wsT_b[l].rearrange("h j i -> j h i"), ['wsT_b'], ['wsT'], slow=True)
                dma('sp', bsb[:], gmlp_b[l:l + 1, :, :].broadcast_to([64, 4, 128]), [], ['bsb'])
            else:
                memset('pool', wsT[:].rearrange("p a b -> p (a b)"), 0.0, ['wsT'])
                for s in range(NS):
                    dma('sp', wsT[s * 32:(s + 1) * 32, :, s * 32:(s + 1) * 32],
                        wsT_b[l, :, 0:32, 0:32].rearrange("h j i -> j h i"), ['wsT_b'], ['wsT'], slow=True)
                    dma('sp', bsb[:, :, s * 32:(s + 1) * 32], gmlp_b[l:l + 1, :, 0:32].broadcast_to([64, 4, 32]),
                        [], ['bsb'], slow=True)

        xbk = ['xb'] + [('xb', oc) for oc in range(8)]

        def layer_norm(l, c0, T, pg, pb, make_bf16):
            Xv = X[:, :, c0:c0 + T]
            xk = [('X', oc) for oc in range(8)]
            cp('pool', lnscr[:, :, 0:T], Xv, xk, ['lnscr'])
            act(sq[:, :, 0:T], Xv, AF.Square, xk, ['sq'])
            b = ps_get()
            for kc in range(8):
                mm(PS[b][:, 0:T], onesS[:], lnscr[:, kc, 0:T], kc == 0, kc == 7, ['onesS', 'lnscr'], ['ps%d' % b])
            b2 = ps_get()
            for kc in range(8):
                mm(PS[b2][:, 0:T], onesS[:], sq[:, kc, 0:T], kc == 0, kc == 7, ['onesS', 'sq'], ['ps%d' % b2])
            cp('act', meanb[:, 0:T], PS[b][:, 0:T], ['ps%d' % b], ['meanb'])
            tt('pool', rstdb[:, 0:T], meanb[:, 0:T], meanb[:, 0:T], ALU.mult, ['meanb'], ['rstdb'])
            tt('dve', rstdb[:, 0:T], PS[b2][:, 0:T], rstdb[:, 0:T], ALU.subtract, ['ps%d' % b2, 'rstdb'], ['rstdb'])
            act(rstdb[:, 0:T], rstdb[:, 0:T], AF.Ln, ['rstdb'], ['rstdb'], bias=LN_EPS)
            act(rstdb[:, 0:T], rstdb[:, 0:T], AF.Exp, ['rstdb'], ['rstdb'], scale=-0.5)
            tt('dve', Xv, Xv, meanb[:, None, 0:T].to_broadcast([128, 8, T]), ALU.subtract, xk + ['meanb'], xk)
            tt('dve', Xv, Xv, rstdb[:, None, 0:T].to_broadcast([128, 8, T]), ALU.mult, xk + ['rstdb'], xk)
            for oc in range(8):
                g = PAR[:, l, pg + oc:pg + oc + 1]
                bb = PAR[:, l, pb + oc:pb + oc + 1]
                if make_bf16:
                    ts('pool', xb[:, oc, 0:T], X[:, oc, c0:c0 + T], g, bb, ALU.mult, ALU.add,
                       [('X', oc), 'PAR'], [('xb', oc)])
                    act(X[:, oc, c0:c0 + T], X[:, oc, c0:c0 + T], AF.Identity, [('X', oc), 'PAR', ('xb', oc)],
                        [('X', oc)], bias=bb, scale=g)
                else:
                    act(X[:, oc, c0:c0 + T], X[:, oc, c0:c0 + T], AF.Identity, [('X', oc), 'PAR'],
                        [('X', oc)], bias=bb, scale=g)

        def tile(l, G, t, last_layer, part):
            sample = G[0] == 's'
            if sample:
                T, nseg, sl, c0 = NS * TS, NS, TS, 0
            else:
                T, nseg, sl, c0 = 256, 1, 256, t * 256
                sidx = G[1]
            nblk = T // 128
            Xv = X[:, :, c0:c0 + T]
            xk = [('X', oc) for oc in range(8)]
            wi = wb['w_in'][l].rearrange("(kc p) n -> p kc n", p=128)

            def wload(col0, ncols):
                rt, rk = ring_get()
                v = rt[:, 0:8 * ncols].rearrange("p (k n) -> p k n", n=ncols)
                dma('sp', v, wi[:, :, col0:col0 + ncols], [('w_in_b', l)], [rk])
                return v, rk

            if part == 'front':
                if sample:
                    dma('act', cosb[:, 0:T], coss_d, [], ['cosb'])
                    dma('act', sinb[:, 0:T], sins_d, [], ['sinb'])
                else:
                    dma('act', cosb[:, 0:T], cosp_d[:, c0:c0 + T], [], ['cosb'])
                    dma('act', sinb[:, 0:T], sinp_d[:, c0:c0 + T], [], ['sinb'])
                cp('pool', xb[:, :, 0:T], Xv, xk, xbk)
                w, rk = wload(CU, 256)
                for h in range(4):
                    b = ps_get()
                    for kc in range(8):
                        mm(PS[b][0:64, 0:T], w[:, kc, h * 64:(h + 1) * 64], xb[:, kc, 0:T], kc == 0, kc == 7,
                           [rk] + xbk, ['ps%d' % b])
                    act(uT[:, h, 0:T], PS[b][0:64, 0:T], AF.Gelu, ['ps%d' % b], ['uT'])
                w, rk = wload(CV, 256)
                for blk in range(nblk):
                    b = ps_get()
                    for kc in range(8):
                        mm(PS[b][:, 0:256], xb[:, kc, blk * 128:(blk + 1) * 128], w[:, kc, :], kc == 0, kc == 7,
                           [rk] + xbk, ['ps%d' % b])
                    if sample:
                        act(vf32[:], PS[b][:, 0:256], AF.Gelu, ['ps%d' % b], ['vf32'])
                        cp('pool', vtok[:, blk, :], vf32[:], ['vf32'], ['vtok'])
                        dma('act', o_sv[l, :, :], vf32[:], ['vf32'], [okey()])
                    else:
                        act(vtok[:, blk, :], PS[b][:, 0:256], AF.Gelu, ['ps%d' % b], ['vtok'])
                for blk in range(nblk):
                    b = ps_get()
                    for h in range(4):
                        mm(PS[b][0:64, h * 128:(h + 1) * 128], vtok[:, blk, h * 64:(h + 1) * 64], wsT[:, h, :],
                           True, True, ['vtok', 'wsT'], ['ps%d' % b])
                    tt('dve', gt[:].rearrange("p (h i) -> p h i", h=4), PS[b][0:64, :].rearrange("p (h i) -> p h i", h=4),
                       bsb[:], ALU.add, ['ps%d' % b, 'bsb'], ['gt'])
                    tt('pool', mA[0:64, :, blk * 128:(blk + 1) * 128], gt[:].rearrange("p (h i) -> p h i", h=4),
                       uT[:, :, blk * 128:(blk + 1) * 128], ALU.mult, ['gt', 'uT'], ['mA'])
                for oc in range(3):
                    if oc == 0:
                        w, rk = wload(CQ, 256)
                    elif oc == 2:
                        w, rk = wload(CQ + 256, 128)
                    ocl = oc % 2
                    b = ps_get()
                    for kc in range(8):
                        mm(PS[b][:, 0:T], w[:, kc, ocl * 128:(ocl + 1) * 128], xb[:, kc, 0:T], kc == 0, kc == 7,
                           [rk] + xbk, ['ps%d' % b])
                    act(cqT[:, oc, 0:T], PS[b][:, 0:T], AF.Identity, ['ps%d' % b, 'PAR'], ['cqT'],
                        scale=PAR[:, l, P_QG + oc:P_QG + oc + 1])
                    act(cqsq[:, oc, 0:T], PS[b][:, 0:T], AF.Square, ['ps%d' % b], ['cqsq'])
                b = ps_get()
                for oc in range(3):
                    mm(PS[b][:, 0:T], onesq[:], cqsq[:, oc, 0:T], oc == 0, oc == 2, ['onesq', 'cqsq'], ['ps%d' % b])
                act(rqb[:, 0:T], PS[b][:, 0:T], AF.Ln, ['ps%d' % b], ['rqb'], bias=RMS_EPS)
                act(rqb[:, 0:T], rqb[:, 0:T], AF.Exp, ['rqb'], ['rqb'], scale=-0.5)
                tt('pool', cosrq[:, 0:T], cosb[:, 0:T], rqb[0:32, 0:T], ALU.mult, ['cosb', 'rqb'], ['cosrq'])
                tt('pool', sinrq[:, 0:T], sinb[:, 0:T], rqb[0:32, 0:T], ALU.mult, ['sinb', 'rqb'], ['sinrq'])
                w, rk = wload(CKV, 256)
                for blk in range(nblk):
                    b = ps_get()
                    for kc in range(8):
                        mm(PS[b][:, 0:256], xb[:, kc, blk * 128:(blk + 1) * 128], w[:, kc, :], kc == 0, kc == 7,
                           [rk] + xbk, ['ps%d' % b])
                    act(ckvn[:, blk, :], PS[b][:, 0:256], AF.Square, ['ps%d' % b], ['ckvn', 'ss'], accum=ss[:, blk:blk + 1])
                    act(ss[:, blk:blk + 1], ss[:, blk:blk + 1], AF.Ln, ['ss'], ['ss'], bias=RMS_EPS, scale=1.0 / 256)
                    act(ss[:, blk:blk + 1], ss[:, blk:blk + 1], AF.Exp, ['ss'], ['ss'], scale=-0.5)
                    stt(ckvn[:, blk, :], PS[b][:, 0:256], ss[:, blk:blk + 1], kvgb[:, 0, :], ALU.mult, ALU.mult,
                        ['ps%d' % b, 'ss', 'kvgb'], ['ckvn'])
                    if sample:
                        dma('act', o_slat[l, :, :], ckvn[:, blk, :], ['ckvn'], [okey()])
                    else:
                        dma('act', o_plat[l, sidx, c0 + blk * 128:c0 + (blk + 1) * 128, :], ckvn[:, blk, :], ['ckvn'], [okey()])
                    b2 = ps_get()
                    for cc in range(2):
                        tr(PS[b2][:, cc * 128:(cc + 1) * 128], ckvn[:, blk, cc * 128:(cc + 1) * 128], ident[:],
                           ['ckvn', 'ident'], ['ps%d' % b2])
                    cp('act', ckvT[:, :, blk * 128:(blk + 1) * 128], PS[b2][:, 0:256].rearrange("p (c n) -> p c n", c=2),
                       ['ps%d' % b2], ['ckvT'])
                b = ps_get()
                for kc in range(8):
                    mm(PS[b][0:32, 0:T], wkr[:, kc, :], xb[:, kc, 0:T], kc == 0, kc == 7, ['wkr'] + xbk, ['ps%d' % b])
                b2 = ps_get()
                for kc in range(8):
                    mm(PS[b2][0:32, 0:T], wkrp[:, kc, :], xb[:, kc, 0:T], kc == 0, kc == 7, ['wkrp'] + xbk, ['ps%d' % b2])
                tt('dve', krt1[:, 0:T], PS[b][0:32, 0:T], cosb[:, 0:T], ALU.mult, ['ps%d' % b, 'cosb'], ['krt1'])
                tt('dve', krot[:, 0:T], PS[b2][0:32, 0:T], sinb[:, 0:T], ALU.mult, ['ps%d' % b2, 'sinb'], ['krot'])
                tt('pool', krot[:, 0:T], krot[:, 0:T], krt1[:, 0:T], ALU.add, ['krot', 'krt1'], ['krot'])
                b = ps_get()
                for blk in range(nblk):
                    tr(PS[b][:, blk * 32:(blk + 1) * 32], krot[:, blk * 128:(blk + 1) * 128], ident[0:32, 0:32],
                       ['krot', 'ident'], ['ps%d' % b])
                cp('dve', krtok[:, 0:nblk, :], PS[b][:, 0:nblk * 32].rearrange("p (c n) -> p c n", n=32), ['ps%d' % b], ['krtok'])
                for blk in range(nblk):
                    if sample:
                        dma('act', o_skr[l, :, :], krtok[:, blk, :], ['krtok'], [okey()])
                    else:
                        dma('act', o_pkr[l, sidx, c0 + blk * 128:c0 + (blk + 1) * 128, :], krtok[:, blk, :], ['krtok'], [okey()])
                for h in range(8):
                    b = ps_get()
                    for kc in range(3):
                        mm(PS[b][0:96, 0:T], wq[:, kc, h, :], cqT[:, kc, 0:T], kc == 0, kc == 2, ['wq', 'cqT'], ['ps%d' % b])
                    b2 = ps_get()
                    for kc in range(3):
                        mm(PS[b2][0:32, 0:T], wqp[:, kc, h, :], cqT[:, kc, 0:T], kc == 0, kc == 2, ['wqp', 'cqT'], ['ps%d' % b2])
                    tt('dve', qT[0:96, h, 0:T], PS[b][0:96, 0:T], rqb[0:96, 0:T], ALU.mult, ['ps%d' % b, 'rqb'], [('qT', h)])
                    tt('dve', qt1[:, 0:T], PS[b][0:32, 0:T], cosrq[:, 0:T], ALU.mult, ['ps%d' % b, 'cosrq'], ['qt1'])
                    tt('dve', qt2[:, 0:T], PS[b2][0:32, 0:T], sinrq[:, 0:T], ALU.mult, ['ps%d' % b2, 'sinrq'], ['qt2'])
                    tt('pool', qT[0:32, h, 0:T], qt1[:, 0:T], qt2[:, 0:T], ALU.add, ['qt1', 'qt2', ('qT', h)], [('qT', h)])
            if part == 'rest':
                seglen = sl + 3
                xcw = [xcin[:, c, 0:nseg * seglen].rearrange("p (s n) -> p s n", n=seglen) for c in range(2)]
                w, rk = wload(CXC, 256)
                if sample:
                    fsrc = ffnc[l].rearrange("s j (c p) -> (s j c) p", p=128)
                    dma('act', stage[0:128, 0:128], fsrc[0:128, :], [], ['stage'])
                    dma('act', stage[0:128, 128:256], fsrc[128:256, :], [], ['stage'])
                    dma('act', stage[0:96, 256:384], fsrc[256:352, :], [], ['stage'])
                    dma('act', stage[0:24, 384:512], lruc[l].rearrange("s j (c p) -> (s j c) p", p=128), [], ['stage'])
                    dma('act', stage[0:8, 512:640], lruh[l].rearrange("s (c p) -> (s c) p", p=128), [], ['stage'])
                    b = ps_get()
                    tr(PS[b][:, 0:128], stage[0:128, 0:128], ident[:], ['stage', 'ident'], ['ps%d' % b])
                    tr(PS[b][:, 128:256], stage[0:128, 128:256], ident[:], ['stage', 'ident'], ['ps%d' % b])
                    tr(PS[b][:, 256:352], stage[0:96, 256:384], ident[0:96, 0:96], ['stage', 'ident'], ['ps%d' % b])
                    tr(PS[b][:, 352:376], stage[0:24, 384:512], ident[0:24, 0:24], ['stage', 'ident'], ['ps%d' % b])
                    tr(PS[b][:, 376:384], stage[0:8, 512:640], ident[0:8, 0:8], ['stage', 'ident'], ['ps%d' % b])
                    cp('dve', fcar[:].rearrange("p c s j -> p s j c"),
                       PS[b][:, 0:352].rearrange("p (s j c) -> p s j c", s=4, j=2), ['ps%d' % b], ['fcar'])
                    for c in range(2):
                        cp('dve', xcw[c][:, :, 0:3], PS[b][:, 352:376].rearrange("p (s j c) -> p s j c", s=4, j=3)[:, :, :, c],
                           ['ps%d' % b], [('xcin', c)])
                    cp('dve', hcar[:, :, 0:NS], PS[b][:, 376:384].rearrange("p (s c) -> p c s", c=2), ['ps%d' % b], ['hcar'])
                elif t == 0:
                    for c in range(2):
                        memset('pool', xcw[c][:, 0, 0:3], 0.0, [('xcin', c)])
                    memset('pool', hcar[:, :, 0:1], 0.0, ['hcar'])
                for c in range(2):
                    b = ps_get()
                    for kc in range(8):
                        mm(PS[b][:, 0:T], w[:, kc, c * 128:(c + 1) * 128], xb[:, kc, 0:T], kc == 0, kc == 7,
                           [rk] + xbk, ['ps%d' % b])
                    cp('act', xcw[c][:, :, 3:3 + sl], PS[b][:, 0:T].rearrange("p (s n) -> p s n", n=sl),
                       ['ps%d' % b], [('xcin', c)])
                w, rk = wload(CG, 256)
                for c in range(2):
                    b = ps_get()
                    for kc in range(8):
                        mm(PS[b][:, 0:T], w[:, kc, c * 128:(c + 1) * 128], xb[:, kc, 0:T], kc == 0, kc == 7,
                           [rk] + xbk, ['ps%d' % b])
                    act(gg[:, c, 0:T], PS[b][:, 0:T], AF.Gelu, ['ps%d' % b], ['gg'])
                for c in range(2):
                    xv3 = xcv[:, 0:T].rearrange("p (s n) -> p s n", n=sl)
                    lw = lambda j: PAR[:, l, P_LCW + 2 * j + c:P_LCW + 2 * j + c + 1]
                    ts('dve', xv3, xcw[c][:, :, 3:3 + sl], lw(3), PAR[:, l, P_LCB + c:P_LCB + c + 1], ALU.mult, ALU.add,
                       [('xcin', c), 'PAR'], ['xcv'])
                    for j in range(3):
                        stt(xv3, xcw[c][:, :, j:j + sl], lw(j), xv3, ALU.mult, ALU.add, [('xcin', c), 'PAR', 'xcv'], ['xcv'])
                    cp('pool', xcb[:, c, 0:T], xcv[:, 0:T], ['xcv'], ['xcb'])
                    if sample:
                        cp('pool', sstg[:, 0:24].rearrange("p (s j c) -> p s j c", s=4, j=3)[:, :, :, c], xcw[c][:, :, sl:sl + 3],
                           [('xcin', c)], ['sstg_in'])
                        if c == 1:
                            b_ = ps_get()
                            tr(PS[b_][0:24, 0:128], sstg[:, 0:24], ident[:], ['sstg_in', 'ident'], ['ps%d' % b_])
                            cp('dve', sstg[0:24, 32:160], PS[b_][0:24, 0:128], ['ps%d' % b_], ['sstg_out'])
                            dma('act', o_sc[l].rearrange("s j (c p) -> (s j c) p", p=128), sstg[0:24, 32:160], ['sstg_out'], [okey()])
                    else:
                        if t == SEQ // 256 - 1:
                            cp('pool', sstg[:, 0:6].rearrange("p (j c) -> p j c", c=2)[:, :, c], xcw[c][:, 0, sl:sl + 3],
                               [('xcin', c)], ['sstg_in'])
                            if c == 1:
                                b_ = ps_get()
                                tr(PS[b_][0:6, 0:128], sstg[:, 0:6], ident[:], ['sstg_in', 'ident'], ['ps%d' % b_])
                                cp('dve', sstg[0:6, 32:160], PS[b_][0:6, 0:128], ['ps%d' % b_], ['sstg_out'])
                                dma('act', o_pc[l, sidx].rearrange("j (c p) -> (j c) p", p=128), sstg[0:6, 32:160], ['sstg_out'], [okey()])
                        else:
                            cp('pool', xcw[c][:, 0, 0:3], xcw[c][:, 0, sl:sl + 3], [('xcin', c)], [('xcin', c)])
                    b = ps_get()
                    mm(PS[b][:, 0:T], wrbd[:, c, :], xcb[:, c, 0:T], True, True, ['wrbd', 'xcb'], ['ps%d' % b])
                    b2 = ps_get()
                    mm(PS[b2][:, 0:T], wibd[:, c, :], xcb[:, c, 0:T], True, True, ['wibd', 'xcb'], ['ps%d' % b2])
                    act(la[:, 0:T], PS[b][:, 0:T], AF.Sigmoid, ['ps%d' % b, 'PAR'], ['la'], bias=PAR[:, l, P_BR + c:P_BR + c + 1])
                    act(lig[:, 0:T], PS[b2][:, 0:T], AF.Sigmoid, ['ps%d' % b2, 'PAR'], ['lig'], bias=PAR[:, l, P_BI + c:P_BI + c + 1])
                    act(ls[:, 0:T], la[:, 0:T], AF.Exp, ['la', 'C1'], ['ls'], scale=C1[:, l, 2 + c:3 + c])
                    act(la[:, 0:T], la[:, 0:T], AF.Exp, ['la', 'C1'], ['la'], scale=C1[:, l, c:c + 1])
                    act(ls[:, 0:T], ls[:, 0:T], AF.Ln, ['ls'], ['ls'], bias=1.0, scale=-1.0)
                    act(ls[:, 0:T], ls[:, 0:T], AF.Exp, ['ls'], ['ls'], scale=0.5)
                    tt('pool', lig[:, 0:T], lig[:, 0:T], xcv[:, 0:T], ALU.mult, ['lig', 'xcv'], ['lig'])
                    tt('pool', lig[:, 0:T], lig[:, 0:T], ls[:, 0:T], ALU.mult, ['lig', 'ls'], ['lig'])
                    for s in range(nseg):
                        S.op('dve', lambda e, s=s, c=c: e.tensor_tensor_scan(
                            out=lh[:, c, s * sl:(s + 1) * sl], data0=la[:, s * sl:(s + 1) * sl],
                            data1=lig[:, s * sl:(s + 1) * sl], initial=hcar[:, c, s:s + 1], op0=ALU.mult, op1=ALU.add),
                            reads=['la', 'lig', 'hcar'], writes=['lh'])
                        cp('pool', hcar[:, c, s:s + 1], lh[:, c, (s + 1) * sl - 1:(s + 1) * sl], ['lh'], ['hcar'])
                    tt('pool', mC[:, c, 0:T], lh[:, c, 0:T], gg[:, c, 0:T], ALU.mult, ['lh', 'gg'], ['mC'])
                if sample:
                    cp('pool', sstg[:, 24:32].rearrange("p (s c) -> p c s", c=2), hcar[:, :, 0:NS], ['hcar'], ['sstg_in'])
                    b_ = ps_get()
                    tr(PS[b_][0:8, 0:128], sstg[:, 24:32], ident[:], ['sstg_in', 'ident'], ['ps%d' % b_])
                    cp('dve', sstg[0:8, 32:160], PS[b_][0:8, 0:128], ['ps%d' % b_], ['sstg_out'])
                    dma('act', o_sh[l].rearrange("s (c p) -> (s c) p", p=128), sstg[0:8, 32:160], ['sstg_out'], [okey()])
                elif t == SEQ // 256 - 1:
                    cp('pool', sstg[:, 24:26], hcar[:, :, 0], ['hcar'], ['sstg_in'])
                    b_ = ps_get()
                    tr(PS[b_][0:2, 0:128], sstg[:, 24:26], ident[:], ['sstg_in', 'ident'], ['ps%d' % b_])
                    cp('dve', sstg[0:2, 32:160], PS[b_][0:2, 0:128], ['ps%d' % b_], ['sstg_out'])
                    dma('act', o_ph[l, sidx, :].rearrange("(c p) -> c p", p=128), sstg[0:2, 32:160], ['sstg_out'], [okey()])
                if sample:
                    sample_attention(l)
                else:
                    prompt_attention(l, t, c0)
                wo = wb['w_o'][l]
                for oc in range(8):
                    rt, rk = ring_get()
                    cs = slice(oc * 128, (oc + 1) * 128)
                    wA = rt[0:64, 0:512].rearrange("p (h n) -> p h n", n=128)
                    wB = rt[0:64, 512:1536].rearrange("p (h n) -> p h n", n=128)
                    wC = rt[:, 1536:1792].rearrange("p (h n) -> p h n", n=128)
                    dma('sp', wA, wo[0:256, cs].rearrange("(h p) n -> p h n", p=64), [('w_o_b', l)], [rk])
                    dma('sp', wB, wo[256:768, cs].rearrange("(h p) n -> p h n", p=64), [('w_o_b', l)], [rk])
                    dma('sp', wC, wo[768:1024, cs].rearrange("(h p) n -> p h n", p=128), [('w_o_b', l)], [rk])
                    b = ps_get()
                    n = 0
                    for h in range(4):
                        mm(PS[b][:, 0:T], rt[:, 0:512].rearrange("p (h n) -> p h n", n=128)[:, h, :], mA[:, h, 0:T], n == 0, False, [rk, 'mA'], ['ps%d' % b]); n += 1
                    for h in range(8):
                        mm(PS[b][:, 0:T], rt[:, 512:1536].rearrange("p (h n) -> p h n", n=128)[:, h, :], mB[:, h, 0:T], False, False, [rk, ('mB', h)], ['ps%d' % b]); n += 1
                    for c in range(2):
                        mm(PS[b][:, 0:T], wC[:, c, :], mC[:, c, 0:T], False, c == 1, [rk, 'mC'], ['ps%d' % b]); n += 1
                    stt(X[:, oc, c0:c0 + T], X[:, oc, c0:c0 + T], ALPHA, PS[b][:, 0:T], ALU.mult, ALU.add,
                        [('X', oc), 'ps%d' % b], [('X', oc)])
                layer_norm(l, c0, T, P_LN1G, P_LN1B, True)
                wu = wb['ffn_w_up'][l].rearrange("(kc p) n -> p kc n", p=128)
                seg2 = sl + 2
                pend = None
                for c in range(22):
                    if c % 2 == 0:
                        rtg, rkg = ring_get()
                        rtv, rkv = ring_get()
                        wg = rtg[:, 0:2048].rearrange("p (k n) -> p k n", n=256)
                        wv = rtv[:, 0:2048].rearrange("p (k n) -> p k n", n=256)
                        dma('sp', wg, wu[:, :, c * 128:c * 128 + 256], [('ffn_w_up_b', l)], [rkg])
                        dma('sp', wv, wu[:, :, DFF + c * 128:DFF + c * 128 + 256], [('ffn_w_up_b', l)], [rkv])
                    o = (c % 2) * 128
                    b = ps_get()
                    for kc in range(8):
                        mm(PS[b][:, 0:T], wg[:, kc, o:o + 128], xb[:, kc, 0:T], kc == 0, kc == 7, [rkg] + xbk, ['ps%d' % b])
                    for kc in range(8):
                        mm(PS[b][:, 256:256 + T], wv[:, kc, o:o + 128], xb[:, kc, 0:T], kc == 0, kc == 7, [rkv] + xbk, ['ps%d' % b])
                    ub = upb[c % 2]
                    ubk = 'upb%d' % (c % 2)
                    cg = cgb2[c % 2]
                    cgk = 'cgb%d' % (c % 2)
                    u4 = ub[:, :, 0:nseg * seg2].rearrange("p g (s n) -> p g s n", n=seg2)
                    pv4 = PS[b][:, :].rearrange("p (g c) -> p g c", g=2)[:, :, 0:T].rearrange("p g (s n) -> p g s n", n=sl)
                    if sample:
                        for gv in range(2):
                            cp('pool', u4[:, gv, :, 0:2], fcar[:, c + 22 * gv, :, :], ['fcar'], [ubk])
                    elif t == 0:
                        memset('pool', u4[:, :, 0, 0:2], 0.0, [ubk])
                    else:
                        cp('pool', u4[:, :, 0, 0:2], fcar[:, c:c + 23:22, 0, :], ['fcar'], [ubk])
                    cp('act', u4[:, :, :, 2:2 + sl], pv4, ['ps%d' % b], [ubk])
                    for gv in range(2):
                        cp('pool', fcar[:, c + 22 * gv, 0:nseg, :], u4[:, gv, :, sl:sl + 2], [ubk], ['fcar'])
                    for gv in range(2):
                        ch = c + 22 * gv
                        cw = lambda j: PAR[:, l, P_FCW + 44 * j + ch:P_FCW + 44 * j + ch + 1]
                        o3 = cg[:, gv, 0:T].rearrange("p (s n) -> p s n", n=sl)
                        act(o3, pv4[:, gv, :, :], AF.Identity, ['ps%d' % b, 'PAR'], [(cgk, gv)], bias=PAR[:, l, P_FCB + ch:P_FCB + ch + 1],
                            scale=cw(2))
                        for j in range(2):
                            stt(o3, u4[:, gv, :, j:j + sl], cw(j), o3, ALU.mult, ALU.add, [ubk, 'PAR', (cgk, gv)], [(cgk, gv)])
                    if pend is not None:
                        pend()

                    def _tail(cg=cg, cgk=cgk, c=c):
                        act(cg[:, 0, 0:T], cg[:, 0, 0:T], AF.Gelu, [(cgk, 0)], [(cgk, 0)])
                        tt('pool', hT[:, c, 0:T], cg[:, 0, 0:T], cg[:, 1, 0:T], ALU.mult, [(cgk, 0), (cgk, 1)], [('hT', c)])
                    pend = _tail
                pend()
                if sample:
                    stg_ = upb[0][:].rearrange("p g n -> p (g n)")[:, 0:352]
                    ostg_ = upb[1][:].rearrange("p g n -> p (g n)")[:, 0:384]
                    cp('pool', stg_.rearrange("p (s j c) -> p s j c", s=4, j=2), fcar[:].rearrange("p c s j -> p s j c"),
                       ['fcar'], ['upb0'])
                    b_ = ps_get()
                    tr(PS[b_][:, 0:128], stg_[:, 0:128], ident[:], ['upb0', 'ident'], ['ps%d' % b_])
                    tr(PS[b_][:, 128:256], stg_[:, 128:256], ident[:], ['upb0', 'ident'], ['ps%d' % b_])
                    tr(PS[b_][0:96, 256:384], stg_[:, 256:352], ident[:], ['upb0', 'ident'], ['ps%d' % b_])
                    cp('dve', ostg_[:, 0:256], PS[b_][:, 0:256], ['ps%d' % b_], ['upb1'])
                    cp('dve', ostg_[0:96, 256:384], PS[b_][0:96, 256:384], ['ps%d' % b_], ['upb1'])
                    fdst = o_sf[l].rearrange("s j (c p) -> (s j c) p", p=128)
                    dma('act', fdst[0:128, :], ostg_[:, 0:128], ['upb1'], [okey()])
                    dma('act', fdst[128:256, :], ostg_[:, 128:256], ['upb1'], [okey()])
                    dma('act', fdst[256:352, :], ostg_[0:96, 256:384], ['upb1'], [okey()])
                elif t == SEQ // 256 - 1:
                    stg_ = upb[0][:].rearrange("p g n -> p (g n)")[:, 0:88]
                    ostg_ = upb[1][:].rearrange("p g n -> p (g n)")[:, 0:128]
                    cp('pool', stg_.rearrange("p (j c) -> p j c", j=2), fcar[:, :, 0, :].rearrange("p c j -> p j c"), ['fcar'], ['upb0'])
                    b_ = ps_get()
                    tr(PS[b_][0:88, 0:128], stg_, ident[:], ['upb0', 'ident'], ['ps%d' % b_])
                    cp('dve', ostg_[0:88, :], PS[b_][0:88, 0:128], ['ps%d' % b_], ['upb1'])
                    dma('act', o_pf[l, sidx].rearrange("j (c p) -> (j c) p", p=128), ostg_[0:88, :], ['upb1'], [okey()])
                wdn = wb['ffn_w_down'][l].rearrange("(kc p) n -> p kc n", p=128)
                for op_ in range(4):
                    bb = (ps_get(hold=True), ps_get(hold=True))
                    for (k0, k1) in ((0, 8), (8, 16), (16, 22)):
                        rt, rk = ring_get()
                        w2 = rt[:, 0:(k1 - k0) * 256].rearrange("p (k n) -> p k n", n=256)
                        dma('sp', w2, wdn[:, k0:k1, op_ * 256:(op_ + 1) * 256], [('ffn_w_down_b', l)], [rk])
                        for j_ in range(2):
                            b = bb[j_]
                            for kc in range(k0, k1):
                                mm(PS[b][:, 0:T], w2[:, kc - k0, j_ * 128:(j_ + 1) * 128], hT[:, kc, 0:T], kc == 0, kc == 21,
                                   [rk, ('hT', kc)], ['ps%d' % b], skip=True)
                    for j_ in range(2):
                        b = bb[j_]
                        oc = op_ * 2 + j_
                        stt(X[:, oc, c0:c0 + T], X[:, oc, c0:c0 + T], ALPHA, PS[b][:, 0:T], ALU.mult, ALU.add,
                            [('X', oc), 'ps%d' % b], [('X', oc)])
                        ps_rel(b)
            if part == 'ln2':
                layer_norm(l, c0, T, P_LN2G, P_LN2B, False)
                if last_layer:
                    for blk in range(nblk):
                        for half in range(2):
                            b = ps_get()
                            for q4 in range(4):
                                oc = half * 4 + q4
                                tr(PS[b][:, q4 * 128:(q4 + 1) * 128], X[:, oc, c0 + blk * 128:c0 + (blk + 1) * 128], ident[:],
                                   [('X', oc), 'ident'], ['ps%d' % b])
                            cp('act' if half else 'dve', otok[:, half * 512:(half + 1) * 512], PS[b][:, :], ['ps%d' % b], [('otok', half)])
                        if sample:
                            dma('act', ys[:, :], otok[:], [('otok', 0), ('otok', 1)], [okey()])
                        else:
                            dma('act', yp[sidx, c0 + blk * 128:c0 + (blk + 1) * 128, :], otok[:], [('otok', 0), ('otok', 1)], [okey()])

        def prompt_attention(l, t, c0):
            T = 256
            for hp in range(4):
                b = ps_get()
                for hh in range(2):
                    h = hp * 2 + hh
                    for kc in range(2):
                        mm(PS[b][0:96, hh * 256:hh * 256 + T], wuk[:, kc, h, :], ckvT[:, kc, 0:T], kc == 0, kc == 1,
                           ['wuk', 'ckvT'], ['ps%d' % b])
                cp('act', KT[0:96, hp * 2:hp * 2 + 2, c0:c0 + T], PS[b][0:96, :].rearrange("p (h n) -> p h n", h=2),
                   ['ps%d' % b], [('KT', hp * 2), ('KT', hp * 2 + 1)])
            cp('pool', KT[0:32, :, c0:c0 + T], krot[:, None, 0:T].to_broadcast([32, 8, T]), ['krot'] + [('KT', h) for h in range(8)],
               [('KT', h) for h in range(8)])
            for blk in range(2):
                kb = t * 2 + blk
                b = ps_get()
                for kc in range(2):
                    mm(PS[b][:, :], ckvT[:, kc, blk * 128:(blk + 1) * 128], wuv[:, kc, :], kc == 0, kc == 1,
                       ['ckvT', 'wuv'], ['ps%d' % b])
                cp('act', VA[:, kb, 0:512], PS[b][:, :], ['ps%d' % b], ['VA'])
            nkb = 2 * t + 2
            blocks = [(h, kb) for h in range(8) for kb in range(nkb)]
            LOOK = 2
            info = {}
            accs = {}

            def emit_qk(i):
                h, kb = blocks[i]
                q0 = 128 if kb == nkb - 1 else 0
                b = ps_get()
                mm(PS[b][:, q0:T], KT[:, h, kb * 128:(kb + 1) * 128], qT[:, h, q0:T], True, True,
                   [('KT', h), ('qT', h)], ['ps%d' % b])
                pt = PT[i % 4]
                pk = 'pT%d' % (i % 4)
                act(pt[:, q0:T], PS[b][:, q0:T], AF.Exp, ['ps%d' % b], [pk], scale=ATTN_SCALE)
                if kb >= nkb - 2:
                    m0 = 0 if kb == nkb - 2 else 128
                    memset('pool', pt[64:128, m0:m0 + 64], 0.0, [pk])
                info[i] = (pt, pk, q0)

            def emit_pv(i):
                h, kb = blocks[i]
                pt, pk, q0 = info.pop(i)
                if kb == 0:
                    accs[h] = (ps_get(hold=True), ps_get(hold=True))
                ob, db = accs[h]
                mm(PS[ob][:, q0:T], VA[:, kb, h * 64:h * 64 + 128], pt[:, q0:T], kb == 0, kb == nkb - 1,
                   ['VA', pk], ['ps%d' % ob], skip=True)
                mm(PS[db][:, q0:T], ones1[:, :], pt[:, q0:T], kb == 0, kb == nkb - 1,
                   ['ones1', pk], ['ps%d' % db], skip=True)
                if kb == nkb - 1:
                    act(bcs[:, 0:T], PS[db][0:64, 0:T], AF.Ln, ['ps%d' % db], ['bcs'])
                    act(bcs[:, 0:T], bcs[:, 0:T], AF.Exp, ['bcs'], ['bcs'], scale=-1.0)
                    tt('dve', mB[0:64, h, 0:T], PS[ob][0:64, 0:T], bcs[:, 0:T], ALU.mult, ['ps%d' % ob, 'bcs'], [('mB', h)])
                    ps_rel(ob)
                    ps_rel(db)
                    del accs[h]

            for i in range(len(blocks) + LOOK):
                if i < len(blocks):
                    emit_qk(i)
                if i - LOOK >= 0:
                    emit_pv(i - LOOK)

        KTf = KT[:].rearrange("p a b -> p (a b)")

        def sample_attention(l):
            T = NS * TS
            wukT = KTf[0:96, 0:2048].rearrange("p (h c) -> p h c", h=8)
            qlat = KTf[:, 2048:4096].rearrange("p (c h n) -> p c h n", c=2, h=8)
            olat = KTf[:, 4096:6144].rearrange("p (c h n) -> p c h n", c=2, h=8)
            cst = [KTf[:, 6144 + i * 2304:6144 + (i + 1) * 2304].bitcast(F32).rearrange("p (k n) -> p k n", n=288)
                   for i in range(2)]
            cbf = [KTf[:, 10752 + i * 1152:10752 + (i + 1) * 1152].rearrange("p (k n) -> p k n", n=288) for i in range(2)]
            cT = [KTf[:, 13056 + i * 1024:13056 + (i + 1) * 1024].rearrange("p (c n) -> p c n", c=2) for i in range(2)]
            krT = [KTf[0:32, 15104 + i * 512:15104 + (i + 1) * 512] for i in range(2)]
            cnew = KTf[:, 16128:16384]
            for h in range(8):
                b = ps_get()
                for cc in range(2):
                    tr(psb(b)[0:96, cc * 128:(cc + 1) * 128], wuk[:, cc, h, :], identb[:], ['wuk', 'identb'], ['ps%d' % b])
                cp('act', wukT[:, h, :], psb(b)[0:96, 0:256], ['ps%d' % b], ['wukT'])
            for h in range(8):
                b = ps_get()
                for cc in range(2):
                    mm(PS[b][:, cc * 128:cc * 128 + T], wukT[:, h, cc * 128:(cc + 1) * 128], qT[0:96, h, 0:T], True, True,
                       ['wukT', ('qT', h)], ['ps%d' % b])
                cp('act', qlat[:, :, h, :], PS[b][:, 0:256].rearrange("p (c n) -> p c n", c=2), ['ps%d' % b], ['qlat'])
            nfull = PAST // 128
            cT4 = [KTf[:, 13056 + i * 256:13056 + (i + 1) * 256].rearrange("p (c n) -> p c n", c=2) for i in range(4)]
            krT4 = [KTf[0:32, 15104 + i * 128:15104 + (i + 1) * 128] for i in range(4)]
            for s in range(NS):
                accs = [ps_get(hold=True) for _ in range(3)]
                groups = [(g0, min(4, nfull - g0)) for g0 in range(0, nfull, 4)] + [(-1, 1)]
                blks = [(gi, g0, k) for gi, (g0, ng) in enumerate(groups) for k in range(ng)]
                nb_ = len(blks)
                st = {}

                def stA(i, s=s, blks=blks, groups=groups, st=st):
                    gi, g0, k = blks[i]
                    ng = groups[gi][1]
                    i2 = gi % 2
                    sk, bk = 'cst%d' % i2, 'cbf%d' % i2
                    i4 = i % 4
                    tk = 'cT4_%d' % i4
                    if g0 >= 0:
                        if k == 0:
                            dma('sp', cst[i2][:, 0:ng, 0:256],
                                clat[l, s, g0 * 128:(g0 + ng) * 128, :].rearrange("(k p) n -> p k n", p=128), [], [sk])
                            dma('sp', cst[i2][:, 0:ng, 256:288],
                                ckr[l, s, g0 * 128:(g0 + ng) * 128, :].rearrange("(k p) n -> p k n", p=128), [], [sk])
                            cp('pool', cbf[i2][:, 0:ng, :], cst[i2][:, 0:ng, :], [sk], [bk])
                        b = ps_get()
                        for cc in range(2):
                            tr(psb(b)[:, cc * 128:(cc + 1) * 128], cbf[i2][:, k, cc * 128:(cc + 1) * 128], identb[:],
                               [bk, 'identb'], ['ps%d' % b])
                        tr(psb(b)[0:32, 256:384], cbf[i2][:, k, 256:288], identb[:], [bk, 'identb'], ['ps%d' % b])
                        cp('act', cT4[i4][:, :, 0:128], psb(b)[:, 0:256].rearrange("p (c n) -> p c n", c=2), ['ps%d' % b], [tk])
                        cp('dve', krT4[i4][:, 0:128], psb(b)[0:32, 256:384], ['ps%d' % b], [tk])
                        st[i] = dict(nk=128, lat=[cT4[i4][:, cc, 0:128] for cc in range(2)], kr=krT4[i4][:, 0:128],
                                     cv=cbf[i2][:, k, 0:256], on=ones1[:, :], tk=tk, bk=bk)
                    else:
                        if s == 0:
                            cp('pool', cnew[:], ckvn[:, 0, :], ['ckvn'], ['cnew'])
                        dma('act', cbf[i2][0:32, 0, 0:256], cnew[s * 32:(s + 1) * 32, :], ['cnew'], [bk])
                        cp('pool', krT4[i4][:, 0:32], krot[:, s * 32:(s + 1) * 32], ['krot'], [tk])
                        st[i] = dict(nk=32, lat=[ckvT[:, cc, s * 32:(s + 1) * 32] for cc in range(2)], kr=krT4[i4][:, 0:32],
                                     cv=cbf[i2][0:32, 0, 0:256], on=ones1[0:32, :], tk=tk, bk=bk)

                def stB(i, s=s, st=st):
                    d = st[i]
                    nk = d['nk']
                    b = ps_get()
                    sv = PS[b][0:nk, 0:256].rearrange("p (h n) -> p h n", h=8)
                    for cc in range(2):
                        mm(sv, d['lat'][cc], qlat[:, cc, :, s * 32:(s + 1) * 32], cc == 0, False,
                           [d['tk'], 'ckvT', 'qlat'], ['ps%d' % b])
                    mm(sv, d['kr'], qT[0:32, :, s * 32:(s + 1) * 32], False, True,
                       [d['tk']] + [('qT', h) for h in range(8)], ['ps%d' % b])
                    pt = PT[i % 4]
                    pk = 'pT%d' % (i % 4)
                    act(pt[0:nk, 0:256], PS[b][0:nk, 0:256], AF.Exp, ['ps%d' % b], [pk], scale=ATTN_SCALE)
                    d['ptv'] = pt[0:nk, 0:256]
                    d['pk'] = pk

                def stC(i, st=st, accs=accs, nb_=nb_):
                    d = st.pop(i)
                    first = i == 0
                    last = i == nb_ - 1
                    for cc in range(2):
                        mm(PS[accs[cc]][:, 0:256], d['cv'][:, cc * 128:(cc + 1) * 128], d['ptv'], first, last,
                           [d['bk'], d['pk']], ['ps%d' % accs[cc]], skip=True)
                    mm(PS[accs[2]][:, 0:256], d['on'], d['ptv'], first, last, ['ones1', d['pk']], ['ps%d' % accs[2]], skip=True)

                for i in range(nb_ + 2):
                    if i < nb_:
                        stA(i)
                    if 0 <= i - 1 < nb_:
                        stB(i - 1)
                    if 0 <= i - 2 < nb_:
                        stC(i - 2)
                S.op('dve', lambda e, a=accs[2]: e.reciprocal(out=rden[:, 0:256], in_=PS[a][:, 0:256]),
                     reads=['ps%d' % accs[2]], writes=['rden'])
                for cc in range(2):
                    tt('dve', olat[:, cc, :, s * 32:(s + 1) * 32], PS[accs[cc]][:, 0:256].rearrange("p (h n) -> p h n", h=8),
                       rden[:, 0:256].rearrange("p (h n) -> p h n", h=8), ALU.mult, ['ps%d' % accs[cc], 'rden'], ['olat'])
                for a in accs:
                    ps_rel(a)
            for h in range(8):
                b = ps_get()
                for cc in range(2):
                    mm(PS[b][0:64, 0:T], wuv[:, cc, h * 64:(h + 1) * 64], olat[:, cc, h, :], cc == 0, cc == 1,
                       ['wuv', 'olat'], ['ps%d' % b])
                cp('act', mB[0:64, h, 0:T], PS[b][0:64, 0:T], ['ps%d' % b], [('mB', h)])

        groups = []
        if cfg.do_prompt:
            groups += [('p', s) for s in range(NP)]
        if cfg.do_sample:
            groups += [('s',)]
        for gi_, G in enumerate(groups):
            S.barrier(skip_rings=('pool',) if gi_ == 0 else ())
            if G[0] == 'p':
                for blk in range(SEQ // 128):
                    dma('sp', stage[:], xp[G[1], blk * 128:(blk + 1) * 128, :], [], ['stage'])
                    for half in range(2):
                        b = ps_get()
                        for q4 in range(4):
                            kc = half * 4 + q4
                            tr(PS[b][:, q4 * 128:(q4 + 1) * 128], stage[:, kc * 128:(kc + 1) * 128], ident[:],
                               ['stage', 'ident'], ['ps%d' % b])
                        cp('act' if half else 'dve', X[:, half * 4:half * 4 + 4, blk * 128:(blk + 1) * 128],
                           PS[b][:, :].rearrange("p (k n) -> p k n", k=4), ['ps%d' % b], [('X', half * 4 + q) for q in range(4)])
                ntile = SEQ // 256
            else:
                dma('sp', stage[:], xs[:, :], [], ['stage'])
                for half in range(2):
                    b = ps_get()
                    for q4 in range(4):
                        kc = half * 4 + q4
                        tr(PS[b][:, q4 * 128:(q4 + 1) * 128], stage[:, kc * 128:(kc + 1) * 128], ident[:],
                           ['stage', 'ident'], ['ps%d' % b])
                    cp('act' if half else 'dve', X[:, half * 4:half * 4 + 4, 0:128],
                       PS[b][:, :].rearrange("p (k n) -> p k n", k=4), ['ps%d' % b], [('X', half * 4 + q) for q in range(4)])
                ntile = 1
            for l in range(L):
                load_layer(l, G[0] == 's')
                tile(l, G, 0, l == L - 1, 'front')
                for t in range(ntile):
                    tile(l, G, t, l == L - 1, 'rest')
                    if t + 1 < ntile:
                        tile(l, G, t + 1, l == L - 1, 'front')
                    tile(l, G, t, l == L - 1, 'ln2')
        S.barrier()
        S.emit(block)
    return nc


def _consts(cfg):
    TS = 32
    inv = 1.0 / (10000.0 ** (np.arange(0, 32, 2, dtype=np.float32) / 32.0))

    def tab(pos):
        ang = pos.astype(np.float32)[None, :] * inv[:, None].astype(np.float32)
        c = np.cos(ang).astype(np.float32)
        s = np.sin(ang).astype(np.float32)
        return np.concatenate([c, c], 0), np.concatenate([-s, s], 0)
    cp_, sp_ = tab(np.arange(cfg.SEQ))
    c1, s1 = tab(cfg.PAST + np.arange(TS))
    return {
        'c_ident': np.eye(128, dtype=np.float32),
        'c_tril': np.tril(np.ones((128, 128), np.float32)),
        'c_cosp': np.ascontiguousarray(cp_), 'c_sinp': np.ascontiguousarray(sp_),
        'c_coss': np.ascontiguousarray(np.tile(c1, (1, cfg.NS))), 'c_sins': np.ascontiguousarray(np.tile(s1, (1, cfg.NS))),
    }


def make_in_maps(cfg, inputs, ncores):
    L, NP, NS = cfg.L, cfg.NP, cfg.NS
    f = lambda a: np.ascontiguousarray(np.asarray(a, dtype=np.float32))
    consts = _consts(cfg)
    maps = []
    for c in range(ncores):
        m = dict(consts)
        m['xp'] = f(inputs['x_prompt'][c * NP:(c + 1) * NP])
        m['xs'] = f(inputs['x_sample'][c * NS:(c + 1) * NS]).reshape(NS * 32, D)
        m['clat'] = f(inputs['cache_kv_latent'][:, c * NS:(c + 1) * NS])
        m['ckr'] = f(inputs['cache_k_rope'][:, c * NS:(c + 1) * NS])
        m['lruh'] = f(inputs['state_lru_h'][:, c * NS:(c + 1) * NS])
        m['lruc'] = f(inputs['state_lru_conv'][:, c * NS:(c + 1) * NS])
        m['ffnc'] = f(inputs['state_ffn_conv'][:, c * NS:(c + 1) * NS])
        for n, s in WEIGHTS:
            m[n] = f(inputs[n]).reshape([L] + s)
        for n, s in VECS:
            m[n] = f(inputs[n])
        m['gmlp_w_s'] = f(inputs['gmlp_w_s'])
        m['gmlp_b_s'] = f(inputs['gmlp_b_s'])
        m['mla_kv_norm_g'] = f(inputs['mla_kv_norm_g'])
        m['lru_conv_w'] = f(inputs['lru_conv_w'])
        m['ffn_conv_w'] = f(inputs['ffn_conv_w'])
        maps.append(m)
    return maps


def gather(cfg, results):
    L, NP, NS = cfg.L, cfg.NP, cfg.NS
    cat = lambda k, ax: np.concatenate([r[k] for r in results], axis=ax)
    y_p = cat('yp', 0)
    y_s = cat('ys', 0).reshape(-1, 32, D)
    p_lat = cat('o_plat', 1)
    p_kr = cat('o_pkr', 1)
    p_h = cat('o_ph', 1)
    p_c = cat('o_pc', 1)
    p_f = cat('o_pf', 1)
    s_lat = np.concatenate([r['o_slat'].reshape(L, NS, 32, 256) for r in results], axis=1)
    s_kr = np.concatenate([r['o_skr'].reshape(L, NS, 32, 32) for r in results], axis=1)
    s_v = np.concatenate([r['o_sv'].reshape(L, NS, 32, 256) for r in results], axis=1)
    s_h = cat('o_sh', 1)
    s_c = cat('o_sc', 1)
    s_f = cat('o_sf', 1)
    return tuple(np.ascontiguousarray(a, dtype=np.float32) for a in
                 (y_p, y_s, p_lat, p_kr, p_h, p_c, p_f, s_lat, s_kr, s_v, s_h, s_c, s_f))


def kernel(**inputs):
    cfg = Cfg()
    nc = build_nc(cfg)
    in_maps = make_in_maps(cfg, inputs, NCORES)
    res = run_bass_kernel_spmd(nc, in_maps, core_ids=list(range(NCORES)))
    return gather(cfg, res.results)
```

```python
import contextlib
import numpy as np
import concourse.bass as bass
import concourse.mybir as mybir
from concourse.bass_utils import run_bass_kernel_spmd

F32 = mybir.dt.float32
BF16 = mybir.dt.bfloat16
AF = mybir.ActivationFunctionType
ALU = mybir.AluOpType

D = 1024
DIN = 1696
DFF = 2816
NCORES = 8
ALPHA = (2.0 * 4) ** 0.25
LN_EPS = 1e-5
RMS_EPS = 1e-6
ATTN_SCALE = 96 ** -0.5
CU, CV, CQ, CKV, CKR, CXC, CG = 0, 256, 512, 896, 1152, 1184, 1440


class Sched:
    ENGS = ('pe', 'act', 'dve', 'pool', 'sp')

    def __init__(self, nc, stack, rings):
        self.nc = nc
        self.ops = {e: [] for e in self.ENGS}
        self.cnt = {e: 0 for e in self.ENGS}
        self.sems = {}
        for e in self.ENGS:
            self.sems[('e', e)] = stack.enter_context(nc.semaphore('s_' + e))
        self.rings = {}
        for q, n in rings.items():
            self.rings[q] = n
            for i in range(n):
                self.sems[('r', q, i)] = stack.enter_context(nc.semaphore('r_%s%d' % (q, i)))
        self.ring_n = {q: 0 for q in rings}
        self.waited = {e: {} for e in self.ENGS}
        self.last_w = {}
        self.readers = {}
        self.nops = 0
        self.alias = {}

    def _exp(self, keys):
        out = []
        for k in keys:
            out.extend(self.alias.get(k, (k,)))
        return out

    def _need(self, eng, semid, val, src, waits):
        if src == eng and eng == 'pe':
            return
        if self.waited[eng].get(semid, 0) >= val:
            return
        if waits.get(semid, 0) < val:
            waits[semid] = val

    def op(self, eng, fn, reads=(), writes=(), dma=False):
        reads = self._exp(reads)
        writes = self._exp(writes)
        waits = {}
        for k in reads:
            lw = self.last_w.get(k)
            if lw is not None:
                self._need(eng, lw[0], lw[1], lw[2], waits)
        for k in writes:
            lw = self.last_w.get(k)
            if lw is not None:
                self._need(eng, lw[0], lw[1], lw[2], waits)
            rd = self.readers.get(k)
            if rd:
                for semid, (val, se) in rd.items():
                    self._need(eng, semid, val, se, waits)
        if dma:
            q = eng
            i = self.ring_n[q]
            self.ring_n[q] += 1
            R = self.rings[q]
            semid = ('r', q, i % R)
            prev = 16 * (i // R)
            if prev > 0 and self.waited[eng].get(semid, 0) < prev and waits.get(semid, 0) < prev:
                waits[semid] = prev
            val = prev + 16
            inc = 16
            src = 'dma'
        else:
            self.cnt[eng] += 1
            semid = ('e', eng)
            val = self.cnt[eng]
            inc = 1
            src = eng
        for sid, v in waits.items():
            if self.waited[eng].get(sid, 0) < v:
                self.waited[eng][sid] = v
        self.ops[eng].append((list(waits.items()), fn, semid, inc))
        for k in reads:
            self.readers.setdefault(k, {})[semid] = (val, src)
        for k in writes:
            self.last_w[k] = (semid, val, src)
            self.readers[k] = {}
        self.nops += 1

    def barrier(self, skip_rings=()):
        cur = {}
        for e in self.ENGS:
            if self.cnt[e] > 0:
                cur[('e', e)] = self.cnt[e]
        for q, R in self.rings.items():
            if q in skip_rings:
                continue
            n = self.ring_n[q]
            for s in range(R):
                k = (n - 1 - s) // R + 1 if n - 1 - s >= 0 else 0
                if k > 0:
                    cur[('r', q, s)] = 16 * k
        for e in self.ENGS:
            waits = []
            for sid, v in cur.items():
                if sid == ('e', e):
                    continue
                if self.waited[e].get(sid, 0) < v:
                    waits.append((sid, v))
                    self.waited[e][sid] = v
            self.ops[e].append((waits, None, None, 0))

    def emit(self, block):
        sems = self.sems

        def run(name):
            def f(eng):
                for waits, fn, semid, inc in self.ops[name]:
                    for sid, v in waits:
                        eng.wait_ge(sems[sid], v)
                    if fn is None:
                        continue
                    fn(eng).then_inc(sems[semid], inc)
            return f
        block.tensor(run('pe'))
        block.scalar(run('act'))
        block.vector(run('dve'))
        block.gpsimd(run('pool'))
        block.sync(run('sp'))


class Cfg:
    def __init__(self, L=4, SEQ=2048, NP=2, NS=4, PAST=4096, do_prompt=True, do_sample=True):
        self.L, self.SEQ, self.NP, self.NS, self.PAST = L, SEQ, NP, NS, PAST
        self.do_prompt, self.do_sample = do_prompt, do_sample


WEIGHTS = [
    ('w_in', [D, DIN]), ('w_o', [D, D]), ('mla_w_uq', [384, 768]), ('mla_w_uk', [256, 512]),
    ('mla_w_uv', [256, 512]), ('lru_w_r', [256, 64]), ('lru_w_i', [256, 64]),
    ('ffn_w_up', [D, 2 * DFF]), ('ffn_w_down', [DFF, D]),
]
VECS = [('ln1_g', D), ('ln1_b', D), ('ln2_g', D), ('ln2_b', D), ('mla_q_norm_g', 384),
        ('lru_conv_b', 256), ('lru_b_r', 256), ('lru_b_i', 256), ('lru_lam', 256),
        ('ffn_conv_b', 2 * DFF)]


def build_nc(cfg):
    L, SEQ, NP, NS, PAST = cfg.L, cfg.SEQ, cfg.NP, cfg.NS, cfg.PAST
    TS = 32
    nc = bass.Bass("TRN2", target_bir_lowering=False, dynamic_dma_scratch_size=2048)
    din = lambda n, s: nc.dram_tensor(n, s, F32, kind="ExternalInput").ap()
    dout = lambda n, s: nc.dram_tensor(n, s, F32, kind="ExternalOutput").ap()
    xp = din('xp', [NP, SEQ, D])
    xs = din('xs', [NS * TS, D])
    clat = din('clat', [L, NS, PAST, 256])
    ckr = din('ckr', [L, NS, PAST, 32])
    lruh = din('lruh', [L, NS, 256])
    lruc = din('lruc', [L, NS, 3, 256])
    ffnc = din('ffnc', [L, NS, 2, 2 * DFF])
    wd = {n: din(n, [L] + s) for n, s in WEIGHTS}
    vd = {n: din(n, [L, s]) for n, s in VECS}
    gmlp_w = din('gmlp_w_s', [L, 4, 128, 128])
    gmlp_b = din('gmlp_b_s', [L, 4, 128])
    kvg = din('mla_kv_norm_g', [L, 256])
    lcw = din('lru_conv_w', [L, 4, 256])
    fcw = din('ffn_conv_w', [L, 3, 2 * DFF])
    ident_d = din('c_ident', [128, 128])
    tril_d = din('c_tril', [128, 128])
    cosp_d = din('c_cosp', [32, SEQ])
    sinp_d = din('c_sinp', [32, SEQ])
    coss_d = din('c_coss', [32, NS * TS])
    sins_d = din('c_sins', [32, NS * TS])

    yp = dout('yp', [NP, SEQ, D])
    ys = dout('ys', [NS * TS, D])
    o_plat = dout('o_plat', [L, NP, SEQ, 256])
    o_pkr = dout('o_pkr', [L, NP, SEQ, 32])
    o_ph = dout('o_ph', [L, NP, 256])
    o_pc = dout('o_pc', [L, NP, 3, 256])
    o_pf = dout('o_pf', [L, NP, 2, 2 * DFF])
    o_slat = dout('o_slat', [L, NS * TS, 256])
    o_skr = dout('o_skr', [L, NS * TS, 32])
    o_sv = dout('o_sv', [L, NS * TS, 256])
    o_sh = dout('o_sh', [L, NS, 256])
    o_sc = dout('o_sc', [L, NS, 3, 256])
    o_sf = dout('o_sf', [L, NS, 2, 2 * DFF])

    wb = {n: nc.dram_tensor(n + '_b', [L] + s, BF16).ap() for n, s in WEIGHTS}
    wsT_b = nc.dram_tensor('wsT_b', [L, 4, 128, 128], BF16).ap()

    with contextlib.ExitStack() as st:
        S = Sched(nc, st, {'sp': 16, 'act': 16, 'pool': 8})
        sbt = lambda n, s, d: st.enter_context(nc.sbuf_tensor(n, s, d))
        XW = max(SEQ, NS * TS)
        X = sbt('X', [128, 8, XW], F32)
        KT = sbt('KT', [128, 8, max(SEQ, 2048)], BF16)
        NKB = SEQ // 128
        VA = sbt('VA', [128, NKB, 576], BF16)
        ident = sbt('ident', [128, 128], F32)
        identb = sbt('identb', [128, 128], BF16)
        tril = sbt('tril', [128, 128], F32)
        onesS = sbt('onesS', [128, 128], BF16)
        ones1 = sbt('ones1', [128, 128], BF16)
        onesq = sbt('onesq', [128, 128], BF16)
        NV = 32 + 3 + 8 + 2 + 2 + 2 + 2 + 132 + 44 + 4
        P_NBR, P_NBI = 227, 229
        PAR = sbt('PAR', [128, L, NV], F32)
        P_LN1G, P_LN1B, P_LN2G, P_LN2B, P_QG, P_LCW, P_LCB, P_BR, P_BI, P_LAM, P_FCW, P_FCB = \
            0, 8, 16, 24, 32, 35, 43, 45, 47, 49, 51, 183
        C1 = sbt('C1', [128, L, 4], F32)
        kvgb = sbt('kvgb', [128, 1, 256], F32)
        wq = sbt('wq', [128, 3, 8, 96], BF16)
        wqp = sbt('wqp', [128, 3, 8, 32], BF16)
        wkr = sbt('wkr', [128, 8, 32], BF16)
        wkrp = sbt('wkrp', [128, 8, 32], BF16)
        wuk = sbt('wuk', [128, 2, 8, 96], BF16)
        wuv = sbt('wuv', [128, 2, 512], BF16)
        wrbd = sbt('wrbd', [128, 2, 128], BF16)
        wibd = sbt('wibd', [128, 2, 128], BF16)
        wsT = sbt('wsT', [128, 4, 128], BF16)
        bsb = sbt('bsb', [64, 4, 128], F32)
        RING = [sbt('ring%d' % i, [128, 2048], BF16) for i in range(5)]
        TM = 256
        xb = sbt('xb', [128, 8, TM], BF16)
        uT = sbt('uT', [64, 4, TM], BF16)
        vtok = sbt('vtok', [128, 2, 256], BF16)
        cqT = sbt('cqT', [128, 3, TM], BF16)
        rqb = sbt('rqb', [128, TM], F32)
        cosrq = sbt('cosrq', [32, TM], F32)
        sinrq = sbt('sinrq', [32, TM], F32)
        cosb = sbt('cosb', [32, TM], F32)
        sinb = sbt('sinb', [32, TM], F32)
        ckvn = sbt('ckvn', [128, 2, 256], F32)
        ss = sbt('ss', [128, 8], F32)
        ckvT = sbt('ckvT', [128, 2, TM], BF16)
        krot = sbt('krot', [32, TM], F32)
        krtok = sbt('krtok', [128, 2, 32], F32)
        xcin = sbt('xcin', [128, 2, 4 * 35 + 224], F32)
        xcv = sbt('xcv', [128, TM], F32)
        xcb = sbt('xcb', [128, 2, TM], BF16)
        la = sbt('la', [128, TM], F32)
        lig = sbt('lig', [128, TM], F32)
        ls = sbt('ls', [128, TM], F32)
        hcar = sbt('hcar', [128, 2, 4], F32)
        gg = sbt('gg', [128, 2, TM], BF16)
        qT = sbt('qT', [128, 8, TM], BF16)
        qt1 = sbt('qt1', [32, TM], F32)
        qt2 = sbt('qt2', [32, TM], F32)
        PT = [sbt('pT%d' % i, [128, TM], BF16) for i in range(4)]
        mA = sbt('mA', [128, 4, TM], BF16)
        mB = sbt('mB', [128, 8, TM], BF16)
        mC = sbt('mC', [128, 2, TM], BF16)
        gtt = sbt('gtt', [128, 512], F32)
        gt = gtt[0:64, :]
        rden = gtt[:, 0:256]
        bcs = gtt[0:64, 256:512]
        meanb = la
        rstdb = lig
        hT = sbt('hT', [128, 22, TM], BF16)
        upb = [sbt('upb%d' % i, [128, 2, 4 * 34 + 124], F32) for i in range(2)]
        cgb = sbt('cgb', [128, 2, TM], F32)
        cgbB = sbt('cgbB', [128, 2, TM], F32)
        cgb2 = [cgb, cgbB]
        fcar = sbt('fcar', [128, 44, 4, 2], F32)
        stage = hT[:].rearrange("p a b -> p (a b)")[:, 0:2048].bitcast(F32)
        otok = stage
        S.alias.update({'sq': [('hT', c) for c in range(8)], 'lh': [('cgb0', 0), ('cgb0', 1)], 'krt1': ['qt1'],
                        'vf32': ['xcv'], 'meanb': ['la'], 'rstdb': ['lig'], 'rden': ['gtt'], 'bcs': ['gtt'],
                        'gt': ['gtt'], ('otok', 0): [('hT', c) for c in range(8)], ('otok', 1): [('hT', c) for c in range(8)],
                        'stage': [('hT', c) for c in range(8)], 'stage2': [('hT', c) for c in range(8)],
                        'cqsq': [('hT', 16), ('hT', 17), ('hT', 18)], 'lnscr': [('hT', c) for c in range(8, 16)]})
        cqsq = hT[:, 16:19, :]
        lnscr = hT[:, 8:16, :]
        sq = hT[:, 0:8, :]
        lh = cgb
        krt1 = qt1
        vf32 = xcv
        sstg = sbt('sstg', [128, 160], F32)
        PS = [st.enter_context(nc.psum_tensor('ps%d' % i, [128, 512], F32)) for i in range(8)]
        block = st.enter_context(nc.Block())

        def mm(out, lhsT, rhs, start, stop, rd, wr, skip=False):
            S.op('pe', lambda e: e.matmul(out, lhsT, rhs, start=start, stop=stop, skip_group_check=skip),
                 reads=rd, writes=wr)

        def tr(out, in_, idn, rd, wr):
            S.op('pe', lambda e: e.transpose(out, in_, idn), reads=rd, writes=wr)

        def act(out, in_, func, rd, wr, bias=None, scale=None, accum=None):
            kw = {}
            if bias is not None:
                kw['bias'] = bias
            if scale is not None:
                kw['scale'] = scale
            if accum is not None:
                kw['accum_out'] = accum
            S.op('act', lambda e: e.activation(out=out, in_=in_, func=func, **kw), reads=rd, writes=wr)

        def tt(eng, out, in0, in1, op, rd, wr):
            S.op(eng, lambda e: e.tensor_tensor(out=out, in0=in0, in1=in1, op=op), reads=rd, writes=wr)

        def ts(eng, out, in0, s1, s2, op0, op1, rd, wr):
            S.op(eng, lambda e: e.tensor_scalar(out=out, in0=in0, scalar1=s1, scalar2=s2, op0=op0, op1=op1),
                 reads=rd, writes=wr)

        def stt(out, in0, sc, in1, op0, op1, rd, wr):
            S.op('dve', lambda e: e.scalar_tensor_tensor(out=out, in0=in0, scalar=sc, in1=in1, op0=op0, op1=op1),
                 reads=rd, writes=wr)

        def cp(eng, out, in_, rd, wr):
            if eng == 'act':
                act(out, in_, AF.Copy, rd, wr)
            else:
                S.op(eng, lambda e: e.tensor_copy(out=out, in_=in_), reads=rd, writes=wr)

        def memset(eng, ap, v, wr):
            S.op(eng, lambda e: e.memset(ap, v), writes=wr)

        def dma(q, out, in_, rd, wr, slow=False):
            S.op(q, lambda e: e.dma_start(out=out, in_=in_, allow_slow_non_contiguous=slow),
                 reads=rd, writes=wr, dma=True)

        psn = [0]
        ps_open = set()

        def ps_get(hold=False):
            for _ in range(8):
                b = psn[0] % 8
                psn[0] += 1
                if b not in ps_open:
                    if hold:
                        ps_open.add(b)
                    return b
            raise RuntimeError("psum exhausted")

        def ps_rel(b):
            ps_open.discard(b)

        def psb(b):
            return PS[b][:].bitcast(BF16)

        ringn = [0]

        def ring_get():
            i = ringn[0] % 5
            ringn[0] += 1
            return RING[i], 'ring%d' % i

        outn = [0]

        def okey():
            outn[0] += 1
            return ('out', outn[0])

        dma('sp', ident[:], ident_d, [], ['ident'])
        dma('sp', tril[:], tril_d, [], ['tril'])
        cp('dve', identb[:], ident[:], ['ident'], ['identb'])
        memset('dve', onesS[:], 1.0 / 1024, ['onesS'])
        memset('dve', ones1[:], 1.0, ['ones1'])
        memset('dve', onesq[:], 1.0 / 384, ['onesq'])
        memset('pool', VA[:].rearrange("p a b -> p (a b)"), 0.0, ['VA'])
        memset('pool', wuk[:].rearrange("p a b c -> p (a b c)"), 0.0, ['wuk'])
        memset('pool', hcar[:].rearrange("p a b -> p (a b)"), 0.0, ['hcar'])
        memset('pool', qT[:].rearrange("p a b -> p (a b)"), 0.0, [('qT', h) for h in range(8)])
        memset('pool', KT[:].rearrange("p a b -> p (a b)"), 0.0, [('KT', h) for h in range(8)])
        memset('pool', mA[:].rearrange("p a b -> p (a b)"), 0.0, ['mA'])
        memset('pool', mB[:].rearrange("p a b -> p (a b)"), 0.0, [('mB', h) for h in range(8)])
        for i_ in range(5):
            memset('dve', RING[i_][:], 0.0, ['ring%d' % i_])
        for l_ in range(L):
            for n, s in WEIGHTS:
                rows = s[0] * s[1] // 2048
                src = wd[n][l_].rearrange("a b -> (a b)").rearrange("(r c) -> r c", c=2048)
                dst = wb[n][l_].rearrange("a b -> (a b)").rearrange("(r c) -> r c", c=2048)
                r0 = 0
                while r0 < rows:
                    r1 = min(rows, r0 + 1920)
                    dma('pool', dst[r0:r1, :], src[r0:r1, :], [], [(n + '_b', l_)])
                    r0 = r1

        lcn = [0]

        def load_cols(dst, vec, n):
            k = 2 + lcn[0] % 6
            lcn[0] += 1
            sv = stage[0:n, k * 128:(k + 1) * 128]
            dma('sp', sv, vec.rearrange("(c p) -> c p", p=128), [], [('stgs', k)])
            b = ps_get()
            tr(PS[b][:, 0:n], sv, ident[0:n, 0:n], [('stgs', k), 'ident'], ['ps%d' % b])
            cp('dve', dst, PS[b][:, 0:n], ['ps%d' % b], ['PAR'])

        for l in range(L):
            for nme, off, n in (('ln1_g', P_LN1G, 8), ('ln1_b', P_LN1B, 8), ('ln2_g', P_LN2G, 8),
                                ('ln2_b', P_LN2B, 8), ('mla_q_norm_g', P_QG, 3), ('lru_conv_b', P_LCB, 2),
                                ('lru_b_r', P_BR, 2), ('lru_b_i', P_BI, 2), ('lru_lam', P_LAM, 2),
                                ('ffn_conv_b', P_FCB, 44)):
                load_cols(PAR[:, l, off:off + n], vd[nme][l, :], n)
            for j in range(4):
                load_cols(PAR[:, l, P_LCW + 2 * j:P_LCW + 2 * j + 2], lcw[l, j, :], 2)
            for j in range(3):
                load_cols(PAR[:, l, P_FCW + 44 * j:P_FCW + 44 * j + 44], fcw[l, j, :], 44)
            ts('dve', PAR[:, l, P_NBR:P_NBR + 2], PAR[:, l, P_BR:P_BR + 2], -1.0, None, ALU.mult, ALU.bypass, ['PAR'], ['PAR'])
            ts('dve', PAR[:, l, P_NBI:P_NBI + 2], PAR[:, l, P_BI:P_BI + 2], -1.0, None, ALU.mult, ALU.bypass, ['PAR'], ['PAR'])
            act(C1[:, l, 0:2], PAR[:, l, P_LAM:P_LAM + 2], AF.Exp, ['PAR'], ['C1'], scale=-1.0)
            act(C1[:, l, 0:2], C1[:, l, 0:2], AF.Ln, ['C1'], ['C1'], bias=1.0)
            ts('dve', C1[:, l, 2:4], C1[:, l, 0:2], -16.0, None, ALU.mult, ALU.bypass, ['C1'], ['C1'])
            ts('dve', C1[:, l, 0:2], C1[:, l, 0:2], -8.0, None, ALU.mult, ALU.bypass, ['C1'], ['C1'])
            for h in range(4):
                dma('sp', stage[:, 0:128], gmlp_w[l, h, :, :], [], [('stgs', 0)])
                tt('dve', stage[:, 128:256], stage[:, 0:128], tril[:], ALU.mult, [('stgs', 0), 'tril'], [('stgs', 1)])
                b = ps_get()
                tr(PS[b][:, 0:128], stage[:, 128:256], ident[:], [('stgs', 1), 'ident'], ['ps%d' % b])
                cp('dve', xb[:, 0, 0:128], PS[b][:, 0:128], ['ps%d' % b], ['xb'])
                dma('sp', wsT_b[l, h, :, :], xb[:, 0, 0:128], ['xb'], ['wsT_b'])

        def load_layer(l, sample):
            uq = wb['mla_w_uq'][l].rearrange("(kc p) (h c) -> p kc h c", p=128, c=96)
            for kc in range(3):
                dma('sp', wq[:, kc, :, 0:32], uq[:, kc, :, 64:96], [('mla_w_uq_b', l)], ['wq'], slow=True)
                dma('sp', wq[:, kc, :, 32:96], uq[:, kc, :, 0:64], [('mla_w_uq_b', l)], ['wq'], slow=True)
                dma('sp', wqp[:, kc, :, 0:16], uq[:, kc, :, 80:96], [('mla_w_uq_b', l)], ['wqp'], slow=True)
                dma('sp', wqp[:, kc, :, 16:32], uq[:, kc, :, 64:80], [('mla_w_uq_b', l)], ['wqp'], slow=True)
            wi = wb['w_in'][l].rearrange("(kc p) n -> p kc n", p=128)
            dma('sp', wkr[:], wi[:, :, CKR:CKR + 32], [('w_in_b', l)], ['wkr'], slow=True)
            dma('sp', wkrp[:, :, 0:16], wi[:, :, CKR + 16:CKR + 32], [('w_in_b', l)], ['wkrp'], slow=True)
            dma('sp', wkrp[:, :, 16:32], wi[:, :, CKR:CKR + 16], [('w_in_b', l)], ['wkrp'], slow=True)
            uk = wb['mla_w_uk'][l].rearrange("(kc p) (h c) -> p kc h c", p=128, c=64)
            for kc in range(2):
                dma('sp', wuk[:, kc, :, 32:96], uk[:, kc, :, :], [('mla_w_uk_b', l)], ['wuk'], slow=True)
            dma('sp', kvgb[:, 0, :], kvg[l:l + 1, :].broadcast_to([128, 256]), [], ['kvgb'])
            dma('sp', wuv[:], wb['mla_w_uv'][l].rearrange("(kc p) n -> p kc n", p=128), [('mla_w_uv_b', l)], ['wuv'])
            memset('pool', wrbd[:].rearrange("p a b -> p (a b)"), 0.0, ['wrbd'])
            memset('pool', wibd[:].rearrange("p a b -> p (a b)"), 0.0, ['wibd'])
            for n in range(4):
                r0 = (n % 2) * 64
                dma('sp', wrbd[r0:r0 + 64, n // 2, r0:r0 + 64], wb['lru_w_r'][l, n * 64:(n + 1) * 64, :],
                    [('lru_w_r_b', l)], ['wrbd'], slow=True)
                dma('sp', wibd[r0:r0 + 64, n // 2, r0:r0 + 64], wb['lru_w_i'][l, n * 64:(n + 1) * 64, :],
                    [('lru_w_i_b', l)], ['wibd'], slow=True)
            if not sample:
                dma('sp', wsT[:], wsT_b[l].rearrange("h j i -> j h i"), ['wsT_b'], ['wsT'], slow=True)
                dma('sp', bsb[:], gmlp_b[l:l + 1, :, :].broadcast_to([64, 4, 128]), [], ['bsb'])
            else:
                memset('pool', wsT[:].rearrange("p a b -> p (a b)"), 0.0, ['wsT'])
                for s in range(NS):
                    dma('sp', wsT[s * 32:(s + 1) * 32, :, s * 32:(s + 1) * 32],
                        wsT_b[l, :, 0:32, 0:32].rearrange("h j i -> j h i"), ['wsT_b'], ['wsT'], slow=True)
                    dma('sp', bsb[:, :, s * 32:(s + 1) * 32], gmlp_b[l:l + 1, :, 0:32].broadcast_to([64, 4, 32]),
                        [], ['bsb'], slow=True)

        xbk = ['xb'] + [('xb', oc) for oc in range(8)]

        def layer_norm(l, c0, T, pg, pb, make_bf16):
            Xv = X[:, :, c0:c0 + T]
            xk = [('X', oc) for oc in range(8)]
            cp('dve', lnscr[:, :, 0:T], Xv, xk, ['lnscr'])
            act(sq[:, :, 0:T], Xv, AF.Square, xk, ['sq'])
            b = ps_get()
            for kc in range(8):
                mm(PS[b][:, 0:T], onesS[:], lnscr[:, kc, 0:T], kc == 0, kc == 7, ['onesS', 'lnscr'], ['ps%d' % b])
            b2 = ps_get()
            for kc in range(8):
                mm(PS[b2][:, 0:T], onesS[:], sq[:, kc, 0:T], kc == 0, kc == 7, ['onesS', 'sq'], ['ps%d' % b2])
            cp('act', meanb[:, 0:T], PS[b][:, 0:T], ['ps%d' % b], ['meanb'])
            tt('pool', rstdb[:, 0:T], meanb[:, 0:T], meanb[:, 0:T], ALU.mult, ['meanb'], ['rstdb'])
            tt('dve', rstdb[:, 0:T], PS[b2][:, 0:T], rstdb[:, 0:T], ALU.subtract, ['ps%d' % b2, 'rstdb'], ['rstdb'])
            act(rstdb[:, 0:T], rstdb[:, 0:T], AF.Ln, ['rstdb'], ['rstdb'], bias=LN_EPS)
            act(rstdb[:, 0:T], rstdb[:, 0:T], AF.Exp, ['rstdb'], ['rstdb'], scale=-0.5)
            tt('dve', Xv, Xv, meanb[:, None, 0:T].to_broadcast([128, 8, T]), ALU.subtract, xk + ['meanb'], xk)
            tt('dve', Xv, Xv, rstdb[:, None, 0:T].to_broadcast([128, 8, T]), ALU.mult, xk + ['rstdb'], xk)
            for oc in range(8):
                g = PAR[:, l, pg + oc:pg + oc + 1]
                bb = PAR[:, l, pb + oc:pb + oc + 1]
                if make_bf16:
                    ts('pool' if oc % 2 else 'dve', xb[:, oc, 0:T], X[:, oc, c0:c0 + T], g, bb, ALU.mult, ALU.add,
                       [('X', oc), 'PAR'], [('xb', oc)])
                    act(X[:, oc, c0:c0 + T], X[:, oc, c0:c0 + T], AF.Identity, [('X', oc), 'PAR', ('xb', oc)],
                        [('X', oc)], bias=bb, scale=g)
                else:
                    act(X[:, oc, c0:c0 + T], X[:, oc, c0:c0 + T], AF.Identity, [('X', oc), 'PAR'],
                        [('X', oc)], bias=bb, scale=g)

        def tile(l, G, t, last_layer, part):
            sample = G[0] == 's'
            if sample:
                T, nseg, sl, c0 = NS * TS, NS, TS, 0
            else:
                T, nseg, sl, c0 = 256, 1, 256, t * 256
                sidx = G[1]
            nblk = T // 128
            Xv = X[:, :, c0:c0 + T]
            xk = [('X', oc) for oc in range(8)]
            wi = wb['w_in'][l].rearrange("(kc p) n -> p kc n", p=128)

            def wload(col0, ncols):
                rt, rk = ring_get()
                v = rt[:, 0:8 * ncols].rearrange("p (k n) -> p k n", n=ncols)
                dma('sp', v, wi[:, :, col0:col0 + ncols], [('w_in_b', l)], [rk])
                return v, rk

            if part == 'front':
                if sample:
                    dma('act', cosb[:, 0:T], coss_d, [], ['cosb'])
                    dma('act', sinb[:, 0:T], sins_d, [], ['sinb'])
                else:
                    dma('act', cosb[:, 0:T], cosp_d[:, c0:c0 + T], [], ['cosb'])
                    dma('act', sinb[:, 0:T], sinp_d[:, c0:c0 + T], [], ['sinb'])
                cp('act', xb[:, :, 0:T], Xv, xk, xbk)
                w, rk = wload(CU, 256)
                for h in range(4):
                    b = ps_get()
                    for kc in range(8):
                        mm(PS[b][0:64, 0:T], w[:, kc, h * 64:(h + 1) * 64], xb[:, kc, 0:T], kc == 0, kc == 7,
                           [rk] + xbk, ['ps%d' % b])
                    act(uT[:, h, 0:T], PS[b][0:64, 0:T], AF.Gelu, ['ps%d' % b], ['uT'])
                w, rk = wload(CV, 256)
                for blk in range(nblk):
                    b = ps_get()
                    for kc in range(8):
                        mm(PS[b][:, 0:256], xb[:, kc, blk * 128:(blk + 1) * 128], w[:, kc, :], kc == 0, kc == 7,
                           [rk] + xbk, ['ps%d' % b])
                    if sample:
                        act(vf32[:], PS[b][:, 0:256], AF.Gelu, ['ps%d' % b], ['vf32'])
                        cp('pool', vtok[:, blk, :], vf32[:], ['vf32'], ['vtok'])
                        dma('act', o_sv[l, :, :], vf32[:], ['vf32'], [okey()])
                    else:
                        act(vtok[:, blk, :], PS[b][:, 0:256], AF.Gelu, ['ps%d' % b], ['vtok'])
                w, rk = wload(CG, 256)
                for c in range(2):
                    b = ps_get()
                    for kc in range(8):
                        mm(PS[b][:, 0:T], w[:, kc, c * 128:(c + 1) * 128], xb[:, kc, 0:T], kc == 0, kc == 7,
                           [rk] + xbk, ['ps%d' % b])
                    act(gg[:, c, 0:T], PS[b][:, 0:T], AF.Gelu, ['ps%d' % b], ['gg'])
                for blk in range(nblk):
                    b = ps_get()
                    for h in range(4):
                        mm(PS[b][0:64, h * 128:(h + 1) * 128], vtok[:, blk, h * 64:(h + 1) * 64], wsT[:, h, :],
                           True, True, ['vtok', 'wsT'], ['ps%d' % b])
                    tt('dve', gt[:].rearrange("p (h i) -> p h i", h=4), PS[b][0:64, :].rearrange("p (h i) -> p h i", h=4),
                       bsb[:], ALU.add, ['ps%d' % b, 'bsb'], ['gt'])
                    tt('pool', mA[0:64, :, blk * 128:(blk + 1) * 128], gt[:].rearrange("p (h i) -> p h i", h=4),
                       uT[:, :, blk * 128:(blk + 1) * 128], ALU.mult, ['gt', 'uT'], ['mA'])
                for oc in range(3):
                    if oc == 0:
                        w, rk = wload(CQ, 256)
                    elif oc == 2:
                        w, rk = wload(CQ + 256, 128)
                    ocl = oc % 2
                    b = ps_get()
                    for kc in range(8):
                        mm(PS[b][:, 0:T], w[:, kc, ocl * 128:(ocl + 1) * 128], xb[:, kc, 0:T], kc == 0, kc == 7,
                           [rk] + xbk, ['ps%d' % b])
                    act(cqT[:, oc, 0:T], PS[b][:, 0:T], AF.Identity, ['ps%d' % b, 'PAR'], ['cqT'],
                        scale=PAR[:, l, P_QG + oc:P_QG + oc + 1])
                    act(cqsq[:, oc, 0:T], PS[b][:, 0:T], AF.Square, ['ps%d' % b], ['cqsq'])
                b = ps_get()
                for oc in range(3):
                    mm(PS[b][:, 0:T], onesq[:], cqsq[:, oc, 0:T], oc == 0, oc == 2, ['onesq', 'cqsq'], ['ps%d' % b])
                act(rqb[:, 0:T], PS[b][:, 0:T], AF.Ln, ['ps%d' % b], ['rqb'], bias=RMS_EPS)
                act(rqb[:, 0:T], rqb[:, 0:T], AF.Exp, ['rqb'], ['rqb'], scale=-0.5)
                tt('pool', cosrq[:, 0:T], cosb[:, 0:T], rqb[0:32, 0:T], ALU.mult, ['cosb', 'rqb'], ['cosrq'])
                tt('pool', sinrq[:, 0:T], sinb[:, 0:T], rqb[0:32, 0:T], ALU.mult, ['sinb', 'rqb'], ['sinrq'])
                w, rk = wload(CKV, 256)
                for blk in range(nblk):
                    b = ps_get()
                    for kc in range(8):
                        mm(PS[b][:, 0:256], xb[:, kc, blk * 128:(blk + 1) * 128], w[:, kc, :], kc == 0, kc == 7,
                           [rk] + xbk, ['ps%d' % b])
                    act(ckvn[:, blk, :], PS[b][:, 0:256], AF.Square, ['ps%d' % b], ['ckvn', 'ss'], accum=ss[:, blk:blk + 1])
                    act(ss[:, blk:blk + 1], ss[:, blk:blk + 1], AF.Ln, ['ss'], ['ss'], bias=RMS_EPS, scale=1.0 / 256)
                    act(ss[:, blk:blk + 1], ss[:, blk:blk + 1], AF.Exp, ['ss'], ['ss'], scale=-0.5)
                    stt(ckvn[:, blk, :], PS[b][:, 0:256], ss[:, blk:blk + 1], kvgb[:, 0, :], ALU.mult, ALU.mult,
                        ['ps%d' % b, 'ss', 'kvgb'], ['ckvn'])
                    if sample:
                        dma('act', o_slat[l, :, :], ckvn[:, blk, :], ['ckvn'], [okey()])
                    else:
                        dma('act', o_plat[l, sidx, c0 + blk * 128:c0 + (blk + 1) * 128, :], ckvn[:, blk, :], ['ckvn'], [okey()])
                    b2 = ps_get()
                    for cc in range(2):
                        tr(PS[b2][:, cc * 128:(cc + 1) * 128], ckvn[:, blk, cc * 128:(cc + 1) * 128], ident[:],
                           ['ckvn', 'ident'], ['ps%d' % b2])
                    cp('act', ckvT[:, :, blk * 128:(blk + 1) * 128], PS[b2][:, 0:256].rearrange("p (c n) -> p c n", c=2),
                       ['ps%d' % b2], ['ckvT'])
                b = ps_get()
                for kc in range(8):
                    mm(PS[b][0:32, 0:T], wkr[:, kc, :], xb[:, kc, 0:T], kc == 0, kc == 7, ['wkr'] + xbk, ['ps%d' % b])
                b2 = ps_get()
                for kc in range(8):
                    mm(PS[b2][0:32, 0:T], wkrp[:, kc, :], xb[:, kc, 0:T], kc == 0, kc == 7, ['wkrp'] + xbk, ['ps%d' % b2])
                tt('dve', krt1[:, 0:T], PS[b][0:32, 0:T], cosb[:, 0:T], ALU.mult, ['ps%d' % b, 'cosb'], ['krt1'])
                tt('dve', krot[:, 0:T], PS[b2][0:32, 0:T], sinb[:, 0:T], ALU.mult, ['ps%d' % b2, 'sinb'], ['krot'])
                tt('pool', krot[:, 0:T], krot[:, 0:T], krt1[:, 0:T], ALU.add, ['krot', 'krt1'], ['krot'])
                b = ps_get()
                for blk in range(nblk):
                    tr(PS[b][:, blk * 32:(blk + 1) * 32], krot[:, blk * 128:(blk + 1) * 128], ident[0:32, 0:32],
                       ['krot', 'ident'], ['ps%d' % b])
                cp('dve', krtok[:, 0:nblk, :], PS[b][:, 0:nblk * 32].rearrange("p (c n) -> p c n", n=32), ['ps%d' % b], ['krtok'])
                for blk in range(nblk):
                    if sample:
                        dma('act', o_skr[l, :, :], krtok[:, blk, :], ['krtok'], [okey()])
                    else:
                        dma('act', o_pkr[l, sidx, c0 + blk * 128:c0 + (blk + 1) * 128, :], krtok[:, blk, :], ['krtok'], [okey()])
                for h in range(8):
                    b = ps_get()
                    for kc in range(3):
                        mm(PS[b][0:96, 0:T], wq[:, kc, h, :], cqT[:, kc, 0:T], kc == 0, kc == 2, ['wq', 'cqT'], ['ps%d' % b])
                    b2 = ps_get()
                    for kc in range(3):
                        mm(PS[b2][0:32, 0:T], wqp[:, kc, h, :], cqT[:, kc, 0:T], kc == 0, kc == 2, ['wqp', 'cqT'], ['ps%d' % b2])
                    tt('dve', qT[0:96, h, 0:T], PS[b][0:96, 0:T], rqb[0:96, 0:T], ALU.mult, ['ps%d' % b, 'rqb'], [('qT', h)])
                    tt('dve', qt1[:, 0:T], PS[b][0:32, 0:T], cosrq[:, 0:T], ALU.mult, ['ps%d' % b, 'cosrq'], ['qt1'])
                    tt('dve', qt2[:, 0:T], PS[b2][0:32, 0:T], sinrq[:, 0:T], ALU.mult, ['ps%d' % b2, 'sinrq'], ['qt2'])
                    tt('pool', qT[0:32, h, 0:T], qt1[:, 0:T], qt2[:, 0:T], ALU.add, ['qt1', 'qt2', ('qT', h)], [('qT', h)])
            if part == 'rest':
                seglen = sl + 3
                xcw = [xcin[:, c, 0:nseg * seglen].rearrange("p (s n) -> p s n", n=seglen) for c in range(2)]
                w, rk = wload(CXC, 256)
                if sample:
                    fsrc = ffnc[l].rearrange("s j (c p) -> (s j c) p", p=128)
                    dma('act', stage[0:128, 0:128], fsrc[0:128, :], [], ['stage'])
                    dma('act', stage[0:128, 128:256], fsrc[128:256, :], [], ['stage'])
                    dma('act', stage[0:96, 256:384], fsrc[256:352, :], [], ['stage'])
                    dma('act', stage[0:24, 384:512], lruc[l].rearrange("s j (c p) -> (s j c) p", p=128), [], ['stage'])
                    dma('act', stage[0:8, 512:640], lruh[l].rearrange("s (c p) -> (s c) p", p=128), [], ['stage'])
                    b = ps_get()
                    tr(PS[b][:, 0:128], stage[0:128, 0:128], ident[:], ['stage', 'ident'], ['ps%d' % b])
                    tr(PS[b][:, 128:256], stage[0:128, 128:256], ident[:], ['stage', 'ident'], ['ps%d' % b])
                    tr(PS[b][:, 256:352], stage[0:96, 256:384], ident[0:96, 0:96], ['stage', 'ident'], ['ps%d' % b])
                    tr(PS[b][:, 352:376], stage[0:24, 384:512], ident[0:24, 0:24], ['stage', 'ident'], ['ps%d' % b])
                    tr(PS[b][:, 376:384], stage[0:8, 512:640], ident[0:8, 0:8], ['stage', 'ident'], ['ps%d' % b])
                    cp('dve', fcar[:].rearrange("p c s j -> p s j c"),
                       PS[b][:, 0:352].rearrange("p (s j c) -> p s j c", s=4, j=2), ['ps%d' % b], ['fcar'])
                    for c in range(2):
                        cp('dve', xcw[c][:, :, 0:3], PS[b][:, 352:376].rearrange("p (s j c) -> p s j c", s=4, j=3)[:, :, :, c],
                           ['ps%d' % b], [('xcin', c)])
                    cp('dve', hcar[:, :, 0:NS], PS[b][:, 376:384].rearrange("p (s c) -> p c s", c=2), ['ps%d' % b], ['hcar'])
                elif t == 0:
                    for c in range(2):
                        memset('pool', xcw[c][:, 0, 0:3], 0.0, [('xcin', c)])
                    memset('pool', hcar[:, :, 0:1], 0.0, ['hcar'])
                for c in range(2):
                    b = ps_get()
                    for kc in range(8):
                        mm(PS[b][:, 0:T], w[:, kc, c * 128:(c + 1) * 128], xb[:, kc, 0:T], kc == 0, kc == 7,
                           [rk] + xbk, ['ps%d' % b])
                    cp('act', xcw[c][:, :, 3:3 + sl], PS[b][:, 0:T].rearrange("p (s n) -> p s n", n=sl),
                       ['ps%d' % b], [('xcin', c)])
                for c in range(2):
                    xv3 = xcv[:, 0:T].rearrange("p (s n) -> p s n", n=sl)
                    lw = lambda j: PAR[:, l, P_LCW + 2 * j + c:P_LCW + 2 * j + c + 1]
                    ts('dve', xv3, xcw[c][:, :, 3:3 + sl], lw(3), PAR[:, l, P_LCB + c:P_LCB + c + 1], ALU.mult, ALU.add,
                       [('xcin', c), 'PAR'], ['xcv'])
                    for j in range(3):
                        stt(xv3, xcw[c][:, :, j:j + sl], lw(j), xv3, ALU.mult, ALU.add, [('xcin', c), 'PAR', 'xcv'], ['xcv'])
                    cp('pool', xcb[:, c, 0:T], xcv[:, 0:T], ['xcv'], ['xcb'])
                    if sample:
                        cp('pool', sstg[:, 0:24].rearrange("p (s j c) -> p s j c", s=4, j=3)[:, :, :, c], xcw[c][:, :, sl:sl + 3],
                           [('xcin', c)], ['sstg_in'])
                        if c == 1:
                            b_ = ps_get()
                            tr(PS[b_][0:24, 0:128], sstg[:, 0:24], ident[:], ['sstg_in', 'ident'], ['ps%d' % b_])
                            cp('dve', sstg[0:24, 32:160], PS[b_][0:24, 0:128], ['ps%d' % b_], ['sstg_out'])
                            dma('act', o_sc[l].rearrange("s j (c p) -> (s j c) p", p=128), sstg[0:24, 32:160], ['sstg_out'], [okey()])
                    else:
                        if t == SEQ // 256 - 1:
                            cp('pool', sstg[:, 0:6].rearrange("p (j c) -> p j c", c=2)[:, :, c], xcw[c][:, 0, sl:sl + 3],
                               [('xcin', c)], ['sstg_in'])
                            if c == 1:
                                b_ = ps_get()
                                tr(PS[b_][0:6, 0:128], sstg[:, 0:6], ident[:], ['sstg_in', 'ident'], ['ps%d' % b_])
                                cp('dve', sstg[0:6, 32:160], PS[b_][0:6, 0:128], ['ps%d' % b_], ['sstg_out'])
                                dma('act', o_pc[l, sidx].rearrange("j (c p) -> (j c) p", p=128), sstg[0:6, 32:160], ['sstg_out'], [okey()])
                        else:
                            cp('pool', xcw[c][:, 0, 0:3], xcw[c][:, 0, sl:sl + 3], [('xcin', c)], [('xcin', c)])
                    b = ps_get()
                    mm(PS[b][:, 0:T], wrbd[:, c, :], xcb[:, c, 0:T], True, True, ['wrbd', 'xcb'], ['ps%d' % b])
                    b2 = ps_get()
                    mm(PS[b2][:, 0:T], wibd[:, c, :], xcb[:, c, 0:T], True, True, ['wibd', 'xcb'], ['ps%d' % b2])
                    act(la[:, 0:T], PS[b][:, 0:T], AF.Exp, ['ps%d' % b, 'PAR'], ['la'], bias=PAR[:, l, P_NBR + c:P_NBR + c + 1], scale=-1.0)
                    act(lig[:, 0:T], PS[b2][:, 0:T], AF.Exp, ['ps%d' % b2, 'PAR'], ['lig'], bias=PAR[:, l, P_NBI + c:P_NBI + c + 1], scale=-1.0)
                    act(la[:, 0:T], la[:, 0:T], AF.Ln, ['la'], ['la'], bias=1.0)
                    act(lig[:, 0:T], lig[:, 0:T], AF.Ln, ['lig'], ['lig'], bias=1.0)
                    act(la[:, 0:T], la[:, 0:T], AF.Exp, ['la'], ['la'], scale=-1.0)
                    act(lig[:, 0:T], lig[:, 0:T], AF.Exp, ['lig'], ['lig'], scale=-1.0)
                    act(ls[:, 0:T], la[:, 0:T], AF.Exp, ['la', 'C1'], ['ls'], scale=C1[:, l, 2 + c:3 + c])
                    act(la[:, 0:T], la[:, 0:T], AF.Exp, ['la', 'C1'], ['la'], scale=C1[:, l, c:c + 1])
                    act(ls[:, 0:T], ls[:, 0:T], AF.Ln, ['ls'], ['ls'], bias=1.0, scale=-1.0)
                    act(ls[:, 0:T], ls[:, 0:T], AF.Exp, ['ls'], ['ls'], scale=0.5)
                    tt('dve', lig[:, 0:T], lig[:, 0:T], xcv[:, 0:T], ALU.mult, ['lig', 'xcv'], ['lig'])
                    tt('dve', lig[:, 0:T], lig[:, 0:T], ls[:, 0:T], ALU.mult, ['lig', 'ls'], ['lig'])
                    for s in range(nseg):
                        S.op('dve', lambda e, s=s, c=c: e.tensor_tensor_scan(
                            out=lh[:, c, s * sl:(s + 1) * sl], data0=la[:, s * sl:(s + 1) * sl],
                            data1=lig[:, s * sl:(s + 1) * sl], initial=hcar[:, c, s:s + 1], op0=ALU.mult, op1=ALU.add),
                            reads=['la', 'lig', 'hcar'], writes=['lh'])
                        cp('pool', hcar[:, c, s:s + 1], lh[:, c, (s + 1) * sl - 1:(s + 1) * sl], ['lh'], ['hcar'])
                    tt('pool', mC[:, c, 0:T], lh[:, c, 0:T], gg[:, c, 0:T], ALU.mult, ['lh', 'gg'], ['mC'])
                if sample:
                    cp('pool', sstg[:, 24:32].rearrange("p (s c) -> p c s", c=2), hcar[:, :, 0:NS], ['hcar'], ['sstg_in'])
                    b_ = ps_get()
                    tr(PS[b_][0:8, 0:128], sstg[:, 24:32], ident[:], ['sstg_in', 'ident'], ['ps%d' % b_])
                    cp('dve', sstg[0:8, 32:160], PS[b_][0:8, 0:128], ['ps%d' % b_], ['sstg_out'])
                    dma('act', o_sh[l].rearrange("s (c p) -> (s c) p", p=128), sstg[0:8, 32:160], ['sstg_out'], [okey()])
                elif t == SEQ // 256 - 1:
                    cp('pool', sstg[:, 24:26], hcar[:, :, 0], ['hcar'], ['sstg_in'])
                    b_ = ps_get()
                    tr(PS[b_][0:2, 0:128], sstg[:, 24:26], ident[:], ['sstg_in', 'ident'], ['ps%d' % b_])
                    cp('dve', sstg[0:2, 32:160], PS[b_][0:2, 0:128], ['ps%d' % b_], ['sstg_out'])
                    dma('act', o_ph[l, sidx, :].rearrange("(c p) -> c p", p=128), sstg[0:2, 32:160], ['sstg_out'], [okey()])
                if sample:
                    sample_attention(l)
                else:
                    prompt_attention(l, t, c0)
                wo = wb['w_o'][l]
                for oc in range(8):
                    rt, rk = ring_get()
                    cs = slice(oc * 128, (oc + 1) * 128)
                    wA = rt[0:64, 0:512].rearrange("p (h n) -> p h n", n=128)
                    wB = rt[0:64, 512:1536].rearrange("p (h n) -> p h n", n=128)
                    wC = rt[:, 1536:1792].rearrange("p (h n) -> p h n", n=128)
                    dma('sp', wA, wo[0:256, cs].rearrange("(h p) n -> p h n", p=64), [('w_o_b', l)], [rk])
                    dma('sp', wB, wo[256:768, cs].rearrange("(h p) n -> p h n", p=64), [('w_o_b', l)], [rk])
                    dma('sp', wC, wo[768:1024, cs].rearrange("(h p) n -> p h n", p=128), [('w_o_b', l)], [rk])
                    b = ps_get()
                    n = 0
                    for h in range(4):
                        mm(PS[b][:, 0:T], rt[:, 0:512].rearrange("p (h n) -> p h n", n=128)[:, h, :], mA[:, h, 0:T], n == 0, False, [rk, 'mA'], ['ps%d' % b]); n += 1
                    for h in range(8):
                        mm(PS[b][:, 0:T], rt[:, 512:1536].rearrange("p (h n) -> p h n", n=128)[:, h, :], mB[:, h, 0:T], False, False, [rk, ('mB', h)], ['ps%d' % b]); n += 1
                    for c in range(2):
                        mm(PS[b][:, 0:T], wC[:, c, :], mC[:, c, 0:T], False, c == 1, [rk, 'mC'], ['ps%d' % b]); n += 1
                    stt(X[:, oc, c0:c0 + T], X[:, oc, c0:c0 + T], ALPHA, PS[b][:, 0:T], ALU.mult, ALU.add,
                        [('X', oc), 'ps%d' % b], [('X', oc)])
                layer_norm(l, c0, T, P_LN1G, P_LN1B, True)
                wu = wb['ffn_w_up'][l].rearrange("(kc p) n -> p kc n", p=128)
                seg2 = sl + 2
                pend = None
                for c in range(22):
                    if c % 2 == 0:
                        rtg, rkg = ring_get()
                        rtv, rkv = ring_get()
                        wg = rtg[:, 0:2048].rearrange("p (k n) -> p k n", n=256)
                        wv = rtv[:, 0:2048].rearrange("p (k n) -> p k n", n=256)
                        dma('sp', wg, wu[:, :, c * 128:c * 128 + 256], [('ffn_w_up_b', l)], [rkg])
                        dma('sp', wv, wu[:, :, DFF + c * 128:DFF + c * 128 + 256], [('ffn_w_up_b', l)], [rkv])
                    o = (c % 2) * 128
                    b = ps_get()
                    for kc in range(8):
                        mm(PS[b][:, 0:T], wg[:, kc, o:o + 128], xb[:, kc, 0:T], kc == 0, kc == 7, [rkg] + xbk, ['ps%d' % b])
                    for kc in range(8):
                        mm(PS[b][:, 256:256 + T], wv[:, kc, o:o + 128], xb[:, kc, 0:T], kc == 0, kc == 7, [rkv] + xbk, ['ps%d' % b])
                    ub = upb[c % 2]
                    ubk = 'upb%d' % (c % 2)
                    cg = cgb2[c % 2]
                    cgk = 'cgb%d' % (c % 2)
                    u4 = ub[:, :, 0:nseg * seg2].rearrange("p g (s n) -> p g s n", n=seg2)
                    pv4 = PS[b][:, :].rearrange("p (g c) -> p g c", g=2)[:, :, 0:T].rearrange("p g (s n) -> p g s n", n=sl)
                    if sample:
                        for gv in range(2):
                            cp('pool', u4[:, gv, :, 0:2], fcar[:, c + 22 * gv, :, :], ['fcar'], [ubk])
                    elif t == 0:
                        memset('pool', u4[:, :, 0, 0:2], 0.0, [ubk])
                    else:
                        cp('pool', u4[:, :, 0, 0:2], fcar[:, c:c + 23:22, 0, :], ['fcar'], [ubk])
                    cp('act', u4[:, :, :, 2:2 + sl], pv4, ['ps%d' % b], [ubk])
                    for gv in range(2):
                        cp('pool', fcar[:, c + 22 * gv, 0:nseg, :], u4[:, gv, :, sl:sl + 2], [ubk], ['fcar'])
                    for gv in range(2):
                        ch = c + 22 * gv
                        cw = lambda j: PAR[:, l, P_FCW + 44 * j + ch:P_FCW + 44 * j + ch + 1]
                        o3 = cg[:, gv, 0:T].rearrange("p (s n) -> p s n", n=sl)
                        act(o3, pv4[:, gv, :, :], AF.Identity, ['ps%d' % b, 'PAR'], [(cgk, gv)], bias=PAR[:, l, P_FCB + ch:P_FCB + ch + 1],
                            scale=cw(2))
                        for j in range(2):
                            stt(o3, u4[:, gv, :, j:j + sl], cw(j), o3, ALU.mult, ALU.add, [ubk, 'PAR', (cgk, gv)], [(cgk, gv)])
                    if pend is not None:
                        pend()

                    def _tail(cg=cg, cgk=cgk, c=c):
                        act(cg[:, 0, 0:T], cg[:, 0, 0:T], AF.Gelu, [(cgk, 0)], [(cgk, 0)])
                        tt('pool', hT[:, c, 0:T], cg[:, 0, 0:T], cg[:, 1, 0:T], ALU.mult, [(cgk, 0), (cgk, 1)], [('hT', c)])
                    pend = _tail
                pend()
                if sample:
                    stg_ = upb[0][:].rearrange("p g n -> p (g n)")[:, 0:352]
                    ostg_ = upb[1][:].rearrange("p g n -> p (g n)")[:, 0:384]
                    cp('pool', stg_.rearrange("p (s j c) -> p s j c", s=4, j=2), fcar[:].rearrange("p c s j -> p s j c"),
                       ['fcar'], ['upb0'])
                    b_ = ps_get()
                    tr(PS[b_][:, 0:128], stg_[:, 0:128], ident[:], ['upb0', 'ident'], ['ps%d' % b_])
                    tr(PS[b_][:, 128:256], stg_[:, 128:256], ident[:], ['upb0', 'ident'], ['ps%d' % b_])
                    tr(PS[b_][0:96, 256:384], stg_[:, 256:352], ident[:], ['upb0', 'ident'], ['ps%d' % b_])
                    cp('dve', ostg_[:, 0:256], PS[b_][:, 0:256], ['ps%d' % b_], ['upb1'])
                    cp('dve', ostg_[0:96, 256:384], PS[b_][0:96, 256:384], ['ps%d' % b_], ['upb1'])
                    fdst = o_sf[l].rearrange("s j (c p) -> (s j c) p", p=128)
                    dma('act', fdst[0:128, :], ostg_[:, 0:128], ['upb1'], [okey()])
                    dma('act', fdst[128:256, :], ostg_[:, 128:256], ['upb1'], [okey()])
                    dma('act', fdst[256:352, :], ostg_[0:96, 256:384], ['upb1'], [okey()])
                elif t == SEQ // 256 - 1:
                    stg_ = upb[0][:].rearrange("p g n -> p (g n)")[:, 0:88]
                    ostg_ = upb[1][:].rearrange("p g n -> p (g n)")[:, 0:128]
                    cp('pool', stg_.rearrange("p (j c) -> p j c", j=2), fcar[:, :, 0, :].rearrange("p c j -> p j c"), ['fcar'], ['upb0'])
                    b_ = ps_get()
                    tr(PS[b_][0:88, 0:128], stg_, ident[:], ['upb0', 'ident'], ['ps%d' % b_])
                    cp('dve', ostg_[0:88, :], PS[b_][0:88, 0:128], ['ps%d' % b_], ['upb1'])
                    dma('act', o_pf[l, sidx].rearrange("j (c p) -> (j c) p", p=128), ostg_[0:88, :], ['upb1'], [okey()])
                wdn = wb['ffn_w_down'][l].rearrange("(kc p) n -> p kc n", p=128)
                for op_ in range(4):
                    bb = (ps_get(hold=True), ps_get(hold=True))
                    for (k0, k1) in ((0, 8), (8, 16), (16, 22)):
                        rt, rk = ring_get()
                        w2 = rt[:, 0:(k1 - k0) * 256].rearrange("p (k n) -> p k n", n=256)
                        dma('sp', w2, wdn[:, k0:k1, op_ * 256:(op_ + 1) * 256], [('ffn_w_down_b', l)], [rk])
                        for j_ in range(2):
                            b = bb[j_]
                            for kc in range(k0, k1):
                                mm(PS[b][:, 0:T], w2[:, kc - k0, j_ * 128:(j_ + 1) * 128], hT[:, kc, 0:T], kc == 0, kc == 21,
                                   [rk, ('hT', kc)], ['ps%d' % b], skip=True)
                    for j_ in range(2):
                        b = bb[j_]
                        oc = op_ * 2 + j_
                        stt(X[:, oc, c0:c0 + T], X[:, oc, c0:c0 + T], ALPHA, PS[b][:, 0:T], ALU.mult, ALU.add,
                            [('X', oc), 'ps%d' % b], [('X', oc)])
                        ps_rel(b)
            if part == 'ln2':
                layer_norm(l, c0, T, P_LN2G, P_LN2B, False)
                if last_layer:
                    for blk in range(nblk):
                        for half in range(2):
                            b = ps_get()
                            for q4 in range(4):
                                oc = half * 4 + q4
                                tr(PS[b][:, q4 * 128:(q4 + 1) * 128], X[:, oc, c0 + blk * 128:c0 + (blk + 1) * 128], ident[:],
                                   [('X', oc), 'ident'], ['ps%d' % b])
                            cp('act' if half else 'dve', otok[:, half * 512:(half + 1) * 512], PS[b][:, :], ['ps%d' % b], [('otok', half)])
                        if sample:
                            dma('act', ys[:, :], otok[:], [('otok', 0), ('otok', 1)], [okey()])
                        else:
                            dma('act', yp[sidx, c0 + blk * 128:c0 + (blk + 1) * 128, :], otok[:], [('otok', 0), ('otok', 1)], [okey()])

        def prompt_attention(l, t, c0):
            T = 256
            for hp in range(4):
                b = ps_get()
                for hh in range(2):
                    h = hp * 2 + hh
                    for kc in range(2):
                        mm(PS[b][0:96, hh * 256:hh * 256 + T], wuk[:, kc, h, :], ckvT[:, kc, 0:T], kc == 0, kc == 1,
                           ['wuk', 'ckvT'], ['ps%d' % b])
                cp('act', KT[0:96, hp * 2:hp * 2 + 2, c0:c0 + T], PS[b][0:96, :].rearrange("p (h n) -> p h n", h=2),
                   ['ps%d' % b], [('KT', hp * 2), ('KT', hp * 2 + 1)])
            cp('dve', KT[0:32, :, c0:c0 + T], krot[:, None, 0:T].to_broadcast([32, 8, T]), ['krot'] + [('KT', h) for h in range(8)],
               [('KT', h) for h in range(8)])
            for blk in range(2):
                kb = t * 2 + blk
                b = ps_get()
                for kc in range(2):
                    mm(PS[b][:, :], ckvT[:, kc, blk * 128:(blk + 1) * 128], wuv[:, kc, :], kc == 0, kc == 1,
                       ['ckvT', 'wuv'], ['ps%d' % b])
                cp('act', VA[:, kb, 0:512], PS[b][:, :], ['ps%d' % b], ['VA'])
            nkb = 2 * t + 2
            blocks = [(h, kb) for h in range(8) for kb in range(nkb)]
            LOOK = 2
            info = {}
            accs = {}

            def emit_qk(i):
                h, kb = blocks[i]
                q0 = 128 if kb == nkb - 1 else 0
                b = ps_get()
                mm(PS[b][:, q0:T], KT[:, h, kb * 128:(kb + 1) * 128], qT[:, h, q0:T], True, True,
                   [('KT', h), ('qT', h)], ['ps%d' % b])
                pt = PT[i % 4]
                pk = 'pT%d' % (i % 4)
                act(pt[:, q0:T], PS[b][:, q0:T], AF.Exp, ['ps%d' % b], [pk], scale=ATTN_SCALE)
                if kb >= nkb - 2:
                    m0 = 0 if kb == nkb - 2 else 128
                    memset('pool', pt[64:128, m0:m0 + 64], 0.0, [pk])
                info[i] = (pt, pk, q0)

            def emit_pv(i):
                h, kb = blocks[i]
                pt, pk, q0 = info.pop(i)
                if kb == 0:
                    accs[h] = (ps_get(hold=True), ps_get(hold=True))
                ob, db = accs[h]
                mm(PS[ob][:, q0:T], VA[:, kb, h * 64:h * 64 + 128], pt[:, q0:T], kb == 0, kb == nkb - 1,
                   ['VA', pk], ['ps%d' % ob], skip=True)
                mm(PS[db][:, q0:T], ones1[:, :], pt[:, q0:T], kb == 0, kb == nkb - 1,
                   ['ones1', pk], ['ps%d' % db], skip=True)
                if kb == nkb - 1:
                    act(bcs[:, 0:T], PS[db][0:64, 0:T], AF.Ln, ['ps%d' % db], ['bcs'])
                    act(bcs[:, 0:T], bcs[:, 0:T], AF.Exp, ['bcs'], ['bcs'], scale=-1.0)
                    tt('dve', mB[0:64, h, 0:T], PS[ob][0:64, 0:T], bcs[:, 0:T], ALU.mult, ['ps%d' % ob, 'bcs'], [('mB', h)])
                    ps_rel(ob)
                    ps_rel(db)
                    del accs[h]

            for i in range(len(blocks) + LOOK):
                if i < len(blocks):
                    emit_qk(i)
                if i - LOOK >= 0:
                    emit_pv(i - LOOK)

        KTf = KT[:].rearrange("p a b -> p (a b)")

        def sample_attention(l):
            T = NS * TS
            wukT = KTf[0:96, 0:2048].rearrange("p (h c) -> p h c", h=8)
            qlat = KTf[:, 2048:4096].rearrange("p (c h n) -> p c h n", c=2, h=8)
            olat = KTf[:, 4096:6144].rearrange("p (c h n) -> p c h n", c=2, h=8)
            cst = [KTf[:, 6144 + i * 2304:6144 + (i + 1) * 2304].bitcast(F32).rearrange("p (k n) -> p k n", n=288)
                   for i in range(2)]
            cbf = [KTf[:, 10752 + i * 1152:10752 + (i + 1) * 1152].rearrange("p (k n) -> p k n", n=288) for i in range(2)]
            cT = [KTf[:, 13056 + i * 1024:13056 + (i + 1) * 1024].rearrange("p (c n) -> p c n", c=2) for i in range(2)]
            krT = [KTf[0:32, 15104 + i * 512:15104 + (i + 1) * 512] for i in range(2)]
            cnew = KTf[:, 16128:16384]
            for h in range(8):
                b = ps_get()
                for cc in range(2):
                    tr(psb(b)[0:96, cc * 128:(cc + 1) * 128], wuk[:, cc, h, :], identb[:], ['wuk', 'identb'], ['ps%d' % b])
                cp('act', wukT[:, h, :], psb(b)[0:96, 0:256], ['ps%d' % b], ['wukT'])
            for h in range(8):
                b = ps_get()
                for cc in range(2):
                    mm(PS[b][:, cc * 128:cc * 128 + T], wukT[:, h, cc * 128:(cc + 1) * 128], qT[0:96, h, 0:T], True, True,
                       ['wukT', ('qT', h)], ['ps%d' % b])
                cp('act', qlat[:, :, h, :], PS[b][:, 0:256].rearrange("p (c n) -> p c n", c=2), ['ps%d' % b], ['qlat'])
            nfull = PAST // 128
            cT4 = [KTf[:, 13056 + i * 256:13056 + (i + 1) * 256].rearrange("p (c n) -> p c n", c=2) for i in range(4)]
            krT4 = [KTf[0:32, 15104 + i * 128:15104 + (i + 1) * 128] for i in range(4)]
            for s in range(NS):
                accs = [ps_get(hold=True) for _ in range(3)]
                groups = [(g0, min(4, nfull - g0)) for g0 in range(0, nfull, 4)] + [(-1, 1)]
                blks = [(gi, g0, k) for gi, (g0, ng) in enumerate(groups) for k in range(ng)]
                nb_ = len(blks)
                st = {}

                def stA(i, s=s, blks=blks, groups=groups, st=st):
                    gi, g0, k = blks[i]
                    ng = groups[gi][1]
                    i2 = gi % 2
                    sk, bk = 'cst%d' % i2, 'cbf%d' % i2
                    i4 = i % 4
                    tk = 'cT4_%d' % i4
                    if g0 >= 0:
                        if k == 0:
                            dma('sp', cst[i2][:, 0:ng, 0:256],
                                clat[l, s, g0 * 128:(g0 + ng) * 128, :].rearrange("(k p) n -> p k n", p=128), [], [sk])
                            dma('sp', cst[i2][:, 0:ng, 256:288],
                                ckr[l, s, g0 * 128:(g0 + ng) * 128, :].rearrange("(k p) n -> p k n", p=128), [], [sk])
                            cp('pool', cbf[i2][:, 0:ng, :], cst[i2][:, 0:ng, :], [sk], [bk])
                        b = ps_get()
                        for cc in range(2):
                            tr(psb(b)[:, cc * 128:(cc + 1) * 128], cbf[i2][:, k, cc * 128:(cc + 1) * 128], identb[:],
                               [bk, 'identb'], ['ps%d' % b])
                        tr(psb(b)[0:32, 256:384], cbf[i2][:, k, 256:288], identb[:], [bk, 'identb'], ['ps%d' % b])
                        cp('act', cT4[i4][:, :, 0:128], psb(b)[:, 0:256].rearrange("p (c n) -> p c n", c=2), ['ps%d' % b], [tk])
                        cp('dve', krT4[i4][:, 0:128], psb(b)[0:32, 256:384], ['ps%d' % b], [tk])
                        st[i] = dict(nk=128, lat=[cT4[i4][:, cc, 0:128] for cc in range(2)], kr=krT4[i4][:, 0:128],
                                     cv=cbf[i2][:, k, 0:256], on=ones1[:, :], tk=tk, bk=bk)
                    else:
                        if s == 0:
                            cp('pool', cnew[:], ckvn[:, 0, :], ['ckvn'], ['cnew'])
                        dma('act', cbf[i2][0:32, 0, 0:256], cnew[s * 32:(s + 1) * 32, :], ['cnew'], [bk])
                        cp('pool', krT4[i4][:, 0:32], krot[:, s * 32:(s + 1) * 32], ['krot'], [tk])
                        st[i] = dict(nk=32, lat=[ckvT[:, cc, s * 32:(s + 1) * 32] for cc in range(2)], kr=krT4[i4][:, 0:32],
                                     cv=cbf[i2][0:32, 0, 0:256], on=ones1[0:32, :], tk=tk, bk=bk)

                def stB(i, s=s, st=st):
                    d = st[i]
                    nk = d['nk']
                    b = ps_get()
                    sv = PS[b][0:nk, 0:256].rearrange("p (h n) -> p h n", h=8)
                    for cc in range(2):
                        mm(sv, d['lat'][cc], qlat[:, cc, :, s * 32:(s + 1) * 32], cc == 0, False,
                           [d['tk'], 'ckvT', 'qlat'], ['ps%d' % b])
                    mm(sv, d['kr'], qT[0:32, :, s * 32:(s + 1) * 32], False, True,
                       [d['tk']] + [('qT', h) for h in range(8)], ['ps%d' % b])
                    pt = PT[i % 4]
                    pk = 'pT%d' % (i % 4)
                    act(pt[0:nk, 0:256], PS[b][0:nk, 0:256], AF.Exp, ['ps%d' % b], [pk], scale=ATTN_SCALE)
                    d['ptv'] = pt[0:nk, 0:256]
                    d['pk'] = pk

                def stC(i, st=st, accs=accs, nb_=nb_):
                    d = st.pop(i)
                    first = i == 0
                    last = i == nb_ - 1
                    for cc in range(2):
                        mm(PS[accs[cc]][:, 0:256], d['cv'][:, cc * 128:(cc + 1) * 128], d['ptv'], first, last,
                           [d['bk'], d['pk']], ['ps%d' % accs[cc]], skip=True)
                    mm(PS[accs[2]][:, 0:256], d['on'], d['ptv'], first, last, ['ones1', d['pk']], ['ps%d' % accs[2]], skip=True)

                for i in range(nb_ + 2):
                    if i < nb_:
                        stA(i)
                    if 0 <= i - 1 < nb_:
                        stB(i - 1)
                    if 0 <= i - 2 < nb_:
                        stC(i - 2)
                S.op('dve', lambda e, a=accs[2]: e.reciprocal(out=rden[:, 0:256], in_=PS[a][:, 0:256]),
                     reads=['ps%d' % accs[2]], writes=['rden'])
                for cc in range(2):
                    tt('dve', olat[:, cc, :, s * 32:(s + 1) * 32], PS[accs[cc]][:, 0:256].rearrange("p (h n) -> p h n", h=8),
                       rden[:, 0:256].rearrange("p (h n) -> p h n", h=8), ALU.mult, ['ps%d' % accs[cc], 'rden'], ['olat'])
                for a in accs:
                    ps_rel(a)
            for h in range(8):
                b = ps_get()
                for cc in range(2):
                    mm(PS[b][0:64, 0:T], wuv[:, cc, h * 64:(h + 1) * 64], olat[:, cc, h, :], cc == 0, cc == 1,
                       ['wuv', 'olat'], ['ps%d' % b])
                cp('act', mB[0:64, h, 0:T], PS[b][0:64, 0:T], ['ps%d' % b], [('mB', h)])

        groups = []
        if cfg.do_prompt:
            groups += [('p', s) for s in range(NP)]
        if cfg.do_sample:
            groups += [('s',)]
        for gi_, G in enumerate(groups):
            S.barrier(skip_rings=('pool',) if gi_ == 0 else ())
            if G[0] == 'p':
                for blk in range(SEQ // 128):
                    dma('sp', stage[:], xp[G[1], blk * 128:(blk + 1) * 128, :], [], ['stage'])
                    for half in range(2):
                        b = ps_get()
                        for q4 in range(4):
                            kc = half * 4 + q4
                            tr(PS[b][:, q4 * 128:(q4 + 1) * 128], stage[:, kc * 128:(kc + 1) * 128], ident[:],
                               ['stage', 'ident'], ['ps%d' % b])
                        cp('act' if half else 'dve', X[:, half * 4:half * 4 + 4, blk * 128:(blk + 1) * 128],
                           PS[b][:, :].rearrange("p (k n) -> p k n", k=4), ['ps%d' % b], [('X', half * 4 + q) for q in range(4)])
                ntile = SEQ // 256
            else:
                dma('sp', stage[:], xs[:, :], [], ['stage'])
                for half in range(2):
                    b = ps_get()
                    for q4 in range(4):
                        kc = half * 4 + q4
                        tr(PS[b][:, q4 * 128:(q4 + 1) * 128], stage[:, kc * 128:(kc + 1) * 128], ident[:],
                           ['stage', 'ident'], ['ps%d' % b])
                    cp('act' if half else 'dve', X[:, half * 4:half * 4 + 4, 0:128],
                       PS[b][:, :].rearrange("p (k n) -> p k n", k=4), ['ps%d' % b], [('X', half * 4 + q) for q in range(4)])
                ntile = 1
            for l in range(L):
                load_layer(l, G[0] == 's')
                tile(l, G, 0, l == L - 1, 'front')
                for t in range(ntile):
                    tile(l, G, t, l == L - 1, 'rest')
                    if t + 1 < ntile:
                        tile(l, G, t + 1, l == L - 1, 'front')
                    tile(l, G, t, l == L - 1, 'ln2')
        S.barrier()
        S.emit(block)
    return nc


def _consts(cfg):
    TS = 32
    inv = 1.0 / (10000.0 ** (np.arange(0, 32, 2, dtype=np.float32) / 32.0))

    def tab(pos):
        ang = pos.astype(np.float32)[None, :] * inv[:, None].astype(np.float32)
        c = np.cos(ang).astype(np.float32)
        s = np.sin(ang).astype(np.float32)
        return np.concatenate([c, c], 0), np.concatenate([-s, s], 0)
    cp_, sp_ = tab(np.arange(cfg.SEQ))
    c1, s1 = tab(cfg.PAST + np.arange(TS))
    return {
        'c_ident': np.eye(128, dtype=np.float32),
        'c_tril': np.tril(np.ones((128, 128), np.float32)),
        'c_cosp': np.ascontiguousarray(cp_), 'c_sinp': np.ascontiguousarray(sp_),
        'c_coss': np.ascontiguousarray(np.tile(c1, (1, cfg.NS))), 'c_sins': np.ascontiguousarray(np.tile(s1, (1, cfg.NS))),
    }


def make_in_maps(cfg, inputs, ncores):
    L, NP, NS = cfg.L, cfg.NP, cfg.NS
    f = lambda a: np.ascontiguousarray(np.asarray(a, dtype=np.float32))
    consts = _consts(cfg)
    maps = []
    for c in range(ncores):
        m = dict(consts)
        m['xp'] = f(inputs['x_prompt'][c * NP:(c + 1) * NP])
        m['xs'] = f(inputs['x_sample'][c * NS:(c + 1) * NS]).reshape(NS * 32, D)
        m['clat'] = f(inputs['cache_kv_latent'][:, c * NS:(c + 1) * NS])
        m['ckr'] = f(inputs['cache_k_rope'][:, c * NS:(c + 1) * NS])
        m['lruh'] = f(inputs['state_lru_h'][:, c * NS:(c + 1) * NS])
        m['lruc'] = f(inputs['state_lru_conv'][:, c * NS:(c + 1) * NS])
        m['ffnc'] = f(inputs['state_ffn_conv'][:, c * NS:(c + 1) * NS])
        for n, s in WEIGHTS:
            m[n] = f(inputs[n]).reshape([L] + s)
        for n, s in VECS:
            m[n] = f(inputs[n])
        m['gmlp_w_s'] = f(inputs['gmlp_w_s'])
        m['gmlp_b_s'] = f(inputs['gmlp_b_s'])
        m['mla_kv_norm_g'] = f(inputs['mla_kv_norm_g'])
        m['lru_conv_w'] = f(inputs['lru_conv_w'])
        m['ffn_conv_w'] = f(inputs['ffn_conv_w'])
        maps.append(m)
    return maps


def gather(cfg, results):
    L, NP, NS = cfg.L, cfg.NP, cfg.NS
    cat = lambda k, ax: np.concatenate([r[k] for r in results], axis=ax)
    y_p = cat('yp', 0)
    y_s = cat('ys', 0).reshape(-1, 32, D)
    p_lat = cat('o_plat', 1)
    p_kr = cat('o_pkr', 1)
    p_h = cat('o_ph', 1)
    p_c = cat('o_pc', 1)
    p_f = cat('o_pf', 1)
    s_lat = np.concatenate([r['o_slat'].reshape(L, NS, 32, 256) for r in results], axis=1)
    s_kr = np.concatenate([r['o_skr'].reshape(L, NS, 32, 32) for r in results], axis=1)
    s_v = np.concatenate([r['o_sv'].reshape(L, NS, 32, 256) for r in results], axis=1)
    s_h = cat('o_sh', 1)
    s_c = cat('o_sc', 1)
    s_f = cat('o_sf', 1)
    return tuple(np.ascontiguousarray(a, dtype=np.float32) for a in
                 (y_p, y_s, p_lat, p_kr, p_h, p_c, p_f, s_lat, s_kr, s_v, s_h, s_c, s_f))


def kernel(**inputs):
    cfg = Cfg()
    nc = build_nc(cfg)
    in_maps = make_in_maps(cfg, inputs, NCORES)
    res = run_bass_kernel_spmd(nc, in_maps, core_ids=list(range(NCORES)))
    return gather(cfg, res.results)
```

```python
import contextlib
import numpy as np
import concourse.bass as bass
import concourse.mybir as mybir
from concourse.bass_utils import run_bass_kernel_spmd

F32 = mybir.dt.float32
BF16 = mybir.dt.bfloat16
AF = mybir.ActivationFunctionType
ALU = mybir.AluOpType

D = 1024
DIN = 1696
DFF = 2816
NCORES = 8
ALPHA = (2.0 * 4) ** 0.25
LN_EPS = 1e-5
RMS_EPS = 1e-6
ATTN_SCALE = 96 ** -0.5
CU, CV, CQ, CKV, CKR, CXC, CG = 0, 256, 512, 896, 1152, 1184, 1440


class Sched:
    ENGS = ('pe', 'act', 'dve', 'pool', 'sp')

    def __init__(self, nc, stack, rings):
        self.nc = nc
        self.ops = {e: [] for e in self.ENGS}
        self.cnt = {e: 0 for e in self.ENGS}
        self.sems = {}
        for e in self.ENGS:
            self.sems[('e', e)] = stack.enter_context(nc.semaphore('s_' + e))
        self.rings = {}
        for q, n in rings.items():
            self.rings[q] = n
            for i in range(n):
                self.sems[('r', q, i)] = stack.enter_context(nc.semaphore('r_%s%d' % (q, i)))
        self.ring_n = {q: 0 for q in rings}
        self.waited = {e: {} for e in self.ENGS}
        self.last_w = {}
        self.readers = {}
        self.nops = 0
        self.alias = {}

    def _exp(self, keys):
        out = []
        for k in keys:
            out.extend(self.alias.get(k, (k,)))
        return out

    def _need(self, eng, semid, val, src, waits):
        if src == eng and eng == 'pe':
            return
        if self.waited[eng].get(semid, 0) >= val:
            return
        if waits.get(semid, 0) < val:
            waits[semid] = val

    def op(self, eng, fn, reads=(), writes=(), dma=False):
        reads = self._exp(reads)
        writes = self._exp(writes)
        waits = {}
        for k in reads:
            lw = self.last_w.get(k)
            if lw is not None:
                self._need(eng, lw[0], lw[1], lw[2], waits)
        for k in writes:
            lw = self.last_w.get(k)
            if lw is not None:
                self._need(eng, lw[0], lw[1], lw[2], waits)
            rd = self.readers.get(k)
            if rd:
                for semid, (val, se) in rd.items():
                    self._need(eng, semid, val, se, waits)
        if dma:
            q = eng
            i = self.ring_n[q]
            self.ring_n[q] += 1
            R = self.rings[q]
            semid = ('r', q, i % R)
            prev = 16 * (i // R)
            if prev > 0 and self.waited[eng].get(semid, 0) < prev and waits.get(semid, 0) < prev:
                waits[semid] = prev
            val = prev + 16
            inc = 16
            src = 'dma'
        else:
            self.cnt[eng] += 1
            semid = ('e', eng)
            val = self.cnt[eng]
            inc = 1
            src = eng
        for sid, v in waits.items():
            if self.waited[eng].get(sid, 0) < v:
                self.waited[eng][sid] = v
        self.ops[eng].append((list(waits.items()), fn, semid, inc))
        for k in reads:
            self.readers.setdefault(k, {})[semid] = (val, src)
        for k in writes:
            self.last_w[k] = (semid, val, src)
            self.readers[k] = {}
        self.nops += 1

    def barrier(self, skip_rings=()):
        cur = {}
        for e in self.ENGS:
            if self.cnt[e] > 0:
                cur[('e', e)] = self.cnt[e]
        for q, R in self.rings.items():
            if q in skip_rings:
                continue
            n = self.ring_n[q]
            for s in range(R):
                k = (n - 1 - s) // R + 1 if n - 1 - s >= 0 else 0
                if k > 0:
                    cur[('r', q, s)] = 16 * k
        for e in self.ENGS:
            waits = []
            for sid, v in cur.items():
                if sid == ('e', e):
                    continue
                if self.waited[e].get(sid, 0) < v:
                    waits.append((sid, v))
                    self.waited[e][sid] = v
            self.ops[e].append((waits, None, None, 0))

    def emit(self, block):
        sems = self.sems

        def run(name):
            def f(eng):
                for waits, fn, semid, inc in self.ops[name]:
                    for sid, v in waits:
                        eng.wait_ge(sems[sid], v)
                    if fn is None:
                        continue
                    fn(eng).then_inc(sems[semid], inc)
            return f
        block.tensor(run('pe'))
        block.scalar(run('act'))
        block.vector(run('dve'))
        block.gpsimd(run('pool'))
        block.sync(run('sp'))


class Cfg:
    def __init__(self, L=4, SEQ=2048, NP=2, NS=4, PAST=4096, do_prompt=True, do_sample=True):
        self.L, self.SEQ, self.NP, self.NS, self.PAST = L, SEQ, NP, NS, PAST
        self.do_prompt, self.do_sample = do_prompt, do_sample


WEIGHTS = [
    ('w_in', [D, DIN]), ('w_o', [D, D]), ('mla_w_uq', [384, 768]), ('mla_w_uk', [256, 512]),
    ('mla_w_uv', [256, 512]), ('lru_w_r', [256, 64]), ('lru_w_i', [256, 64]),
    ('ffn_w_up', [D, 2 * DFF]), ('ffn_w_down', [DFF, D]),
]
VECS = [('ln1_g', D), ('ln1_b', D), ('ln2_g', D), ('ln2_b', D), ('mla_q_norm_g', 384),
        ('lru_conv_b', 256), ('lru_b_r', 256), ('lru_b_i', 256), ('lru_lam', 256),
        ('ffn_conv_b', 2 * DFF)]


def build_nc(cfg):
    L, SEQ, NP, NS, PAST = cfg.L, cfg.SEQ, cfg.NP, cfg.NS, cfg.PAST
    TS = 32
    nc = bass.Bass("TRN2", target_bir_lowering=False, dynamic_dma_scratch_size=2048)
    din = lambda n, s: nc.dram_tensor(n, s, F32, kind="ExternalInput").ap()
    dout = lambda n, s: nc.dram_tensor(n, s, F32, kind="ExternalOutput").ap()
    xp = din('xp', [NP, SEQ, D])
    xs = din('xs', [NS * TS, D])
    clat = din('clat', [L, NS, PAST, 256])
    ckr = din('ckr', [L, NS, PAST, 32])
    lruh = din('lruh', [L, NS, 256])
    lruc = din('lruc', [L, NS, 3, 256])
    ffnc = din('ffnc', [L, NS, 2, 2 * DFF])
    wd = {n: din(n, [L] + s) for n, s in WEIGHTS}
    vd = {n: din(n, [L, s]) for n, s in VECS}
    gmlp_w = din('gmlp_w_s', [L, 4, 128, 128])
    gmlp_b = din('gmlp_b_s', [L, 4, 128])
    kvg = din('mla_kv_norm_g', [L, 256])
    lcw = din('lru_conv_w', [L, 4, 256])
    fcw = din('ffn_conv_w', [L, 3, 2 * DFF])
    ident_d = din('c_ident', [128, 128])
    tril_d = din('c_tril', [128, 128])
    cosp_d = din('c_cosp', [32, SEQ])
    sinp_d = din('c_sinp', [32, SEQ])
    coss_d = din('c_coss', [32, NS * TS])
    sins_d = din('c_sins', [32, NS * TS])

    yp = dout('yp', [NP, SEQ, D])
    ys = dout('ys', [NS * TS, D])
    o_plat = dout('o_plat', [L, NP, SEQ, 256])
    o_pkr = dout('o_pkr', [L, NP, SEQ, 32])
    o_ph = dout('o_ph', [L, NP, 256])
    o_pc = dout('o_pc', [L, NP, 3, 256])
    o_pf = dout('o_pf', [L, NP, 2, 2 * DFF])
    o_slat = dout('o_slat', [L, NS * TS, 256])
    o_skr = dout('o_skr', [L, NS * TS, 32])
    o_sv = dout('o_sv', [L, NS * TS, 256])
    o_sh = dout('o_sh', [L, NS, 256])
    o_sc = dout('o_sc', [L, NS, 3, 256])
    o_sf = dout('o_sf', [L, NS, 2, 2 * DFF])

    wb = {n: nc.dram_tensor(n + '_b', [L] + s, BF16).ap() for n, s in WEIGHTS}
    wsT_b = nc.dram_tensor('wsT_b', [L, 4, 128, 128], BF16).ap()

    with contextlib.ExitStack() as st:
        S = Sched(nc, st, {'sp': 16, 'act': 16, 'pool': 8})
        sbt = lambda n, s, d: st.enter_context(nc.sbuf_tensor(n, s, d))
        XW = max(SEQ, NS * TS)
        X = sbt('X', [128, 8, XW], F32)
        KT = sbt('KT', [128, 8, max(SEQ, 2048)], BF16)
        NKB = SEQ // 128
        VA = sbt('VA', [128, NKB, 576], BF16)
        ident = sbt('ident', [128, 128], F32)
        identb = sbt('identb', [128, 128], BF16)
        tril = sbt('tril', [128, 128], F32)
        onesS = sbt('onesS', [128, 128], BF16)
        ones1 = sbt('ones1', [128, 128], BF16)
        onesq = sbt('onesq', [128, 128], BF16)
        NV = 32 + 3 + 8 + 2 + 2 + 2 + 2 + 132 + 44 + 4
        P_NBR, P_NBI = 227, 229
        PAR = sbt('PAR', [128, L, NV], F32)
        P_LN1G, P_LN1B, P_LN2G, P_LN2B, P_QG, P_LCW, P_LCB, P_BR, P_BI, P_LAM, P_FCW, P_FCB = \
            0, 8, 16, 24, 32, 35, 43, 45, 47, 49, 51, 183
        C1 = sbt('C1', [128, L, 4], F32)
        kvgb = sbt('kvgb', [128, 1, 256], F32)
        wq = sbt('wq', [128, 3, 8, 96], BF16)
        wqp = sbt('wqp', [128, 3, 8, 32], BF16)
        wkr = sbt('wkr', [128, 8, 32], BF16)
        wkrp = sbt('wkrp', [128, 8, 32], BF16)
        wuk = sbt('wuk', [128, 2, 8, 96], BF16)
        wuv = sbt('wuv', [128, 2, 512], BF16)
        wrbd = sbt('wrbd', [128, 2, 128], BF16)
        wibd = sbt('wibd', [128, 2, 128], BF16)
        wsT = sbt('wsT', [128, 4, 128], BF16)
        bsb = sbt('bsb', [64, 4, 128], F32)
        RING = [sbt('ring%d' % i, [128, 2048], BF16) for i in range(5)]
        TM = 256
        xb = sbt('xb', [128, 8, TM], BF16)
        uT = sbt('uT', [64, 4, TM], BF16)
        vtok = sbt('vtok', [128, 2, 256], BF16)
        cqT = sbt('cqT', [128, 3, TM], BF16)
        rqb = sbt('rqb', [128, TM], F32)
        cosrq = sbt('cosrq', [32, TM], F32)
        sinrq = sbt('sinrq', [32, TM], F32)
        cosb = sbt('cosb', [32, TM], F32)
        sinb = sbt('sinb', [32, TM], F32)
        ckvn = sbt('ckvn', [128, 2, 256], F32)
        ss = sbt('ss', [128, 8], F32)
        ckvT = sbt('ckvT', [128, 2, TM], BF16)
        krot = sbt('krot', [32, TM], F32)
        krtok = sbt('krtok', [128, 2, 32], F32)
        xcin = sbt('xcin', [128, 2, 4 * 35 + 224], F32)
        xcv = sbt('xcv', [128, TM], F32)
        xcb = sbt('xcb', [128, 2, TM], BF16)
        la = sbt('la', [128, TM], F32)
        lig = sbt('lig', [128, TM], F32)
        ls = sbt('ls', [128, TM], F32)
        hcar = sbt('hcar', [128, 2, 4], F32)
        gg = sbt('gg', [128, 2, TM], BF16)
        qT = sbt('qT', [128, 8, TM], BF16)
        qt1 = sbt('qt1', [32, TM], F32)
        qt2 = sbt('qt2', [32, TM], F32)
        PT = [sbt('pT%d' % i, [128, TM], BF16) for i in range(4)]
        mA = sbt('mA', [128, 4, TM], BF16)
        mB = sbt('mB', [128, 8, TM], BF16)
        mC = sbt('mC', [128, 2, TM], BF16)
        gtt = sbt('gtt', [128, 512], F32)
        gt = gtt[0:64, :]
        rden = gtt[:, 0:256]
        bcs = gtt[0:64, 256:512]
        meanb = la
        rstdb = lig
        hT = sbt('hT', [128, 22, TM], BF16)
        upb = [sbt('upb%d' % i, [128, 2, 4 * 34 + 124], F32) for i in range(2)]
        cgb = sbt('cgb', [128, 2, TM], F32)
        cgbB = sbt('cgbB', [128, 2, TM], F32)
        cgb2 = [cgb, cgbB]
        fcar = sbt('fcar', [128, 44, 4, 2], F32)
        stage = hT[:].rearrange("p a b -> p (a b)")[:, 0:2048].bitcast(F32)
        otok = stage
        S.alias.update({'sq': [('hT', c) for c in range(8)], 'lh': [('cgb0', 0), ('cgb0', 1)], 'krt1': ['qt1'],
                        'vf32': ['xcv'], 'meanb': ['la'], 'rstdb': ['lig'], 'rden': ['gtt'], 'bcs': ['gtt'],
                        'gt': ['gtt'], ('otok', 0): [('hT', c) for c in range(8)], ('otok', 1): [('hT', c) for c in range(8)],
                        'stage': [('hT', c) for c in range(8)], 'stage2': [('hT', c) for c in range(8)],
                        'cqsq': [('hT', 16), ('hT', 17), ('hT', 18)], 'lnscr': [('hT', c) for c in range(8, 16)]})
        cqsq = hT[:, 16:19, :]
        lnscr = hT[:, 8:16, :]
        sq = hT[:, 0:8, :]
        lh = cgb
        krt1 = qt1
        vf32 = xcv
        sstg = sbt('sstg', [128, 160], F32)
        PS = [st.enter_context(nc.psum_tensor('ps%d' % i, [128, 512], F32)) for i in range(8)]
        block = st.enter_context(nc.Block())

        def mm(out, lhsT, rhs, start, stop, rd, wr, skip=False):
            S.op('pe', lambda e: e.matmul(out, lhsT, rhs, start=start, stop=stop, skip_group_check=skip),
                 reads=rd, writes=wr)

        def tr(out, in_, idn, rd, wr):
            S.op('pe', lambda e: e.transpose(out, in_, idn), reads=rd, writes=wr)

        def act(out, in_, func, rd, wr, bias=None, scale=None, accum=None):
            kw = {}
            if bias is not None:
                kw['bias'] = bias
            if scale is not None:
                kw['scale'] = scale
            if accum is not None:
                kw['accum_out'] = accum
            S.op('act', lambda e: e.activation(out=out, in_=in_, func=func, **kw), reads=rd, writes=wr)

        def tt(eng, out, in0, in1, op, rd, wr):
            S.op(eng, lambda e: e.tensor_tensor(out=out, in0=in0, in1=in1, op=op), reads=rd, writes=wr)

        def ts(eng, out, in0, s1, s2, op0, op1, rd, wr):
            S.op(eng, lambda e: e.tensor_scalar(out=out, in0=in0, scalar1=s1, scalar2=s2, op0=op0, op1=op1),
                 reads=rd, writes=wr)

        def stt(out, in0, sc, in1, op0, op1, rd, wr):
            S.op('dve', lambda e: e.scalar_tensor_tensor(out=out, in0=in0, scalar=sc, in1=in1, op0=op0, op1=op1),
                 reads=rd, writes=wr)

        def cp(eng, out, in_, rd, wr):
            if eng == 'act':
                act(out, in_, AF.Copy, rd, wr)
            else:
                S.op(eng, lambda e: e.tensor_copy(out=out, in_=in_), reads=rd, writes=wr)

        def memset(eng, ap, v, wr):
            S.op(eng, lambda e: e.memset(ap, v), writes=wr)

        def dma(q, out, in_, rd, wr, slow=False):
            S.op(q, lambda e: e.dma_start(out=out, in_=in_, allow_slow_non_contiguous=slow),
                 reads=rd, writes=wr, dma=True)

        psn = [0]
        ps_open = set()

        def ps_get(hold=False):
            for _ in range(8):
                b = psn[0] % 8
                psn[0] += 1
                if b not in ps_open:
                    if hold:
                        ps_open.add(b)
                    return b
            raise RuntimeError("psum exhausted")

        def ps_rel(b):
            ps_open.discard(b)

        def psb(b):
            return PS[b][:].bitcast(BF16)

        ringn = [0]

        def ring_get():
            i = ringn[0] % 5
            ringn[0] += 1
            return RING[i], 'ring%d' % i

        outn = [0]

        def okey():
            outn[0] += 1
            return ('out', outn[0])

        dma('sp', ident[:], ident_d, [], ['ident'])
        dma('sp', tril[:], tril_d, [], ['tril'])
        cp('dve', identb[:], ident[:], ['ident'], ['identb'])
        memset('dve', onesS[:], 1.0 / 1024, ['onesS'])
        memset('dve', ones1[:], 1.0, ['ones1'])
        memset('dve', onesq[:], 1.0 / 384, ['onesq'])
        memset('pool', VA[:].rearrange("p a b -> p (a b)"), 0.0, ['VA'])
        memset('pool', wuk[:].rearrange("p a b c -> p (a b c)"), 0.0, ['wuk'])
        memset('pool', hcar[:].rearrange("p a b -> p (a b)"), 0.0, ['hcar'])
        memset('pool', qT[:].rearrange("p a b -> p (a b)"), 0.0, [('qT', h) for h in range(8)])
        memset('pool', KT[:].rearrange("p a b -> p (a b)"), 0.0, [('KT', h) for h in range(8)])
        memset('pool', mA[:].rearrange("p a b -> p (a b)"), 0.0, ['mA'])
        memset('pool', mB[:].rearrange("p a b -> p (a b)"), 0.0, [('mB', h) for h in range(8)])
        for i_ in range(5):
            memset('dve', RING[i_][:], 0.0, ['ring%d' % i_])
        for l_ in range(L):
            for n, s in WEIGHTS:
                rows = s[0] * s[1] // 2048
                src = wd[n][l_].rearrange("a b -> (a b)").rearrange("(r c) -> r c", c=2048)
                dst = wb[n][l_].rearrange("a b -> (a b)").rearrange("(r c) -> r c", c=2048)
                r0 = 0
                while r0 < rows:
                    r1 = min(rows, r0 + 1920)
                    dma('pool', dst[r0:r1, :], src[r0:r1, :], [], [(n + '_b', l_)])
                    r0 = r1

        lcn = [0]

        def load_cols(dst, vec, n):
            k = 2 + lcn[0] % 6
            lcn[0] += 1
            sv = stage[0:n, k * 128:(k + 1) * 128]
            dma('sp', sv, vec.rearrange("(c p) -> c p", p=128), [], [('stgs', k)])
            b = ps_get()
            tr(PS[b][:, 0:n], sv, ident[0:n, 0:n], [('stgs', k), 'ident'], ['ps%d' % b])
            cp('dve', dst, PS[b][:, 0:n], ['ps%d' % b], ['PAR'])

        for l in range(L):
            for nme, off, n in (('ln1_g', P_LN1G, 8), ('ln1_b', P_LN1B, 8), ('ln2_g', P_LN2G, 8),
                                ('ln2_b', P_LN2B, 8), ('mla_q_norm_g', P_QG, 3), ('lru_conv_b', P_LCB, 2),
                                ('lru_b_r', P_BR, 2), ('lru_b_i', P_BI, 2), ('lru_lam', P_LAM, 2),
                                ('ffn_conv_b', P_FCB, 44)):
                load_cols(PAR[:, l, off:off + n], vd[nme][l, :], n)
            for j in range(4):
                load_cols(PAR[:, l, P_LCW + 2 * j:P_LCW + 2 * j + 2], lcw[l, j, :], 2)
            for j in range(3):
                load_cols(PAR[:, l, P_FCW + 44 * j:P_FCW + 44 * j + 44], fcw[l, j, :], 44)
            ts('dve', PAR[:, l, P_NBR:P_NBR + 2], PAR[:, l, P_BR:P_BR + 2], -1.0, None, ALU.mult, ALU.bypass, ['PAR'], ['PAR'])
            ts('dve', PAR[:, l, P_NBI:P_NBI + 2], PAR[:, l, P_BI:P_BI + 2], -1.0, None, ALU.mult, ALU.bypass, ['PAR'], ['PAR'])
            act(C1[:, l, 0:2], PAR[:, l, P_LAM:P_LAM + 2], AF.Exp, ['PAR'], ['C1'], scale=-1.0)
            act(C1[:, l, 0:2], C1[:, l, 0:2], AF.Ln, ['C1'], ['C1'], bias=1.0)
            ts('dve', C1[:, l, 2:4], C1[:, l, 0:2], -16.0, None, ALU.mult, ALU.bypass, ['C1'], ['C1'])
            ts('dve', C1[:, l, 0:2], C1[:, l, 0:2], -8.0, None, ALU.mult, ALU.bypass, ['C1'], ['C1'])
            for h in range(4):
                dma('sp', stage[:, 0:128], gmlp_w[l, h, :, :], [], [('stgs', 0)])
                tt('dve', stage[:, 128:256], stage[:, 0:128], tril[:], ALU.mult, [('stgs', 0), 'tril'], [('stgs', 1)])
                b = ps_get()
                tr(PS[b][:, 0:128], stage[:, 128:256], ident[:], [('stgs', 1), 'ident'], ['ps%d' % b])
                cp('dve', xb[:, 0, 0:128], PS[b][:, 0:128], ['ps%d' % b], ['xb'])
                dma('sp', wsT_b[l, h, :, :], xb[:, 0, 0:128], ['xb'], ['wsT_b'])

        def load_layer(l, sample):
            uq = wb['mla_w_uq'][l].rearrange("(kc p) (h c) -> p kc h c", p=128, c=96)
            for kc in range(3):
                dma('sp', wq[:, kc, :, 0:32], uq[:, kc, :, 64:96], [('mla_w_uq_b', l)], ['wq'], slow=True)
                dma('sp', wq[:, kc, :, 32:96], uq[:, kc, :, 0:64], [('mla_w_uq_b', l)], ['wq'], slow=True)
                dma('sp', wqp[:, kc, :, 0:16], uq[:, kc, :, 80:96], [('mla_w_uq_b', l)], ['wqp'], slow=True)
                dma('sp', wqp[:, kc, :, 16:32], uq[:, kc, :, 64:80], [('mla_w_uq_b', l)], ['wqp'], slow=True)
            wi = wb['w_in'][l].rearrange("(kc p) n -> p kc n", p=128)
            dma('sp', wkr[:], wi[:, :, CKR:CKR + 32], [('w_in_b', l)], ['wkr'], slow=True)
            dma('sp', wkrp[:, :, 0:16], wi[:, :, CKR + 16:CKR + 32], [('w_in_b', l)], ['wkrp'], slow=True)
            dma('sp', wkrp[:, :, 16:32], wi[:, :, CKR:CKR + 16], [('w_in_b', l)], ['wkrp'], slow=True)
            uk = wb['mla_w_uk'][l].rearrange("(kc p) (h c) -> p kc h c", p=128, c=64)
            for kc in range(2):
                dma('sp', wuk[:, kc, :, 32:96], uk[:, kc, :, :], [('mla_w_uk_b', l)], ['wuk'], slow=True)
            dma('sp', kvgb[:, 0, :], kvg[l:l + 1, :].broadcast_to([128, 256]), [], ['kvgb'])
            dma('sp', wuv[:], wb['mla_w_uv'][l].rearrange("(kc p) n -> p kc n", p=128), [('mla_w_uv_b', l)], ['wuv'])
            memset('pool', wrbd[:].rearrange("p a b -> p (a b)"), 0.0, ['wrbd'])
            memset('pool', wibd[:].rearrange("p a b -> p (a b)"), 0.0, ['wibd'])
            for n in range(4):
                r0 = (n % 2) * 64
                dma('sp', wrbd[r0:r0 + 64, n // 2, r0:r0 + 64], wb['lru_w_r'][l, n * 64:(n + 1) * 64, :],
                    [('lru_w_r_b', l)], ['wrbd'], slow=True)
                dma('sp', wibd[r0:r0 + 64, n // 2, r0:r0 + 64], wb['lru_w_i'][l, n * 64:(n + 1) * 64, :],
                    [('lru_w_i_b', l)], ['wibd'], slow=True)
            if not sample:
                dma('sp', wsT[:], wsT_b[l].rearrange("h j i -> j h i"), ['wsT_b'], ['wsT'], slow=True)
                dma('sp', bsb[:], gmlp_b[l:l + 1, :, :].broadcast_to([64, 4, 128]), [], ['bsb'])
            else:
                memset('pool', wsT[:].rearrange("p a b -> p (a b)"), 0.0, ['wsT'])
                for s in range(NS):
                    dma('sp', wsT[s * 32:(s + 1) * 32, :, s * 32:(s + 1) * 32],
                        wsT_b[l, :, 0:32, 0:32].rearrange("h j i -> j h i"), ['wsT_b'], ['wsT'], slow=True)
                    dma('sp', bsb[:, :, s * 32:(s + 1) * 32], gmlp_b[l:l + 1, :, 0:32].broadcast_to([64, 4, 32]),
                        [], ['bsb'], slow=True)

        xbk = ['xb'] + [('xb', oc) for oc in range(8)]

        def layer_norm(l, c0, T, pg, pb, make_bf16):
            Xv = X[:, :, c0:c0 + T]
            xk = [('X', oc) for oc in range(8)]
            cp('dve', lnscr[:, :, 0:T], Xv, xk, ['lnscr'])
            act(sq[:, :, 0:T], Xv, AF.Square, xk, ['sq'])
            b = ps_get()
            for kc in range(8):
                mm(PS[b][:, 0:T], onesS[:], lnscr[:, kc, 0:T], kc == 0, kc == 7, ['onesS', 'lnscr'], ['ps%d' % b])
            b2 = ps_get()
            for kc in range(8):
                mm(PS[b2][:, 0:T], onesS[:], sq[:, kc, 0:T], kc == 0, kc == 7, ['onesS', 'sq'], ['ps%d' % b2])
            cp('act', meanb[:, 0:T], PS[b][:, 0:T], ['ps%d' % b], ['meanb'])
            tt('pool', rstdb[:, 0:T], meanb[:, 0:T], meanb[:, 0:T], ALU.mult, ['meanb'], ['rstdb'])
            tt('dve', rstdb[:, 0:T], PS[b2][:, 0:T], rstdb[:, 0:T], ALU.subtract, ['ps%d' % b2, 'rstdb'], ['rstdb'])
            act(rstdb[:, 0:T], rstdb[:, 0:T], AF.Ln, ['rstdb'], ['rstdb'], bias=LN_EPS)
            act(rstdb[:, 0:T], rstdb[:, 0:T], AF.Exp, ['rstdb'], ['rstdb'], scale=-0.5)
            tt('dve', Xv, Xv, meanb[:, None, 0:T].to_broadcast([128, 8, T]), ALU.subtract, xk + ['meanb'], xk)
            tt('dve', Xv, Xv, rstdb[:, None, 0:T].to_broadcast([128, 8, T]), ALU.mult, xk + ['rstdb'], xk)
            for oc in range(8):
                g = PAR[:, l, pg + oc:pg + oc + 1]
                bb = PAR[:, l, pb + oc:pb + oc + 1]
                if make_bf16:
                    ts('pool' if oc % 2 else 'dve', xb[:, oc, 0:T], X[:, oc, c0:c0 + T], g, bb, ALU.mult, ALU.add,
                       [('X', oc), 'PAR'], [('xb', oc)])
                    act(X[:, oc, c0:c0 + T], X[:, oc, c0:c0 + T], AF.Identity, [('X', oc), 'PAR', ('xb', oc)],
                        [('X', oc)], bias=bb, scale=g)
                else:
                    act(X[:, oc, c0:c0 + T], X[:, oc, c0:c0 + T], AF.Identity, [('X', oc), 'PAR'],
                        [('X', oc)], bias=bb, scale=g)

        def tile(l, G, t, last_layer, part):
            sample = G[0] == 's'
            if sample:
                T, nseg, sl, c0 = NS * TS, NS, TS, 0
            else:
                T, nseg, sl, c0 = 256, 1, 256, t * 256
                sidx = G[1]
            nblk = T // 128
            Xv = X[:, :, c0:c0 + T]
            xk = [('X', oc) for oc in range(8)]
            wi = wb['w_in'][l].rearrange("(kc p) n -> p kc n", p=128)

            def wload(col0, ncols):
                rt, rk = ring_get()
                v = rt[:, 0:8 * ncols].rearrange("p (k n) -> p k n", n=ncols)
                dma('sp', v, wi[:, :, col0:col0 + ncols], [('w_in_b', l)], [rk])
                return v, rk

            if part == 'front':
                if sample:
                    dma('act', cosb[:, 0:T], coss_d, [], ['cosb'])
                    dma('act', sinb[:, 0:T], sins_d, [], ['sinb'])
                else:
                    dma('act', cosb[:, 0:T], cosp_d[:, c0:c0 + T], [], ['cosb'])
                    dma('act', sinb[:, 0:T], sinp_d[:, c0:c0 + T], [], ['sinb'])
                cp('act', xb[:, :, 0:T], Xv, xk, xbk)
                w, rk = wload(CU, 256)
                for h in range(4):
                    b = ps_get()
                    for kc in range(8):
                        mm(PS[b][0:64, 0:T], w[:, kc, h * 64:(h + 1) * 64], xb[:, kc, 0:T], kc == 0, kc == 7,
                           [rk] + xbk, ['ps%d' % b])
                    act(uT[:, h, 0:T], PS[b][0:64, 0:T], AF.Gelu, ['ps%d' % b], ['uT'])
                w, rk = wload(CV, 256)
                for blk in range(nblk):
                    b = ps_get()
                    for kc in range(8):
                        mm(PS[b][:, 0:256], xb[:, kc, blk * 128:(blk + 1) * 128], w[:, kc, :], kc == 0, kc == 7,
                           [rk] + xbk, ['ps%d' % b])
                    if sample:
                        act(vf32[:], PS[b][:, 0:256], AF.Gelu, ['ps%d' % b], ['vf32'])
                        cp('pool', vtok[:, blk, :], vf32[:], ['vf32'], ['vtok'])
                        dma('act', o_sv[l, :, :], vf32[:], ['vf32'], [okey()])
                    else:
                        act(vtok[:, blk, :], PS[b][:, 0:256], AF.Gelu, ['ps%d' % b], ['vtok'])
                seglen = sl + 3
                xcw = [xcin[:, c, 0:nseg * seglen].rearrange("p (s n) -> p s n", n=seglen) for c in range(2)]
                w, rk = wload(CXC, 256)
                for c in range(2):
                    b = ps_get()
                    for kc in range(8):
                        mm(PS[b][:, 0:T], w[:, kc, c * 128:(c + 1) * 128], xb[:, kc, 0:T], kc == 0, kc == 7,
                           [rk] + xbk, ['ps%d' % b])
                    cp('act', xcw[c][:, :, 3:3 + sl], PS[b][:, 0:T].rearrange("p (s n) -> p s n", n=sl),
                       ['ps%d' % b], [('xcin', c)])
                w, rk = wload(CG, 256)
                for c in range(2):
                    b = ps_get()
                    for kc in range(8):
                        mm(PS[b][:, 0:T], w[:, kc, c * 128:(c + 1) * 128], xb[:, kc, 0:T], kc == 0, kc == 7,
                           [rk] + xbk, ['ps%d' % b])
                    act(gg[:, c, 0:T], PS[b][:, 0:T], AF.Gelu, ['ps%d' % b], ['gg'])
                for blk in range(nblk):
                    b = ps_get()
                    for h in range(4):
                        mm(PS[b][0:64, h * 128:(h + 1) * 128], vtok[:, blk, h * 64:(h + 1) * 64], wsT[:, h, :],
                           True, True, ['vtok', 'wsT'], ['ps%d' % b])
                    tt('dve', gt[:].rearrange("p (h i) -> p h i", h=4), PS[b][0:64, :].rearrange("p (h i) -> p h i", h=4),
                       bsb[:], ALU.add, ['ps%d' % b, 'bsb'], ['gt'])
                    tt('pool', mA[0:64, :, blk * 128:(blk + 1) * 128], gt[:].rearrange("p (h i) -> p h i", h=4),
                       uT[:, :, blk * 128:(blk + 1) * 128], ALU.mult, ['gt', 'uT'], ['mA'])
                for oc in range(3):
                    if oc == 0:
                        w, rk = wload(CQ, 256)
                    elif oc == 2:
                        w, rk = wload(CQ + 256, 128)
                    ocl = oc % 2
                    b = ps_get()
                    for kc in range(8):
                        mm(PS[b][:, 0:T], w[:, kc, ocl * 128:(ocl + 1) * 128], xb[:, kc, 0:T], kc == 0, kc == 7,
                           [rk] + xbk, ['ps%d' % b])
                    act(cqT[:, oc, 0:T], PS[b][:, 0:T], AF.Identity, ['ps%d' % b, 'PAR'], ['cqT'],
                        scale=PAR[:, l, P_QG + oc:P_QG + oc + 1])
                    act(cqsq[:, oc, 0:T], PS[b][:, 0:T], AF.Square, ['ps%d' % b], ['cqsq'])
                b = ps_get()
                for oc in range(3):
                    mm(PS[b][:, 0:T], onesq[:], cqsq[:, oc, 0:T], oc == 0, oc == 2, ['onesq', 'cqsq'], ['ps%d' % b])
                act(rqb[:, 0:T], PS[b][:, 0:T], AF.Ln, ['ps%d' % b], ['rqb'], bias=RMS_EPS)
                act(rqb[:, 0:T], rqb[:, 0:T], AF.Exp, ['rqb'], ['rqb'], scale=-0.5)
                tt('pool', cosrq[:, 0:T], cosb[:, 0:T], rqb[0:32, 0:T], ALU.mult, ['cosb', 'rqb'], ['cosrq'])
                tt('pool', sinrq[:, 0:T], sinb[:, 0:T], rqb[0:32, 0:T], ALU.mult, ['sinb', 'rqb'], ['sinrq'])
                w, rk = wload(CKV, 256)
                for blk in range(nblk):
                    b = ps_get()
                    for kc in range(8):
                        mm(PS[b][:, 0:256], xb[:, kc, blk * 128:(blk + 1) * 128], w[:, kc, :], kc == 0, kc == 7,
                           [rk] + xbk, ['ps%d' % b])
                    act(ckvn[:, blk, :], PS[b][:, 0:256], AF.Square, ['ps%d' % b], ['ckvn', 'ss'], accum=ss[:, blk:blk + 1])
                    act(ss[:, blk:blk + 1], ss[:, blk:blk + 1], AF.Ln, ['ss'], ['ss'], bias=RMS_EPS, scale=1.0 / 256)
                    act(ss[:, blk:blk + 1], ss[:, blk:blk + 1], AF.Exp, ['ss'], ['ss'], scale=-0.5)
                    stt(ckvn[:, blk, :], PS[b][:, 0:256], ss[:, blk:blk + 1], kvgb[:, 0, :], ALU.mult, ALU.mult,
                        ['ps%d' % b, 'ss', 'kvgb'], ['ckvn'])
                    if sample:
                        dma('act', o_slat[l, :, :], ckvn[:, blk, :], ['ckvn'], [okey()])
                    else:
                        dma('act', o_plat[l, sidx, c0 + blk * 128:c0 + (blk + 1) * 128, :], ckvn[:, blk, :], ['ckvn'], [okey()])
                    b2 = ps_get()
                    for cc in range(2):
                        tr(PS[b2][:, cc * 128:(cc + 1) * 128], ckvn[:, blk, cc * 128:(cc + 1) * 128], ident[:],
                           ['ckvn', 'ident'], ['ps%d' % b2])
                    cp('act', ckvT[:, :, blk * 128:(blk + 1) * 128], PS[b2][:, 0:256].rearrange("p (c n) -> p c n", c=2),
                       ['ps%d' % b2], ['ckvT'])
                b = ps_get()
                for kc in range(8):
                    mm(PS[b][0:32, 0:T], wkr[:, kc, :], xb[:, kc, 0:T], kc == 0, kc == 7, ['wkr'] + xbk, ['ps%d' % b])
                b2 = ps_get()
                for kc in range(8):
                    mm(PS[b2][0:32, 0:T], wkrp[:, kc, :], xb[:, kc, 0:T], kc == 0, kc == 7, ['wkrp'] + xbk, ['ps%d' % b2])
                tt('dve', krt1[:, 0:T], PS[b][0:32, 0:T], cosb[:, 0:T], ALU.mult, ['ps%d' % b, 'cosb'], ['krt1'])
                tt('dve', krot[:, 0:T], PS[b2][0:32, 0:T], sinb[:, 0:T], ALU.mult, ['ps%d' % b2, 'sinb'], ['krot'])
                tt('pool', krot[:, 0:T], krot[:, 0:T], krt1[:, 0:T], ALU.add, ['krot', 'krt1'], ['krot'])
                b = ps_get()
                for blk in range(nblk):
                    tr(PS[b][:, blk * 32:(blk + 1) * 32], krot[:, blk * 128:(blk + 1) * 128], ident[0:32, 0:32],
                       ['krot', 'ident'], ['ps%d' % b])
                cp('dve', krtok[:, 0:nblk, :], PS[b][:, 0:nblk * 32].rearrange("p (c n) -> p c n", n=32), ['ps%d' % b], ['krtok'])
                for blk in range(nblk):
                    if sample:
                        dma('act', o_skr[l, :, :], krtok[:, blk, :], ['krtok'], [okey()])
                    else:
                        dma('act', o_pkr[l, sidx, c0 + blk * 128:c0 + (blk + 1) * 128, :], krtok[:, blk, :], ['krtok'], [okey()])
                for h in range(8):
                    b = ps_get()
                    for kc in range(3):
                        mm(PS[b][0:96, 0:T], wq[:, kc, h, :], cqT[:, kc, 0:T], kc == 0, kc == 2, ['wq', 'cqT'], ['ps%d' % b])
                    b2 = ps_get()
                    for kc in range(3):
                        mm(PS[b2][0:32, 0:T], wqp[:, kc, h, :], cqT[:, kc, 0:T], kc == 0, kc == 2, ['wqp', 'cqT'], ['ps%d' % b2])
                    tt('dve', qT[0:96, h, 0:T], PS[b][0:96, 0:T], rqb[0:96, 0:T], ALU.mult, ['ps%d' % b, 'rqb'], [('qT', h)])
                    tt('dve', qt1[:, 0:T], PS[b][0:32, 0:T], cosrq[:, 0:T], ALU.mult, ['ps%d' % b, 'cosrq'], ['qt1'])
                    tt('dve', qt2[:, 0:T], PS[b2][0:32, 0:T], sinrq[:, 0:T], ALU.mult, ['ps%d' % b2, 'sinrq'], ['qt2'])
                    tt('pool', qT[0:32, h, 0:T], qt1[:, 0:T], qt2[:, 0:T], ALU.add, ['qt1', 'qt2', ('qT', h)], [('qT', h)])
                if not sample:
                    for hp in range(4):
                        b = ps_get()
                        for hh in range(2):
                            h = hp * 2 + hh
                            for kc in range(2):
                                mm(PS[b][0:96, hh * 256:hh * 256 + T], wuk[:, kc, h, :], ckvT[:, kc, 0:T], kc == 0, kc == 1,
                                   ['wuk', 'ckvT'], ['ps%d' % b])
                        cp('act', KT[0:96, hp * 2:hp * 2 + 2, c0:c0 + T], PS[b][0:96, :].rearrange("p (h n) -> p h n", h=2),
                           ['ps%d' % b], [('KT', hp * 2), ('KT', hp * 2 + 1)])
                    cp('dve', KT[0:32, :, c0:c0 + T], krot[:, None, 0:T].to_broadcast([32, 8, T]), ['krot'] + [('KT', h) for h in range(8)],
                       [('KT', h) for h in range(8)])
                    for blk in range(2):
                        kb = t * 2 + blk
                        b = ps_get()
                        for kc in range(2):
                            mm(PS[b][:, :], ckvT[:, kc, blk * 128:(blk + 1) * 128], wuv[:, kc, :], kc == 0, kc == 1,
                               ['ckvT', 'wuv'], ['ps%d' % b])
                        cp('act', VA[:, kb, 0:512], PS[b][:, :], ['ps%d' % b], ['VA'])
            if part == 'rest':
                seglen = sl + 3
                xcw = [xcin[:, c, 0:nseg * seglen].rearrange("p (s n) -> p s n", n=seglen) for c in range(2)]
                if sample:
                    fsrc = ffnc[l].rearrange("s j (c p) -> (s j c) p", p=128)
                    dma('act', stage[0:128, 0:128], fsrc[0:128, :], [], ['stage'])
                    dma('act', stage[0:128, 128:256], fsrc[128:256, :], [], ['stage'])
                    dma('act', stage[0:96, 256:384], fsrc[256:352, :], [], ['stage'])
                    dma('act', stage[0:24, 384:512], lruc[l].rearrange("s j (c p) -> (s j c) p", p=128), [], ['stage'])
                    dma('act', stage[0:8, 512:640], lruh[l].rearrange("s (c p) -> (s c) p", p=128), [], ['stage'])
                    b = ps_get()
                    tr(PS[b][:, 0:128], stage[0:128, 0:128], ident[:], ['stage', 'ident'], ['ps%d' % b])
                    tr(PS[b][:, 128:256], stage[0:128, 128:256], ident[:], ['stage', 'ident'], ['ps%d' % b])
                    tr(PS[b][:, 256:352], stage[0:96, 256:384], ident[0:96, 0:96], ['stage', 'ident'], ['ps%d' % b])
                    tr(PS[b][:, 352:376], stage[0:24, 384:512], ident[0:24, 0:24], ['stage', 'ident'], ['ps%d' % b])
                    tr(PS[b][:, 376:384], stage[0:8, 512:640], ident[0:8, 0:8], ['stage', 'ident'], ['ps%d' % b])
                    cp('dve', fcar[:].rearrange("p c s j -> p s j c"),
                       PS[b][:, 0:352].rearrange("p (s j c) -> p s j c", s=4, j=2), ['ps%d' % b], ['fcar'])
                    for c in range(2):
                        cp('dve', xcw[c][:, :, 0:3], PS[b][:, 352:376].rearrange("p (s j c) -> p s j c", s=4, j=3)[:, :, :, c],
                           ['ps%d' % b], [('xcin', c)])
                    cp('dve', hcar[:, :, 0:NS], PS[b][:, 376:384].rearrange("p (s c) -> p c s", c=2), ['ps%d' % b], ['hcar'])
                elif t == 0:
                    for c in range(2):
                        memset('pool', xcw[c][:, 0, 0:3], 0.0, [('xcin', c)])
                    memset('pool', hcar[:, :, 0:1], 0.0, ['hcar'])
                for c in range(2):
                    xv3 = xcv[:, 0:T].rearrange("p (s n) -> p s n", n=sl)
                    lw = lambda j: PAR[:, l, P_LCW + 2 * j + c:P_LCW + 2 * j + c + 1]
                    ts('dve', xv3, xcw[c][:, :, 3:3 + sl], lw(3), PAR[:, l, P_LCB + c:P_LCB + c + 1], ALU.mult, ALU.add,
                       [('xcin', c), 'PAR'], ['xcv'])
                    for j in range(3):
                        stt(xv3, xcw[c][:, :, j:j + sl], lw(j), xv3, ALU.mult, ALU.add, [('xcin', c), 'PAR', 'xcv'], ['xcv'])
                    cp('pool', xcb[:, c, 0:T], xcv[:, 0:T], ['xcv'], ['xcb'])
                    if sample:
                        cp('pool', sstg[:, 0:24].rearrange("p (s j c) -> p s j c", s=4, j=3)[:, :, :, c], xcw[c][:, :, sl:sl + 3],
                           [('xcin', c)], ['sstg_in'])
                        if c == 1:
                            b_ = ps_get()
                            tr(PS[b_][0:24, 0:128], sstg[:, 0:24], ident[:], ['sstg_in', 'ident'], ['ps%d' % b_])
                            cp('dve', sstg[0:24, 32:160], PS[b_][0:24, 0:128], ['ps%d' % b_], ['sstg_out'])
                            dma('act', o_sc[l].rearrange("s j (c p) -> (s j c) p", p=128), sstg[0:24, 32:160], ['sstg_out'], [okey()])
                    else:
                        if t == SEQ // 256 - 1:
                            cp('pool', sstg[:, 0:6].rearrange("p (j c) -> p j c", c=2)[:, :, c], xcw[c][:, 0, sl:sl + 3],
                               [('xcin', c)], ['sstg_in'])
                            if c == 1:
                                b_ = ps_get()
                                tr(PS[b_][0:6, 0:128], sstg[:, 0:6], ident[:], ['sstg_in', 'ident'], ['ps%d' % b_])
                                cp('dve', sstg[0:6, 32:160], PS[b_][0:6, 0:128], ['ps%d' % b_], ['sstg_out'])
                                dma('act', o_pc[l, sidx].rearrange("j (c p) -> (j c) p", p=128), sstg[0:6, 32:160], ['sstg_out'], [okey()])
                        else:
                            cp('pool', xcw[c][:, 0, 0:3], xcw[c][:, 0, sl:sl + 3], [('xcin', c)], [('xcin', c)])
                    b = ps_get()
                    mm(PS[b][:, 0:T], wrbd[:, c, :], xcb[:, c, 0:T], True, True, ['wrbd', 'xcb'], ['ps%d' % b])
                    b2 = ps_get()
                    mm(PS[b2][:, 0:T], wibd[:, c, :], xcb[:, c, 0:T], True, True, ['wibd', 'xcb'], ['ps%d' % b2])
                    act(la[:, 0:T], PS[b][:, 0:T], AF.Exp, ['ps%d' % b, 'PAR'], ['la'], bias=PAR[:, l, P_NBR + c:P_NBR + c + 1], scale=-1.0)
                    act(lig[:, 0:T], PS[b2][:, 0:T], AF.Exp, ['ps%d' % b2, 'PAR'], ['lig'], bias=PAR[:, l, P_NBI + c:P_NBI + c + 1], scale=-1.0)
                    act(la[:, 0:T], la[:, 0:T], AF.Ln, ['la'], ['la'], bias=1.0)
                    act(lig[:, 0:T], lig[:, 0:T], AF.Ln, ['lig'], ['lig'], bias=1.0)
                    act(la[:, 0:T], la[:, 0:T], AF.Exp, ['la'], ['la'], scale=-1.0)
                    act(lig[:, 0:T], lig[:, 0:T], AF.Exp, ['lig'], ['lig'], scale=-1.0)
                    act(ls[:, 0:T], la[:, 0:T], AF.Exp, ['la', 'C1'], ['ls'], scale=C1[:, l, 2 + c:3 + c])
                    act(la[:, 0:T], la[:, 0:T], AF.Exp, ['la', 'C1'], ['la'], scale=C1[:, l, c:c + 1])
                    act(ls[:, 0:T], ls[:, 0:T], AF.Ln, ['ls'], ['ls'], bias=1.0, scale=-1.0)
                    act(ls[:, 0:T], ls[:, 0:T], AF.Exp, ['ls'], ['ls'], scale=0.5)
                    tt('dve', lig[:, 0:T], lig[:, 0:T], xcv[:, 0:T], ALU.mult, ['lig', 'xcv'], ['lig'])
                    tt('dve', lig[:, 0:T], lig[:, 0:T], ls[:, 0:T], ALU.mult, ['lig', 'ls'], ['lig'])
                    for s in range(nseg):
                        S.op('dve', lambda e, s=s, c=c: e.tensor_tensor_scan(
                            out=lh[:, c, s * sl:(s + 1) * sl], data0=la[:, s * sl:(s + 1) * sl],
                            data1=lig[:, s * sl:(s + 1) * sl], initial=hcar[:, c, s:s + 1], op0=ALU.mult, op1=ALU.add),
                            reads=['la', 'lig', 'hcar'], writes=['lh'])
                        cp('pool', hcar[:, c, s:s + 1], lh[:, c, (s + 1) * sl - 1:(s + 1) * sl], ['lh'], ['hcar'])
                    tt('pool', mC[:, c, 0:T], lh[:, c, 0:T], gg[:, c, 0:T], ALU.mult, ['lh', 'gg'], ['mC'])
                if sample:
                    cp('pool', sstg[:, 24:32].rearrange("p (s c) -> p c s", c=2), hcar[:, :, 0:NS], ['hcar'], ['sstg_in'])
                    b_ = ps_get()
                    tr(PS[b_][0:8, 0:128], sstg[:, 24:32], ident[:], ['sstg_in', 'ident'], ['ps%d' % b_])
                    cp('dve', sstg[0:8, 32:160], PS[b_][0:8, 0:128], ['ps%d' % b_], ['sstg_out'])
                    dma('act', o_sh[l].rearrange("s (c p) -> (s c) p", p=128), sstg[0:8, 32:160], ['sstg_out'], [okey()])
                elif t == SEQ // 256 - 1:
                    cp('pool', sstg[:, 24:26], hcar[:, :, 0], ['hcar'], ['sstg_in'])
                    b_ = ps_get()
                    tr(PS[b_][0:2, 0:128], sstg[:, 24:26], ident[:], ['sstg_in', 'ident'], ['ps%d' % b_])
                    cp('dve', sstg[0:2, 32:160], PS[b_][0:2, 0:128], ['ps%d' % b_], ['sstg_out'])
                    dma('act', o_ph[l, sidx, :].rearrange("(c p) -> c p", p=128), sstg[0:2, 32:160], ['sstg_out'], [okey()])
                if sample:
                    sample_attention(l)
                else:
                    prompt_attention(l, t, c0)
                wo = wb['w_o'][l]
                for oc in range(8):
                    rt, rk = ring_get()
                    cs = slice(oc * 128, (oc + 1) * 128)
                    wA = rt[0:64, 0:512].rearrange("p (h n) -> p h n", n=128)
                    wB = rt[0:64, 512:1536].rearrange("p (h n) -> p h n", n=128)
                    wC = rt[:, 1536:1792].rearrange("p (h n) -> p h n", n=128)
                    dma('sp', wA, wo[0:256, cs].rearrange("(h p) n -> p h n", p=64), [('w_o_b', l)], [rk])
                    dma('sp', wB, wo[256:768, cs].rearrange("(h p) n -> p h n", p=64), [('w_o_b', l)], [rk])
                    dma('sp', wC, wo[768:1024, cs].rearrange("(h p) n -> p h n", p=128), [('w_o_b', l)], [rk])
                    b = ps_get()
                    n = 0
                    for h in range(4):
                        mm(PS[b][:, 0:T], rt[:, 0:512].rearrange("p (h n) -> p h n", n=128)[:, h, :], mA[:, h, 0:T], n == 0, False, [rk, 'mA'], ['ps%d' % b]); n += 1
                    for h in range(8):
                        mm(PS[b][:, 0:T], rt[:, 512:1536].rearrange("p (h n) -> p h n", n=128)[:, h, :], mB[:, h, 0:T], False, False, [rk, ('mB', h)], ['ps%d' % b]); n += 1
                    for c in range(2):
                        mm(PS[b][:, 0:T], wC[:, c, :], mC[:, c, 0:T], False, c == 1, [rk, 'mC'], ['ps%d' % b]); n += 1
                    stt(X[:, oc, c0:c0 + T], X[:, oc, c0:c0 + T], ALPHA, PS[b][:, 0:T], ALU.mult, ALU.add,
                        [('X', oc), 'ps%d' % b], [('X', oc)])
                layer_norm(l, c0, T, P_LN1G, P_LN1B, True)
                wu = wb['ffn_w_up'][l].rearrange("(kc p) n -> p kc n", p=128)
                seg2 = sl + 2
                pend = None
                for c in range(22):
                    if c % 2 == 0:
                        rtg, rkg = ring_get()
                        rtv, rkv = ring_get()
                        wg = rtg[:, 0:2048].rearrange("p (k n) -> p k n", n=256)
                        wv = rtv[:, 0:2048].rearrange("p (k n) -> p k n", n=256)
                        dma('sp', wg, wu[:, :, c * 128:c * 128 + 256], [('ffn_w_up_b', l)], [rkg])
                        dma('sp', wv, wu[:, :, DFF + c * 128:DFF + c * 128 + 256], [('ffn_w_up_b', l)], [rkv])
                    o = (c % 2) * 128
                    b = ps_get()
                    for kc in range(8):
                        mm(PS[b][:, 0:T], wg[:, kc, o:o + 128], xb[:, kc, 0:T], kc == 0, kc == 7, [rkg] + xbk, ['ps%d' % b])
                    for kc in range(8):
                        mm(PS[b][:, 256:256 + T], wv[:, kc, o:o + 128], xb[:, kc, 0:T], kc == 0, kc == 7, [rkv] + xbk, ['ps%d' % b])
                    ub = upb[c % 2]
                    ubk = 'upb%d' % (c % 2)
                    cg = cgb2[c % 2]
                    cgk = 'cgb%d' % (c % 2)
                    u4 = ub[:, :, 0:nseg * seg2].rearrange("p g (s n) -> p g s n", n=seg2)
                    pv4 = PS[b][:, :].rearrange("p (g c) -> p g c", g=2)[:, :, 0:T].rearrange("p g (s n) -> p g s n", n=sl)
                    if sample:
                        for gv in range(2):
                            cp('pool', u4[:, gv, :, 0:2], fcar[:, c + 22 * gv, :, :], ['fcar'], [ubk])
                    elif t == 0:
                        memset('pool', u4[:, :, 0, 0:2], 0.0, [ubk])
                    else:
                        cp('pool', u4[:, :, 0, 0:2], fcar[:, c:c + 23:22, 0, :], ['fcar'], [ubk])
                    cp('act', u4[:, :, :, 2:2 + sl], pv4, ['ps%d' % b], [ubk])
                    for gv in range(2):
                        cp('pool', fcar[:, c + 22 * gv, 0:nseg, :], u4[:, gv, :, sl:sl + 2], [ubk], ['fcar'])
                    for gv in range(2):
                        ch = c + 22 * gv
                        cw = lambda j: PAR[:, l, P_FCW + 44 * j + ch:P_FCW + 44 * j + ch + 1]
                        o3 = cg[:, gv, 0:T].rearrange("p (s n) -> p s n", n=sl)
                        act(o3, pv4[:, gv, :, :], AF.Identity, ['ps%d' % b, 'PAR'], [(cgk, gv)], bias=PAR[:, l, P_FCB + ch:P_FCB + ch + 1],
                            scale=cw(2))
                        for j in range(2):
                            stt(o3, u4[:, gv, :, j:j + sl], cw(j), o3, ALU.mult, ALU.add, [ubk, 'PAR', (cgk, gv)], [(cgk, gv)])
                    if pend is not None:
                        pend()

                    def _tail(cg=cg, cgk=cgk, c=c):
                        act(cg[:, 0, 0:T], cg[:, 0, 0:T], AF.Gelu, [(cgk, 0)], [(cgk, 0)])
                        tt('pool', hT[:, c, 0:T], cg[:, 0, 0:T], cg[:, 1, 0:T], ALU.mult, [(cgk, 0), (cgk, 1)], [('hT', c)])
                    pend = _tail
                pend()
                if sample:
                    stg_ = upb[0][:].rearrange("p g n -> p (g n)")[:, 0:352]
                    ostg_ = upb[1][:].rearrange("p g n -> p (g n)")[:, 0:384]
                    cp('pool', stg_.rearrange("p (s j c) -> p s j c", s=4, j=2), fcar[:].rearrange("p c s j -> p s j c"),
                       ['fcar'], ['upb0'])
                    b_ = ps_get()
                    tr(PS[b_][:, 0:128], stg_[:, 0:128], ident[:], ['upb0', 'ident'], ['ps%d' % b_])
                    tr(PS[b_][:, 128:256], stg_[:, 128:256], ident[:], ['upb0', 'ident'], ['ps%d' % b_])
                    tr(PS[b_][0:96, 256:384], stg_[:, 256:352], ident[:], ['upb0', 'ident'], ['ps%d' % b_])
                    cp('dve', ostg_[:, 0:256], PS[b_][:, 0:256], ['ps%d' % b_], ['upb1'])
                    cp('dve', ostg_[0:96, 256:384], PS[b_][0:96, 256:384], ['ps%d' % b_], ['upb1'])
                    fdst = o_sf[l].rearrange("s j (c p) -> (s j c) p", p=128)
                    dma('act', fdst[0:128, :], ostg_[:, 0:128], ['upb1'], [okey()])
                    dma('act', fdst[128:256, :], ostg_[:, 128:256], ['upb1'], [okey()])
                    dma('act', fdst[256:352, :], ostg_[0:96, 256:384], ['upb1'], [okey()])
                elif t == SEQ // 256 - 1:
                    stg_ = upb[0][:].rearrange("p g n -> p (g n)")[:, 0:88]
                    ostg_ = upb[1][:].rearrange("p g n -> p (g n)")[:, 0:128]
                    cp('pool', stg_.rearrange("p (j c) -> p j c", j=2), fcar[:, :, 0, :].rearrange("p c j -> p j c"), ['fcar'], ['upb0'])
                    b_ = ps_get()
                    tr(PS[b_][0:88, 0:128], stg_, ident[:], ['upb0', 'ident'], ['ps%d' % b_])
                    cp('dve', ostg_[0:88, :], PS[b_][0:88, 0:128], ['ps%d' % b_], ['upb1'])
                    dma('act', o_pf[l, sidx].rearrange("j (c p) -> (j c) p", p=128), ostg_[0:88, :], ['upb1'], [okey()])
                wdn = wb['ffn_w_down'][l].rearrange("(kc p) n -> p kc n", p=128)
                for op_ in range(4):
                    bb = (ps_get(hold=True), ps_get(hold=True))
                    for (k0, k1) in ((0, 8), (8, 16), (16, 22)):
                        rt, rk = ring_get()
                        w2 = rt[:, 0:(k1 - k0) * 256].rearrange("p (k n) -> p k n", n=256)
                        dma('sp', w2, wdn[:, k0:k1, op_ * 256:(op_ + 1) * 256], [('ffn_w_down_b', l)], [rk])
                        for j_ in range(2):
                            b = bb[j_]
                            for kc in range(k0, k1):
                                mm(PS[b][:, 0:T], w2[:, kc - k0, j_ * 128:(j_ + 1) * 128], hT[:, kc, 0:T], kc == 0, kc == 21,
                                   [rk, ('hT', kc)], ['ps%d' % b], skip=True)
                    for j_ in range(2):
                        b = bb[j_]
                        oc = op_ * 2 + j_
                        stt(X[:, oc, c0:c0 + T], X[:, oc, c0:c0 + T], ALPHA, PS[b][:, 0:T], ALU.mult, ALU.add,
                            [('X', oc), 'ps%d' % b], [('X', oc)])
                        ps_rel(b)
            if part == 'ln2':
                layer_norm(l, c0, T, P_LN2G, P_LN2B, False)
                if last_layer:
                    for blk in range(nblk):
                        for half in range(2):
                            b = ps_get()
                            for q4 in range(4):
                                oc = half * 4 + q4
                                tr(PS[b][:, q4 * 128:(q4 + 1) * 128], X[:, oc, c0 + blk * 128:c0 + (blk + 1) * 128], ident[:],
                                   [('X', oc), 'ident'], ['ps%d' % b])
                            cp('act' if half else 'dve', otok[:, half * 512:(half + 1) * 512], PS[b][:, :], ['ps%d' % b], [('otok', half)])
                        if sample:
                            dma('act', ys[:, :], otok[:], [('otok', 0), ('otok', 1)], [okey()])
                        else:
                            dma('act', yp[sidx, c0 + blk * 128:c0 + (blk + 1) * 128, :], otok[:], [('otok', 0), ('otok', 1)], [okey()])

        def prompt_attention(l, t, c0):
            T = 256
            nkb = 2 * t + 2
            blocks = [(h, kb) for h in range(8) for kb in range(nkb)]
            LOOK = 2
            info = {}
            accs = {}

            def emit_qk(i):
                h, kb = blocks[i]
                q0 = 128 if kb == nkb - 1 else 0
                b = ps_get()
                mm(PS[b][:, q0:T], KT[:, h, kb * 128:(kb + 1) * 128], qT[:, h, q0:T], True, True,
                   [('KT', h), ('qT', h)], ['ps%d' % b])
                pt = PT[i % 4]
                pk = 'pT%d' % (i % 4)
                act(pt[:, q0:T], PS[b][:, q0:T], AF.Exp, ['ps%d' % b], [pk], scale=ATTN_SCALE)
                if kb >= nkb - 2:
                    m0 = 0 if kb == nkb - 2 else 128
                    memset('pool', pt[64:128, m0:m0 + 64], 0.0, [pk])
                info[i] = (pt, pk, q0)

            def emit_pv(i):
                h, kb = blocks[i]
                pt, pk, q0 = info.pop(i)
                if kb == 0:
                    accs[h] = (ps_get(hold=True), ps_get(hold=True))
                ob, db = accs[h]
                mm(PS[ob][:, q0:T], VA[:, kb, h * 64:h * 64 + 128], pt[:, q0:T], kb == 0, kb == nkb - 1,
                   ['VA', pk], ['ps%d' % ob], skip=True)
                mm(PS[db][:, q0:T], ones1[:, :], pt[:, q0:T], kb == 0, kb == nkb - 1,
                   ['ones1', pk], ['ps%d' % db], skip=True)
                if kb == nkb - 1:
                    act(bcs[:, 0:T], PS[db][0:64, 0:T], AF.Ln, ['ps%d' % db], ['bcs'])
                    act(bcs[:, 0:T], bcs[:, 0:T], AF.Exp, ['bcs'], ['bcs'], scale=-1.0)
                    tt('dve', mB[0:64, h, 0:T], PS[ob][0:64, 0:T], bcs[:, 0:T], ALU.mult, ['ps%d' % ob, 'bcs'], [('mB', h)])
                    ps_rel(ob)
                    ps_rel(db)
                    del accs[h]

            for i in range(len(blocks) + LOOK):
                if i < len(blocks):
                    emit_qk(i)
                if i - LOOK >= 0:
                    emit_pv(i - LOOK)

        KTf = KT[:].rearrange("p a b -> p (a b)")

        def sample_attention(l):
            T = NS * TS
            wukT = KTf[0:96, 0:2048].rearrange("p (h c) -> p h c", h=8)
            qlat = KTf[:, 2048:4096].rearrange("p (c h n) -> p c h n", c=2, h=8)
            olat = KTf[:, 4096:6144].rearrange("p (c h n) -> p c h n", c=2, h=8)
            cst = [KTf[:, 6144 + i * 2304:6144 + (i + 1) * 2304].bitcast(F32).rearrange("p (k n) -> p k n", n=288)
                   for i in range(2)]
            cbf = [KTf[:, 10752 + i * 1152:10752 + (i + 1) * 1152].rearrange("p (k n) -> p k n", n=288) for i in range(2)]
            cT = [KTf[:, 13056 + i * 1024:13056 + (i + 1) * 1024].rearrange("p (c n) -> p c n", c=2) for i in range(2)]
            krT = [KTf[0:32, 15104 + i * 512:15104 + (i + 1) * 512] for i in range(2)]
            cnew = KTf[:, 16128:16384]
            for h in range(8):
                b = ps_get()
                for cc in range(2):
                    tr(psb(b)[0:96, cc * 128:(cc + 1) * 128], wuk[:, cc, h, :], identb[:], ['wuk', 'identb'], ['ps%d' % b])
                cp('act', wukT[:, h, :], psb(b)[0:96, 0:256], ['ps%d' % b], ['wukT'])
            for h in range(8):
                b = ps_get()
                for cc in range(2):
                    mm(PS[b][:, cc * 128:cc * 128 + T], wukT[:, h, cc * 128:(cc + 1) * 128], qT[0:96, h, 0:T], True, True,
                       ['wukT', ('qT', h)], ['ps%d' % b])
                cp('act', qlat[:, :, h, :], PS[b][:, 0:256].rearrange("p (c n) -> p c n", c=2), ['ps%d' % b], ['qlat'])
            nfull = PAST // 128
            cT4 = [KTf[:, 13056 + i * 256:13056 + (i + 1) * 256].rearrange("p (c n) -> p c n", c=2) for i in range(4)]
            krT4 = [KTf[0:32, 15104 + i * 128:15104 + (i + 1) * 128] for i in range(4)]
            for s in range(NS):
                accs = [ps_get(hold=True) for _ in range(3)]
                groups = [(g0, min(4, nfull - g0)) for g0 in range(0, nfull, 4)] + [(-1, 1)]
                blks = [(gi, g0, k) for gi, (g0, ng) in enumerate(groups) for k in range(ng)]
                nb_ = len(blks)
                st = {}

                def stA(i, s=s, blks=blks, groups=groups, st=st):
                    gi, g0, k = blks[i]
                    ng = groups[gi][1]
                    i2 = gi % 2
                    sk, bk = 'cst%d' % i2, 'cbf%d' % i2
                    i4 = i % 4
                    tk = 'cT4_%d' % i4
                    if g0 >= 0:
                        if k == 0:
                            dma('sp', cst[i2][:, 0:ng, 0:256],
                                clat[l, s, g0 * 128:(g0 + ng) * 128, :].rearrange("(k p) n -> p k n", p=128), [], [sk])
                            dma('sp', cst[i2][:, 0:ng, 256:288],
                                ckr[l, s, g0 * 128:(g0 + ng) * 128, :].rearrange("(k p) n -> p k n", p=128), [], [sk])
                            cp('pool', cbf[i2][:, 0:ng, :], cst[i2][:, 0:ng, :], [sk], [bk])
                        b = ps_get()
                        for cc in range(2):
                            tr(psb(b)[:, cc * 128:(cc + 1) * 128], cbf[i2][:, k, cc * 128:(cc + 1) * 128], identb[:],
                               [bk, 'identb'], ['ps%d' % b])
                        tr(psb(b)[0:32, 256:384], cbf[i2][:, k, 256:288], identb[:], [bk, 'identb'], ['ps%d' % b])
                        cp('act', cT4[i4][:, :, 0:128], psb(b)[:, 0:256].rearrange("p (c n) -> p c n", c=2), ['ps%d' % b], [tk])
                        cp('dve', krT4[i4][:, 0:128], psb(b)[0:32, 256:384], ['ps%d' % b], [tk])
                        st[i] = dict(nk=128, lat=[cT4[i4][:, cc, 0:128] for cc in range(2)], kr=krT4[i4][:, 0:128],
                                     cv=cbf[i2][:, k, 0:256], on=ones1[:, :], tk=tk, bk=bk)
                    else:
                        if s == 0:
                            cp('pool', cnew[:], ckvn[:, 0, :], ['ckvn'], ['cnew'])
                        dma('act', cbf[i2][0:32, 0, 0:256], cnew[s * 32:(s + 1) * 32, :], ['cnew'], [bk])
                        cp('pool', krT4[i4][:, 0:32], krot[:, s * 32:(s + 1) * 32], ['krot'], [tk])
                        st[i] = dict(nk=32, lat=[ckvT[:, cc, s * 32:(s + 1) * 32] for cc in range(2)], kr=krT4[i4][:, 0:32],
                                     cv=cbf[i2][0:32, 0, 0:256], on=ones1[0:32, :], tk=tk, bk=bk)

                def stB(i, s=s, st=st):
                    d = st[i]
                    nk = d['nk']
                    b = ps_get()
                    sv = PS[b][0:nk, 0:256].rearrange("p (h n) -> p h n", h=8)
                    for cc in range(2):
                        mm(sv, d['lat'][cc], qlat[:, cc, :, s * 32:(s + 1) * 32], cc == 0, False,
                           [d['tk'], 'ckvT', 'qlat'], ['ps%d' % b])
                    mm(sv, d['kr'], qT[0:32, :, s * 32:(s + 1) * 32], False, True,
                       [d['tk']] + [('qT', h) for h in range(8)], ['ps%d' % b])
                    pt = PT[i % 4]
                    pk = 'pT%d' % (i % 4)
                    act(pt[0:nk, 0:256], PS[b][0:nk, 0:256], AF.Exp, ['ps%d' % b], [pk], scale=ATTN_SCALE)
                    d['ptv'] = pt[0:nk, 0:256]
                    d['pk'] = pk

                def stC(i, st=st, accs=accs, nb_=nb_):
                    d = st.pop(i)
                    first = i == 0
                    last = i == nb_ - 1
                    for cc in range(2):
                        mm(PS[accs[cc]][:, 0:256], d['cv'][:, cc * 128:(cc + 1) * 128], d['ptv'], first, last,
                           [d['bk'], d['pk']], ['ps%d' % accs[cc]], skip=True)
                    mm(PS[accs[2]][:, 0:256], d['on'], d['ptv'], first, last, ['ones1', d['pk']], ['ps%d' % accs[2]], skip=True)

                for i in range(nb_ + 2):
                    if i < nb_:
                        stA(i)
                    if 0 <= i - 1 < nb_:
                        stB(i - 1)
                    if 0 <= i - 2 < nb_:
                        stC(i - 2)
                S.op('dve', lambda e, a=accs[2]: e.reciprocal(out=rden[:, 0:256], in_=PS[a][:, 0:256]),
                     reads=['ps%d' % accs[2]], writes=['rden'])
                for cc in range(2):
                    tt('dve', olat[:, cc, :, s * 32:(s + 1) * 32], PS[accs[cc]][:, 0:256].rearrange("p (h n) -> p h n", h=8),
                       rden[:, 0:256].rearrange("p (h n) -> p h n", h=8), ALU.mult, ['ps%d' % accs[cc], 'rden'], ['olat'])
                for a in accs:
                    ps_rel(a)
            for h in range(8):
                b = ps_get()
                for cc in range(2):
                    mm(PS[b][0:64, 0:T], wuv[:, cc, h * 64:(h + 1) * 64], olat[:, cc, h, :], cc == 0, cc == 1,
                       ['wuv', 'olat'], ['ps%d' % b])
                cp('act', mB[0:64, h, 0:T], PS[b][0:64, 0:T], ['ps%d' % b], [('mB', h)])

        groups = []
        if cfg.do_prompt:
            groups += [('p', s) for s in range(NP)]
        if cfg.do_sample:
            groups += [('s',)]
        for gi_, G in enumerate(groups):
            S.barrier(skip_rings=('pool',) if gi_ == 0 else ())
            if G[0] == 'p':
                for blk in range(SEQ // 128):
                    dma('sp', stage[:], xp[G[1], blk * 128:(blk + 1) * 128, :], [], ['stage'])
                    for half in range(2):
                        b = ps_get()
                        for q4 in range(4):
                            kc = half * 4 + q4
                            tr(PS[b][:, q4 * 128:(q4 + 1) * 128], stage[:, kc * 128:(kc + 1) * 128], ident[:],
                               ['stage', 'ident'], ['ps%d' % b])
                        cp('act' if half else 'dve', X[:, half * 4:half * 4 + 4, blk * 128:(blk + 1) * 128],
                           PS[b][:, :].rearrange("p (k n) -> p k n", k=4), ['ps%d' % b], [('X', half * 4 + q) for q in range(4)])
                ntile = SEQ // 256
            else:
                dma('sp', stage[:], xs[:, :], [], ['stage'])
                for half in range(2):
                    b = ps_get()
                    for q4 in range(4):
                        kc = half * 4 + q4
                        tr(PS[b][:, q4 * 128:(q4 + 1) * 128], stage[:, kc * 128:(kc + 1) * 128], ident[:],
                           ['stage', 'ident'], ['ps%d' % b])
                    cp('act' if half else 'dve', X[:, half * 4:half * 4 + 4, 0:128],
                       PS[b][:, :].rearrange("p (k n) -> p k n", k=4), ['ps%d' % b], [('X', half * 4 + q) for q in range(4)])
                ntile = 1
            for l in range(L):
                load_layer(l, G[0] == 's')
                tile(l, G, 0, l == L - 1, 'front')
                for t in range(ntile):
                    tile(l, G, t, l == L - 1, 'rest')
                    if t + 1 < ntile:
                        tile(l, G, t + 1, l == L - 1, 'front')
                    tile(l, G, t, l == L - 1, 'ln2')
        S.barrier()
        S.emit(block)
    return nc


def _consts(cfg):
    TS = 32
    inv = 1.0 / (10000.0 ** (np.arange(0, 32, 2, dtype=np.float32) / 32.0))

    def tab(pos):
        ang = pos.astype(np.float32)[None, :] * inv[:, None].astype(np.float32)
        c = np.cos(ang).astype(np.float32)
        s = np.sin(ang).astype(np.float32)
        return np.concatenate([c, c], 0), np.concatenate([-s, s], 0)
    cp_, sp_ = tab(np.arange(cfg.SEQ))
    c1, s1 = tab(cfg.PAST + np.arange(TS))
    return {
        'c_ident': np.eye(128, dtype=np.float32),
        'c_tril': np.tril(np.ones((128, 128), np.float32)),
        'c_cosp': np.ascontiguousarray(cp_), 'c_sinp': np.ascontiguousarray(sp_),
        'c_coss': np.ascontiguousarray(np.tile(c1, (1, cfg.NS))), 'c_sins': np.ascontiguousarray(np.tile(s1, (1, cfg.NS))),
    }


def make_in_maps(cfg, inputs, ncores):
    L, NP, NS = cfg.L, cfg.NP, cfg.NS
    f = lambda a: np.ascontiguousarray(np.asarray(a, dtype=np.float32))
    consts = _consts(cfg)
    maps = []
    for c in range(ncores):
        m = dict(consts)
        m['xp'] = f(inputs['x_prompt'][c * NP:(c + 1) * NP])
        m['xs'] = f(inputs['x_sample'][c * NS:(c + 1) * NS]).reshape(NS * 32, D)
        m['clat'] = f(inputs['cache_kv_latent'][:, c * NS:(c + 1) * NS])
        m['ckr'] = f(inputs['cache_k_rope'][:, c * NS:(c + 1) * NS])
        m['lruh'] = f(inputs['state_lru_h'][:, c * NS:(c + 1) * NS])
        m['lruc'] = f(inputs['state_lru_conv'][:, c * NS:(c + 1) * NS])
        m['ffnc'] = f(inputs['state_ffn_conv'][:, c * NS:(c + 1) * NS])
        for n, s in WEIGHTS:
            m[n] = f(inputs[n]).reshape([L] + s)
        for n, s in VECS:
            m[n] = f(inputs[n])
        m['gmlp_w_s'] = f(inputs['gmlp_w_s'])
        m['gmlp_b_s'] = f(inputs['gmlp_b_s'])
        m['mla_kv_norm_g'] = f(inputs['mla_kv_norm_g'])
        m['lru_conv_w'] = f(inputs['lru_conv_w'])
        m['ffn_conv_w'] = f(inputs['ffn_conv_w'])
        maps.append(m)
    return maps


def gather(cfg, results):
    L, NP, NS = cfg.L, cfg.NP, cfg.NS
    cat = lambda k, ax: np.concatenate([r[k] for r in results], axis=ax)
    y_p = cat('yp', 0)
    y_s = cat('ys', 0).reshape(-1, 32, D)
    p_lat = cat('o_plat', 1)
    p_kr = cat('o_pkr', 1)
    p_h = cat('o_ph', 1)
    p_c = cat('o_pc', 1)
    p_f = cat('o_pf', 1)
    s_lat = np.concatenate([r['o_slat'].reshape(L, NS, 32, 256) for r in results], axis=1)
    s_kr = np.concatenate([r['o_skr'].reshape(L, NS, 32, 32) for r in results], axis=1)
    s_v = np.concatenate([r['o_sv'].reshape(L, NS, 32, 256) for r in results], axis=1)
    s_h = cat('o_sh', 1)
    s_c = cat('o_sc', 1)
    s_f = cat('o_sf', 1)
    return tuple(np.ascontiguousarray(a, dtype=np.float32) for a in
                 (y_p, y_s, p_lat, p_kr, p_h, p_c, p_f, s_lat, s_kr, s_v, s_h, s_c, s_f))


def kernel(**inputs):
    cfg = Cfg()
    nc = build_nc(cfg)
    in_maps = make_in_maps(cfg, inputs, NCORES)
    res = run_bass_kernel_spmd(nc, in_maps, core_ids=list(range(NCORES)))
    return gather(cfg, res.results)
```
